# Optimizing a Trainium2 kernel written in Bass

```python
import functools
import jax, jax.numpy as jnp
from jax import lax
import numpy as np

D_MODEL = 1024
BATCH = 8
SEQ = 4096
DEPTH = 2
DEC_BATCH = 128
DEC_SEQ = 8
PAST_LEN = 16384
PAGE_SIZE = 128

HEAD_DIM = 64
CONV_WIDTH = 3
A_WIDTH = D_MODEL // 4
B_HEADS = (3 * D_MODEL // 8) // HEAD_DIM
C_Q_HEADS = (3 * D_MODEL // 8) // HEAD_DIM
C_KV_HEADS = C_Q_HEADS // 3
B_WIDTH = B_HEADS * HEAD_DIM
C_WIDTH = C_Q_HEADS * HEAD_DIM
C_KV_WIDTH = C_KV_HEADS * HEAD_DIM
MIX_WIDTH = A_WIDTH + B_WIDTH + C_WIDTH
IN_WIDTH = 3 * A_WIDTH + 3 * B_WIDTH + C_WIDTH + 2 * C_KV_WIDTH
DILATIONS = ((128, 1), (512, 4), (2048, 16))
B_WINDOW_MAX = 2048
C_WINDOW = 128
BLOCK = 128
D_FF = -(-8 * D_MODEL // (3 * 256)) * 256
EPS = 1e-6
ATTN_SCALE = HEAD_DIM ** -0.5

kernel_name = "hybrid_conv_dilated_swa_decoder_step"


def rmsnorm(x, g):
    x32 = x.astype(jnp.float32)
    y = x32 * lax.rsqrt(jnp.mean(x32 * x32, axis=-1, keepdims=True) + EPS)
    return (y * g.astype(jnp.float32)).astype(x.dtype)


def adaln(c, w_mod, b_mod):
    mod = jax.nn.silu(c) @ w_mod + b_mod
    return jnp.split(mod[:, None, :], 6, axis=-1)


def swiglu(h, w_gate_up, w_down):
    gate, up = jnp.split(h @ w_gate_up, 2, axis=-1)
    return (jax.nn.silu(gate) * up) @ w_down


def split_points():
    sizes = [A_WIDTH] * 3 + [B_WIDTH] * 3 + [C_WIDTH, C_KV_WIDTH, C_KV_WIDTH]
    return [int(v) for v in np.cumsum(sizes)[:-1]]


def project(h, w_in):
    n, s, _ = h.shape
    ga, gc, xa, qb, kb, vb, qc, kc, vc = jnp.split(h @ w_in, split_points(), axis=-1)
    heads = lambda t: t.reshape(n, s, -1, HEAD_DIM)
    return ga, gc, xa, heads(qb), heads(kb), heads(vb), heads(qc), heads(kc), heads(vc)


def short_conv(u_ext, w, n_out):
    return sum(w[i] * u_ext[:, i:i + n_out] for i in range(CONV_WIDTH))


def banded_attention(q, k, v, max_dist, sinks=None):
    n, L, hq, hd = q.shape
    hkv = k.shape[2]
    rep = hq // hkv
    nb = L // BLOCK
    qb = q.reshape(n, nb, BLOCK, hkv, rep, hd)

    def with_prev(t):
        t = t.reshape(n, nb, BLOCK, hkv, hd)
        prev = jnp.pad(t[:, :-1], ((0, 0), (1, 0), (0, 0), (0, 0), (0, 0)))
        return jnp.concatenate([prev, t], axis=2)

    kk, vv = with_prev(k), with_prev(v)
    s = jnp.einsum('nbqgrd,nbkgd->nbgrqk', qb, kk, preferred_element_type=jnp.float32) * ATTN_SCALE
    qi = jnp.arange(BLOCK)[:, None]
    kj = jnp.arange(2 * BLOCK)[None, :]
    dist = BLOCK + qi - kj
    kpos = jnp.arange(nb)[:, None, None] * BLOCK + kj[None] - BLOCK
    valid = (dist >= 0) & (dist <= max_dist) & (kpos >= 0)
    s = jnp.where(valid[None, :, None, None], s, -jnp.inf)
    m = s.max(-1)
    if sinks is not None:
        sk = sinks.astype(jnp.float32).reshape(hkv, rep)[None, None, :, :, None]
        m = jnp.maximum(m, sk)
    p = jnp.exp(s - m[..., None])
    l = p.sum(-1)
    if sinks is not None:
        l = l + jnp.exp(sk - m)
    o = jnp.einsum('nbgrqk,nbkgd->nbqgrd', p.astype(v.dtype), vv, preferred_element_type=jnp.float32)
    to_rows = lambda t: jnp.moveaxis(t, -1, 2).reshape(n, L, hq)
    m, l = to_rows(m), to_rows(l)
    o = o.reshape(n, L, hq, hd) / l[..., None]
    return o.astype(q.dtype), m, l


def combine_by_denominator(outs, ms, ls):
    m_max = functools.reduce(jnp.maximum, ms)
    ws = [l * jnp.exp(m - m_max) for m, l in zip(ms, ls)]
    total = sum(ws)
    y = sum(w[..., None] * o.astype(jnp.float32) for w, o in zip(ws, outs)) / total[..., None]
    return y.astype(outs[0].dtype)


def dilated_prompt(q, k, v):
    b, s, h, hd = q.shape
    outs, ms, ls = [], [], []
    for window, d in DILATIONS:
        L = s // d
        Lp = -(-L // BLOCK) * BLOCK

        def to_phase(t):
            t = t.reshape(b, L, d, h, hd).transpose(0, 2, 1, 3, 4).reshape(b * d, L, h, hd)
            return jnp.pad(t, ((0, 0), (0, Lp - L), (0, 0), (0, 0)))

        def from_phase(t):
            t = t[:, :L].reshape((b, d, L) + t.shape[2:])
            return jnp.swapaxes(t, 1, 2).reshape((b, s) + t.shape[3:])

        o, m, l = banded_attention(to_phase(q), to_phase(k), to_phase(v), window // d)
        outs.append(from_phase(o)); ms.append(from_phase(m)); ls.append(from_phase(l))
    return combine_by_denominator(outs, ms, ls)


def dilated_sample(q, k_cat, v_cat, n_past):
    ds = q.shape[1]
    q_idx = n_past + jnp.arange(ds)
    outs, ms, ls = [], [], []
    for window, d in DILATIONS:
        idx = q_idx[:, None] - d * jnp.arange(window // d + 1)[None, :]
        valid = idx >= 0
        idx = jnp.maximum(idx, 0)
        kg = k_cat[:, idx]
        vg = v_cat[:, idx]
        s = jnp.einsum('bqhd,bqkhd->bqhk', q, kg, preferred_element_type=jnp.float32) * ATTN_SCALE
        s = jnp.where(valid[None, :, None, :], s, -jnp.inf)
        m = s.max(-1)
        p = jnp.exp(s - m[..., None])
        l = p.sum(-1)
        o = jnp.einsum('bqhk,bqkhd->bqhd', p.astype(v_cat.dtype), vg, preferred_element_type=jnp.float32)
        outs.append((o / l[..., None]).astype(q.dtype)); ms.append(m); ls.append(l)
    return combine_by_denominator(outs, ms, ls)


def window_sample(q, k_cat, v_cat, n_past, sinks):
    db, ds, hq, hd = q.shape
    hkv = k_cat.shape[2]
    rep = hq // hkv
    dist = (n_past + jnp.arange(ds))[:, None] - jnp.arange(k_cat.shape[1])[None, :]
    valid = (dist >= 0) & (dist < C_WINDOW)
    s = jnp.einsum('bqgrd,bkgd->bgrqk', q.reshape(db, ds, hkv, rep, hd), k_cat,
                   preferred_element_type=jnp.float32) * ATTN_SCALE
    s = jnp.where(valid, s, -jnp.inf)
    sk = sinks.astype(jnp.float32).reshape(hkv, rep)[None, :, :, None]
    m = jnp.maximum(s.max(-1), sk)
    p = jnp.exp(s - m[..., None])
    l = p.sum(-1) + jnp.exp(sk - m)
    o = jnp.einsum('bgrqk,bkgd->bqgrd', p.astype(v_cat.dtype), v_cat, preferred_element_type=jnp.float32)
    o = o / jnp.moveaxis(l, -1, 1)[..., None]
    return o.reshape(db, ds, hq, hd).astype(q.dtype)


def mix_prompt(h, w_in, conv_w, sinks):
    n, s, _ = h.shape
    ga, gc, xa, qb, kb, vb, qc, kc, vc = project(h, w_in)
    u = gc * xa
    u_ext = jnp.pad(u, ((0, 0), (CONV_WIDTH - 1, 0), (0, 0)))
    ya = ga * short_conv(u_ext, conv_w, s)
    yb = dilated_prompt(qb, kb, vb)
    yc, _, _ = banded_attention(qc, kc, vc, C_WINDOW - 1, sinks)
    mix = jnp.concatenate([ya, yb.reshape(n, s, B_WIDTH), yc.reshape(n, s, C_WIDTH)], axis=-1)
    wb, wc = min(B_WINDOW_MAX, s), min(C_WINDOW, s)
    states = (u[:, s - (CONV_WIDTH - 1):], kb[:, s - wb:], vb[:, s - wb:], kc[:, s - wc:], vc[:, s - wc:])
    return mix, states


def mix_sample(h, conv_buf, bk_buf, bv_buf, ck_buf, cv_buf, w_in, conv_w, sinks):
    n, s, _ = h.shape
    ga, gc, xa, qb, kb, vb, qc, kc, vc = project(h, w_in)
    u_ext = jnp.concatenate([conv_buf, gc * xa], axis=1)
    ya = ga * short_conv(u_ext, conv_w, s)
    kb_cat = jnp.concatenate([bk_buf, kb], axis=1)
    vb_cat = jnp.concatenate([bv_buf, vb], axis=1)
    yb = dilated_sample(qb, kb_cat, vb_cat, bk_buf.shape[1])
    kc_cat = jnp.concatenate([ck_buf, kc], axis=1)
    vc_cat = jnp.concatenate([cv_buf, vc], axis=1)
    yc = window_sample(qc, kc_cat, vc_cat, ck_buf.shape[1], sinks)
    mix = jnp.concatenate([ya, yb.reshape(n, s, B_WIDTH), yc.reshape(n, s, C_WIDTH)], axis=-1)
    tb, tc, tu = kb_cat.shape[1], kc_cat.shape[1], u_ext.shape[1]
    wb, wc = min(B_WINDOW_MAX, tb), min(C_WINDOW, tc)
    states = (u_ext[:, tu - (CONV_WIDTH - 1):], kb_cat[:, tb - wb:], vb_cat[:, tb - wb:],
              kc_cat[:, tc - wc:], vc_cat[:, tc - wc:])
    return mix, states


def apply_layer(x, mod, mixer, g_mix, g_ffn, w_out, w_gate_up, w_down):
    shift1, scale1, gate1, shift2, scale2, gate2 = mod
    mix, states = mixer(rmsnorm(x, g_mix) * (1 + scale1) + shift1)
    x = x + gate1 * (mix @ w_out)
    x = x + gate2 * swiglu(rmsnorm(x, g_ffn) * (1 + scale2) + shift2, w_gate_up, w_down)
    return x, states


def setup_inputs(seed: int = 0) -> dict:
    key = jax.random.key(seed)
    ks = iter(jax.random.split(key, 32))
    nrm = lambda shape, scale=1.0: jax.random.normal(next(ks), shape, jnp.float32) * scale
    wb = min(B_WINDOW_MAX, PAST_LEN)
    wc = min(C_WINDOW, PAST_LEN)
    return {
        "x_prompt": nrm((BATCH, SEQ, D_MODEL)),
        "x_sample": nrm((DEC_BATCH, DEC_SEQ, D_MODEL)),
        "state_conv": nrm((DEPTH, DEC_BATCH, CONV_WIDTH - 1, A_WIDTH)),
        "cache_b_k": nrm((DEPTH, DEC_BATCH, wb, B_HEADS, HEAD_DIM)),
        "cache_b_v": nrm((DEPTH, DEC_BATCH, wb, B_HEADS, HEAD_DIM)),
        "cache_c_k": nrm((DEPTH, DEC_BATCH, wc, C_KV_HEADS, HEAD_DIM)),
        "cache_c_v": nrm((DEPTH, DEC_BATCH, wc, C_KV_HEADS, HEAD_DIM)),
        "c_prompt": nrm((BATCH, D_MODEL)),
        "c_sample": nrm((DEC_BATCH, D_MODEL)),
        "w_mod": nrm((DEPTH, D_MODEL, 6 * D_MODEL), 0.5 * D_MODEL ** -0.5),
        "b_mod": nrm((DEPTH, 6 * D_MODEL), 0.02),
        "norm_mix": 1.0 + nrm((DEPTH, D_MODEL), 0.02),
        "norm_ffn": 1.0 + nrm((DEPTH, D_MODEL), 0.02),
        "w_in": nrm((DEPTH, D_MODEL, IN_WIDTH), D_MODEL ** -0.5),
        "conv_w": nrm((DEPTH, CONV_WIDTH, A_WIDTH), CONV_WIDTH ** -0.5),
        "sinks": nrm((DEPTH, C_Q_HEADS)),
        "w_out": nrm((DEPTH, MIX_WIDTH, D_MODEL), MIX_WIDTH ** -0.5),
        "w_gate_up": nrm((DEPTH, D_MODEL, 2 * D_FF), D_MODEL ** -0.5),
        "w_down": nrm((DEPTH, D_FF, D_MODEL), D_FF ** -0.5),
        "norm_final": 1.0 + nrm((D_MODEL,), 0.02),
    }


def reference(x_prompt, x_sample, state_conv, cache_b_k, cache_b_v, cache_c_k, cache_c_v,
              c_prompt, c_sample, w_mod, b_mod, norm_mix, norm_ffn, w_in, conv_w, sinks,
              w_out, w_gate_up, w_down, norm_final):
    y_p, y_s = x_prompt, x_sample
    p_states, s_states = [], []
    for layer in range(DEPTH):
        mod_p = adaln(c_prompt, w_mod[layer], b_mod[layer])
        mod_s = adaln(c_sample, w_mod[layer], b_mod[layer])
        shared = (norm_mix[layer], norm_ffn[layer], w_out[layer], w_gate_up[layer], w_down[layer])
        y_p, st = apply_layer(
            y_p, mod_p,
            lambda h: mix_prompt(h, w_in[layer], conv_w[layer], sinks[layer]),
            *shared)
        p_states.append(st)
        y_s, st = apply_layer(
            y_s, mod_s,
            lambda h: mix_sample(h, state_conv[layer], cache_b_k[layer], cache_b_v[layer],
                                 cache_c_k[layer], cache_c_v[layer], w_in[layer], conv_w[layer], sinks[layer]),
            *shared)
        s_states.append(st)
    y_prompt = rmsnorm(y_p, norm_final)
    y_sample = rmsnorm(y_s, norm_final)
    conv_p, bk_p, bv_p, ck_p, cv_p = [jnp.stack(t) for t in zip(*p_states)]
    conv_s, bk_s, bv_s, ck_s, cv_s = [jnp.stack(t) for t in zip(*s_states)]
    return (y_prompt, y_sample, conv_p, conv_s, bk_p, bk_s, bv_p, bv_s, ck_p, ck_s, cv_p, cv_s)
```

```python
import numpy as np
import concourse.bass as bass
import concourse.mybir as mybir
from concourse.bass_utils import run_bass_kernel_spmd

F32 = mybir.dt.float32
BF16 = mybir.dt.bfloat16
AF = mybir.ActivationFunctionType
ALU = mybir.AluOpType
AX = mybir.AxisListType


class Buf:
    __slots__ = ("name", "last_w", "readers", "excl")

    def __init__(self, name="", excl=False):
        self.name = name
        self.last_w = None
        self.readers = {}
        self.excl = excl


class Eng:
    def __init__(self, name, eng, sem, self_sync):
        self.name = name
        self.eng = eng
        self.sem = sem
        self.count = 0
        self.waited = {}
        self.self_sync = self_sync


class DSem:
    def __init__(self, sem):
        self.sem = sem
        self.count = 0


class TK:
    def __init__(self, nc, sems):
        self.nc = nc
        self.free_sems = list(sems)
        self.semobj = {}
        mk = lambda n, e, ss: Eng(n, e, self._sem(n), ss)
        self.pe = mk("pe", nc.tensor, False)
        self.act = mk("act", nc.scalar, True)
        self.dve = mk("dve", nc.vector, True)
        self.pool = mk("pool", nc.gpsimd, True)
        self.sp = mk("sp", nc.sync, False)
        self.engs = [self.pe, self.act, self.dve, self.pool, self.sp]
        self.dsems = []

    def _sem(self, name):
        s = self.free_sems.pop()
        self.semobj[id(s)] = s
        return s

    def dsem(self):
        d = DSem(self._sem("d"))
        self.dsems.append(d)
        return d

    def _deps(self, reads, writes):
        deps = {}

        def add(ev):
            if ev is None:
                return
            k, v = ev
            if deps.get(k, 0) < v:
                deps[k] = v
        for b in reads:
            add(b.last_w)
            if b.excl:
                for k, v in b.readers.items():
                    add((k, v))
        for b in writes:
            add(b.last_w)
            for k, v in b.readers.items():
                add((k, v))
        return deps

    def _wait(self, e, deps):
        for k, v in deps.items():
            if k == id(e.sem) and not e.self_sync:
                continue
            if e.waited.get(k, 0) >= v:
                continue
            e.eng.wait_ge(self.semobj[k], v)
            e.waited[k] = v

    def _mark(self, ev, reads, writes):
        k, v = ev
        for b in reads:
            if b.excl:
                b.last_w = ev
                b.readers = {}
            elif b.readers.get(k, 0) < v:
                b.readers[k] = v
        for b in writes:
            b.last_w = ev
            b.readers = {}

    def op(self, e, fn, reads=(), writes=()):
        self._wait(e, self._deps(reads, writes))
        inst = fn()
        e.count += 1
        inst.then_inc(e.sem, 1)
        self._mark((id(e.sem), e.count), reads, writes)

    def dma(self, q, ds, pairs, reads=(), writes=()):
        deps = self._deps(reads, writes)
        if ds.count:
            k = id(ds.sem)
            if deps.get(k, 0) < ds.count:
                deps[k] = ds.count
        self._wait(q, deps)
        for (o, i) in pairs:
            q.eng.dma_start(out=o, in_=i).then_inc(ds.sem, 16)
            ds.count += 16
        self._mark((id(ds.sem), ds.count), reads, writes)

    def barrier(self):
        tot = {}
        for e in self.engs:
            if e.count:
                tot[id(e.sem)] = e.count
        for d in self.dsems:
            if d.count:
                tot[id(d.sem)] = d.count
        for e in self.engs:
            for k, v in tot.items():
                if k == id(e.sem) and not e.self_sync:
                    continue
                if e.waited.get(k, 0) >= v:
                    continue
                e.eng.wait_ge(self.semobj[k], v)
                e.waited[k] = v


D = 1024
SEQ = 4096
NT = SEQ // 128
NB = 16
DS = 8
L = 2
INW = 2560
DFF = 2816
NFC = DFF // 128
EPS = 1e-6
SC = 0.125
WB = 2048
C_GA, C_GC, C_XA, C_QB, C_KB, C_VB, C_QC, C_KC, C_VC = 0, 256, 512, 768, 1152, 1536, 1920, 2304, 2432
DILS = (1, 4, 16)


class SBA:
    def __init__(self, big, words):
        self.big = big
        self.words = words
        self.off = 0

    def f32(self, n):
        assert self.off + n <= self.words, ("SBUF overflow", self.off, n, self.words)
        ap = self.big[:, self.off:self.off + n]
        self.off += n
        return ap

    def bf16(self, n):
        w = (n + 1) // 2
        ap = self.f32(w).bitcast(BF16)
        return ap[:, 0:n]

    def mark(self):
        return self.off

    def release(self, m):
        self.off = m


class Ring:
    def __init__(self, aps, dsems=None, name="r"):
        self.aps = aps
        self.bufs = [Buf(f"{name}{i}") for i in range(len(aps))]
        self.ds = dsems
        self.i = -1

    def next(self):
        self.i = (self.i + 1) % len(self.aps)
        return self.cur()

    def cur(self):
        i = self.i
        return self.aps[i], self.bufs[i], (self.ds[i] if self.ds else None)


def build(nlayers=L, stop_after=None, dbg=False):
    nc = bass.Bass("TRN2", target_bir_lowering=False)

    def din(name, shape):
        return nc.dram_tensor(name, shape, F32, kind="ExternalInput").ap()

    def dout(name, shape):
        return nc.dram_tensor(name, shape, F32, kind="ExternalOutput").ap()

    def dscr(name, shape, dt):
        if dbg:
            return nc.dram_tensor(name, shape, dt, kind="ExternalOutput").ap()
        return nc.dram_tensor(name, shape, dt).ap()

    xp = din("xp", [SEQ, D]); xs = din("xs", [128, D])
    cpe = din("cpe", [128, D]); cse = din("cse", [128, D])
    sconv = din("sconv", [L, 32, 256])
    cbk = din("cbk", [L, NB, WB, 384]); cbv = din("cbv", [L, NB, WB, 384])
    cck = din("cck", [L, NB, 128, 128]); ccv = din("ccv", [L, NB, 128, 128])
    w_mod = din("w_mod", [L, D, 6 * D]); b_mod = din("b_mod", [L, 6 * D])
    norm_mix = din("norm_mix", [L, D]); norm_ffn = din("norm_ffn", [L, D])
    w_in = din("w_in", [L, D, INW]); conv_w = din("conv_w", [L, 3, 256])
    sinkP = din("sinkP", [L, 3, 128]); sinkS = din("sinkS", [L, 24, 2])
    w_out = din("w_out", [L, D, D]); w_gu = din("w_gu", [L, D, 2 * DFF]); w_down = din("w_down", [L, DFF, D])
    norm_final = din("norm_final", [D])
    c_ident = din("c_ident", [128, 128])
    c_maskB = din("c_maskB", [128, 512]); c_maskC = din("c_maskC", [128, 512])
    c_multB = din("c_multB", [128, 128]); c_multBn = din("c_multBn", [8, 8])
    c_maskCs = din("c_maskCs", [128, 8]); c_maskCn = din("c_maskCn", [8, 8])
    yp = dout("yp", [SEQ, D]); ys = dout("ys", [128, D])
    convp = dout("convp", [L, 2, 256]); convs = dout("convs", [L, NB, 2, 256])
    bkp = dout("bkp", [L, WB, 384]); bks = dout("bks", [L, NB, WB, 384])
    bvp = dout("bvp", [L, WB, 384]); bvs = dout("bvs", [L, NB, WB, 384])
    ckp = dout("ckp", [L, 128, 128]); cks = dout("cks", [L, NB, 128, 128])
    cvp = dout("cvp", [L, 128, 128]); cvs = dout("cvs", [L, NB, 128, 128])
    MODP = dscr("MODP", [L, 6, 128, D], F32); MODS = dscr("MODS", [L, 6, 128, D], F32)
    QK = dscr("QK", [10, 128, SEQ], BF16)
    VB = dscr("VB", [SEQ, 384], BF16); VC = dscr("VC", [SEQ, 128], BF16)
    VSB = dscr("VSB", [128, 384], BF16); VSC = dscr("VSC", [128, 128], BF16)
    XA = dscr("XA", [SEQ + 128, D], F32); XB = dscr("XB", [SEQ + 128, D], F32)
    DBd = {}

    outc = [0]

    def DB(*key):
        if key == ("OUT",):
            outc[0] += 1
            key = ("OUT", outc[0])
        if key not in DBd:
            DBd[key] = Buf(str(key))
        return DBd[key]

    SBW = 51 * 1024
    big = nc.sbuf_tensor("big", [128, SBW], F32).__enter__()
    sb = SBA(big, SBW)
    ps = [nc.psum_tensor(f"ps{i}", [128, 512], F32).__enter__() for i in range(8)]
    psb = [p[:].bitcast(BF16) for p in ps]
    pb = [Buf(f"ps{i}", excl=True) for i in range(8)]
    sems = [nc.alloc_semaphore(name=f"ks{i}") for i in range(100)]
    tk = TK(nc, sems)
    PE, ACT, DVE, POOL, SP = tk.pe, tk.act, tk.dve, tk.pool, tk.sp
    dpool = [tk.dsem() for _ in range(74)]
    dpool_sw = [tk.dsem() for _ in range(16)]
    dsi = [0, 0]

    def ds_take(n=1, sw=False):
        pool, ix = (dpool_sw, 1) if sw else (dpool, 0)
        r = pool[dsi[ix]:dsi[ix] + n]
        assert len(r) == n, "out of dma sems"
        dsi[ix] += n
        return r if n > 1 else r[0]

    def ncdma():
        return nc.allow_non_contiguous_dma(reason="small strided loads")

    identb = sb.bf16(128); identf = sb.f32(128)
    maskB = sb.bf16(512); maskC = sb.bf16(512)
    multB = sb.bf16(128); multBn = sb.bf16(8); maskCs = sb.bf16(8); maskCn = sb.bf16(8)
    onesb = sb.bf16(64); neghalf = sb.f32(4); gfin = sb.f32(D)
    bC = Buf("const")
    dc = ds_take(sw=True)
    tk.dma(POOL, dc, [(identb, c_ident), (maskB, c_maskB), (maskC, c_maskC), (multB, c_multB),
                      (multBn[0:8, :], c_multBn), (maskCs, c_maskCs), (maskCn[0:8, :], c_maskCn)], writes=[bC])
    tk.dma(SP, ds_take(), [(identf, c_ident), (gfin, norm_final.partition_broadcast(128))], writes=[bC])
    tk.op(DVE, lambda: nc.vector.memset(onesb, 1.0), writes=[bC])
    tk.op(DVE, lambda: nc.vector.memset(neghalf, -0.5), writes=[bC])
    ds_base = list(dsi)

    def phase_begin():
        dsi[0], dsi[1] = ds_base
        return sb.mark()

    def phase_end(m):
        sb.release(m)
        tk.barrier()

    def rstd_from_ss(ss, rs, n, bss, brs):
        tk.op(POOL, lambda: nc.gpsimd.tensor_scalar(out=rs[:, 0:n], in0=ss[:, 0:n], scalar1=1.0 / D, scalar2=EPS,
                                                    op0=ALU.mult, op1=ALU.add), reads=[bss], writes=[brs])
        tk.op(POOL, lambda: nc.gpsimd.tensor_tensor(out=rs[:, 0:n], in0=rs[:, 0:n], in1=neghalf[:, 0:n], op=ALU.pow),
              reads=[brs, bC], writes=[brs])

    def transpose8(src_bf, bsrc, bank, bbank):
        def f():
            for k in range(8):
                i = nc.tensor.transpose(out=psb[bank][:, k * 128:(k + 1) * 128], in_=src_bf[:, k * 128:(k + 1) * 128],
                                        identity=identb)
            return i
        tk.op(PE, f, reads=[bsrc, bC], writes=[bbank])

    def mm_acc(out, lhs_fn, rhs_fn, nk):
        def f():
            for k in range(nk):
                i = nc.tensor.matmul(out, lhsT=lhs_fn(k), rhs=rhs_fn(k), start=(k == 0), stop=(k == nk - 1))
            return i
        return f

    st = dict(nc=nc, tk=tk, sb=sb, ps=ps, psb=psb, pb=pb)

    def phase_mod():
        m = phase_begin()
        cp = sb.f32(D); cs = sb.f32(D)
        scb = [sb.bf16(D), sb.bf16(D)]
        scT = sb.bf16(2 * D).rearrange("p (g k t) -> p g k t", g=2, k=8)
        gam = [sb.f32(D), sb.f32(D)]
        wm = Ring([sb.bf16(8 * 512).rearrange("p (k c) -> p k c", k=8) for _ in range(2)], ds_take(2, sw=True), "wm")
        bm = Ring([sb.f32(512) for _ in range(2)], ds_take(2), "bm")
        stg = Ring([sb.f32(512) for _ in range(4)], ds_take(4), "stg")
        bcp, bsc, bscT, bg = Buf(), Buf(), Buf(), Buf()
        tk.dma(SP, ds_take(), [(cp, cpe), (cs, cse)], writes=[bcp])
        tk.op(ACT, lambda: nc.scalar.activation(out=scb[0], in_=cp, func=AF.Silu), reads=[bcp], writes=[bsc])
        tk.op(ACT, lambda: nc.scalar.activation(out=scb[1], in_=cs, func=AF.Silu), reads=[bcp], writes=[bsc])
        for g in range(2):
            transpose8(scb[g], bsc, g, pb[g])
            tk.op(DVE, lambda g=g: nc.vector.tensor_copy(out=scT[:, g], in_=psb[g].rearrange("p (k t) -> p k t", k=8)),
                  reads=[pb[g]], writes=[bscT])
        dg = ds_take()
        for l in range(nlayers):
            tk.dma(SP, dg, [(gam[0], norm_mix[l].partition_broadcast(128)), (gam[1], norm_ffn[l].partition_broadcast(128))],
                   writes=[bg])
            for n in range(12):
                w_ap, w_b, w_d = wm.next()
                b_ap, b_b, b_d = bm.next()
                src = w_mod[l, :, n * 512:(n + 1) * 512].rearrange("(k p) c -> p k c", p=128)
                tk.dma(POOL, w_d, [(w_ap, src)], writes=[w_b])
                tk.dma(SP, b_d, [(b_ap, b_mod[l, n * 512:(n + 1) * 512].partition_broadcast(128))], writes=[b_b])
                j, half = n // 2, n % 2
                for g in range(2):
                    bank = 2 + ((2 * n + g) % 4)
                    tk.op(PE, mm_acc(ps[bank][:, :], lambda k, g=g: scT[:, g, k, :], lambda k, w_ap=w_ap: w_ap[:, k, :], 8),
                          reads=[bscT, w_b], writes=[pb[bank]])
                    s_ap, s_b, s_d = stg.next()
                    tk.op(DVE, lambda s_ap=s_ap, bank=bank, b_ap=b_ap: nc.vector.tensor_tensor(
                        out=s_ap, in0=ps[bank][:, :], in1=b_ap, op=ALU.add), reads=[pb[bank], b_b], writes=[s_b])
                    if j in (1, 4):
                        gg = gam[0 if j == 1 else 1][:, half * 512:(half + 1) * 512]
                        tk.op(DVE, lambda s_ap=s_ap, gg=gg: nc.vector.scalar_tensor_tensor(
                            out=s_ap, in0=s_ap, scalar=1.0, in1=gg, op0=ALU.add, op1=ALU.mult), reads=[s_b, bg], writes=[s_b])
                    dst = (MODP, MODS)[g][l, j, :, half * 512:(half + 1) * 512]
                    tk.dma(SP, s_d, [(dst, s_ap)], reads=[s_b], writes=[DB("MOD", g, l, j, half)])
        phase_end(m)

    def mod_bufs(g, l, j):
        return [DB("MOD", g, l, j, 0), DB("MOD", g, l, j, 1)]

    def make_norm(l, which):
        o = {}
        jg, jsh = (1, 0) if which == 0 else (4, 3)
        o["gs"] = sb.f32(8); o["sh"] = sb.f32(8); o["b"] = Buf()
        with ncdma():
            tk.dma(SP, ds_take(), [(o["gs"], MODP[l, jg, 0, :].rearrange("(k p) -> p k", p=128)),
                                   (o["sh"], MODP[l, jsh, 0, :].rearrange("(k p) -> p k", p=128))],
                   reads=mod_bufs(0, l, jg) + mod_bufs(0, l, jsh), writes=[o["b"]])
        o["junk"] = sb.bf16(D); o["bjunk"] = Buf()
        o["ss"] = sb.f32(4); o["rs"] = sb.f32(4); o["bss"] = Buf(); o["brs"] = Buf()
        o["xsb"] = Ring([sb.bf16(D) for _ in range(2)], None, "xsb")
        o["tb"] = 0
        o["l"], o["which"], o["jg"], o["jsh"] = l, which, jg, jsh
        return o

    def norm_tile_prompt(o, xt, bx, hT_dst, bh, tbanks):
        ss, rs = o["ss"], o["rs"]
        tk.op(ACT, lambda: nc.scalar.activation(out=o["junk"], in_=xt, func=AF.Square, accum_out=ss[:, 0:1]),
              reads=[bx], writes=[o["bjunk"], o["bss"]])
        rstd_from_ss(ss, rs, 1, o["bss"], o["brs"])
        x_ap, x_b, _ = o["xsb"].next()
        tk.op(ACT, lambda: nc.scalar.activation(out=x_ap, in_=xt, func=AF.Copy, scale=rs[:, 0:1]),
              reads=[bx, o["brs"]], writes=[x_b])
        bank = tbanks[o["tb"] % len(tbanks)]; o["tb"] += 1
        transpose8(x_ap, x_b, bank, pb[bank])
        pv = psb[bank].rearrange("p (k t) -> p k t", k=8)
        for k in range(8):
            e = DVE if k % 2 == 0 else POOL
            if e is DVE:
                tk.op(DVE, lambda k=k: nc.vector.tensor_scalar(out=hT_dst[:, k, :], in0=pv[:, k, :], scalar1=o["gs"][:, k:k + 1],
                                                               scalar2=o["sh"][:, k:k + 1], op0=ALU.mult, op1=ALU.add),
                      reads=[pb[bank], o["b"]], writes=[bh])
            else:
                tk.op(ACT, lambda k=k: nc.scalar.activation(out=hT_dst[:, k, :], in_=pv[:, k, :], func=AF.Identity,
                                                            scale=o["gs"][:, k:k + 1], bias=o["sh"][:, k:k + 1]),
                      reads=[pb[bank], o["b"]], writes=[bh])

    def norm_tile_sample(o, xt, bx, hT_dst, bh, tbanks, gsS, shS, bmodS, tmp):
        ss, rs = o["ss"], o["rs"]
        tk.op(ACT, lambda: nc.scalar.activation(out=o["junk"], in_=xt, func=AF.Square, accum_out=ss[:, 0:1]),
              reads=[bx], writes=[o["bjunk"], o["bss"]])
        rstd_from_ss(ss, rs, 1, o["bss"], o["brs"])
        x_ap, x_b, _ = o["xsb"].next()
        tap, tb_ = tmp
        tk.op(DVE, lambda: nc.vector.scalar_tensor_tensor(out=tap, in0=xt, scalar=rs[:, 0:1], in1=gsS, op0=ALU.mult, op1=ALU.mult),
              reads=[bx, o["brs"], bmodS], writes=[tb_])
        tk.op(DVE, lambda: nc.vector.tensor_tensor(out=x_ap, in0=tap, in1=shS, op=ALU.add), reads=[tb_, bmodS], writes=[x_b])
        bank = tbanks[o["tb"] % len(tbanks)]; o["tb"] += 1
        transpose8(x_ap, x_b, bank, pb[bank])
        tk.op(DVE, lambda: nc.vector.tensor_copy(out=hT_dst, in_=psb[bank].rearrange("p (k t) -> p k t", k=8)),
              reads=[pb[bank]], writes=[bh])

    def x_src(l, t):
        if l == 0:
            return (xp[t * 128:(t + 1) * 128, :] if t < NT else xs), []
        return XB[t * 128:(t + 1) * 128, :], [DB("XB", t)]

    def load_w_in(l):
        win = sb.bf16(8 * INW).rearrange("p (k c) -> p k c", k=8)
        bwin = Buf()
        src = w_in[l].rearrange("(k p) c -> p k c", p=128)
        dsw = ds_take(sw=True)
        pairs = []
        for c0 in (0, 512, 1024, 1536):
            c1 = min(c0 + 512, C_QC)
            pairs.append((win[:, :, c0:c1], src[:, :, c0:c1]))
        for j in range(3):
            o0 = C_QC + 128 * j
            pairs.append((win[:, :, o0:o0 + 64], src[:, :, C_QC + 64 * j:C_QC + 64 * j + 64]))
            pairs.append((win[:, :, o0 + 64:o0 + 128], src[:, :, C_QC + 64 * (j + 3):C_QC + 64 * (j + 3) + 64]))
        pairs.append((win[:, :, C_KC:INW], src[:, :, C_KC:INW]))
        for pr in pairs:
            tk.dma(POOL, dsw, [pr], writes=[bwin])
        cw = sb.f32(6).rearrange("p (c i) -> p c i", c=2)
        with ncdma():
            tk.dma(SP, ds_take(), [(cw[:, c, i:i + 1], conv_w[l, i, c * 128:(c + 1) * 128].rearrange("(p o) -> p o", o=1))
                                   for c in range(2) for i in range(3)], writes=[bwin])
        return win, cw, bwin

    def phase_a1(l, mixT, bmix):
        m = phase_begin()
        win, cw, bwin = load_w_in(l)
        nm = make_norm(l, 0)
        m2 = sb.mark()
        xr = Ring([sb.f32(D) for _ in range(3)], ds_take(3), "x")
        hr = Ring([sb.bf16(8 * 512).rearrange("p (k t) -> p k t", k=8) for _ in range(2)], None, "hT")
        qst = Ring([sb.bf16(512) for _ in range(3)], ds_take(3), "qst")
        vst = Ring([sb.bf16(512) for _ in range(2)], ds_take(2), "vst")
        fst = Ring([sb.f32(384) for _ in range(3)], ds_take(3), "fst")
        gaS = sb.f32(1024).rearrange("p (c t) -> p c t", c=2); gcS = sb.f32(1024).rearrange("p (c t) -> p c t", c=2)
        ub = sb.f32(2 * 514).rearrange("p (c t) -> p c t", c=2)
        cacc = sb.f32(512)
        bga, bgc, bub, bcacc = Buf(), Buf(), Buf(), Buf()
        tk.op(POOL, lambda: nc.gpsimd.memset(ub[:, :, 0:2], 0.0), writes=[bub])
        pbank = [2]

        def nbank():
            b = pbank[0]
            pbank[0] = 2 + (pbank[0] - 2 + 1) % 6
            return b

        import os
        for g in range(int(os.environ.get('NGROUPS', NT // 4))):
            h_ap, h_b, _ = hr.next()
            for tt in range(4):
                t = 4 * g + tt
                x_ap, x_b, x_d = xr.next()
                src, sbufs = x_src(l, t)
                tk.dma(SP, x_d, [(x_ap, src)], reads=sbufs, writes=[x_b])
                norm_tile_prompt(nm, x_ap, x_b, h_ap[:, :, tt * 128:(tt + 1) * 128], h_b, (0, 1))
            N = 512
            pos0 = g * 512

            def fm(col, bank):
                tk.op(PE, mm_acc(ps[bank][:, 0:N], lambda k: win[:, k, col:col + 128], lambda k: h_ap[:, k, 0:N], 8),
                      reads=[bwin, h_b], writes=[pb[bank]])
            PARTS = os.environ.get('A1_PARTS', 'cqt')
            for c in (range(2) if 'c' in PARTS else []):
                bk = nbank(); fm(C_GA + 128 * c, bk)
                tk.op(ACT, lambda c=c, bk=bk: nc.scalar.copy(out=gaS[:, c, :], in_=ps[bk][:, 0:N]), reads=[pb[bk]], writes=[bga])
            for c in (range(2) if 'c' in PARTS else []):
                bk = nbank(); fm(C_GC + 128 * c, bk)
                tk.op(ACT, lambda c=c, bk=bk: nc.scalar.copy(out=gcS[:, c, :], in_=ps[bk][:, 0:N]), reads=[pb[bk]], writes=[bgc])
            for c in (range(2) if 'c' in PARTS else []):
                bk = nbank(); fm(C_XA + 128 * c, bk)
                tk.op(DVE, lambda c=c, bk=bk: nc.vector.tensor_tensor(out=ub[:, c, 2:2 + N], in0=ps[bk][:, 0:N], in1=gcS[:, c, :],
                                                                        op=ALU.mult), reads=[pb[bk], bgc], writes=[bub])
                tk.op(DVE, lambda c=c: nc.vector.tensor_scalar(out=cacc, in0=ub[:, c, 2:2 + N], scalar1=cw[:, c, 2:3], scalar2=None,
                                                               op0=ALU.mult), reads=[bub, bwin], writes=[bcacc])
                tk.op(DVE, lambda c=c: nc.vector.scalar_tensor_tensor(out=cacc, in0=ub[:, c, 1:1 + N], scalar=cw[:, c, 1:2], in1=cacc,
                                                                      op0=ALU.mult, op1=ALU.add), reads=[bub, bwin, bcacc], writes=[bcacc])
                tk.op(DVE, lambda c=c: nc.vector.scalar_tensor_tensor(out=cacc, in0=ub[:, c, 0:N], scalar=cw[:, c, 0:1], in1=cacc,
                                                                      op0=ALU.mult, op1=ALU.add), reads=[bub, bwin, bcacc], writes=[bcacc])
                tk.op(DVE, lambda c=c: nc.vector.tensor_tensor(out=mixT[:, c, pos0:pos0 + N], in0=cacc, in1=gaS[:, c, :], op=ALU.mult),
                      reads=[bcacc, bga], writes=[bmix])
                if g == NT // 4 - 1:
                    with ncdma():
                        tk.dma(SP, ds_take(), [(convp[l, :, c * 128:(c + 1) * 128].rearrange("j p -> p j"), ub[:, c, N:N + 2])],
                               reads=[bub], writes=[DB("OUT")])
                tk.op(POOL, lambda c=c: nc.gpsimd.tensor_copy(out=ub[:, c, 0:2], in_=ub[:, c, N:N + 2]), reads=[bub], writes=[bub])
            for ci, col in enumerate([C_QB, C_QB + 128, C_QB + 256, C_KB, C_KB + 128, C_KB + 256,
                                      C_QC, C_QC + 128, C_QC + 256, C_KC] if 'q' in PARTS else []):
                bk = nbank(); fm(col, bk)
                s_ap, s_b, s_d = qst.next()
                e = ACT if ci % 2 == 0 else DVE
                if e is ACT:
                    tk.op(ACT, lambda s_ap=s_ap, bk=bk: nc.scalar.copy(out=s_ap, in_=ps[bk][:, 0:N]), reads=[pb[bk]], writes=[s_b])
                else:
                    tk.op(DVE, lambda s_ap=s_ap, bk=bk: nc.vector.tensor_copy(out=s_ap, in_=ps[bk][:, 0:N]), reads=[pb[bk]], writes=[s_b])
                tk.dma(SP, s_d, [(QK[ci, :, pos0:pos0 + N], s_ap)], reads=[s_b], writes=[DB("QK", ci, g)])
            for tt in (range(4) if 't' in PARTS else []):
                t = 4 * g + tt
                hs = lambda k, tt=tt: h_ap[:, k, tt * 128:(tt + 1) * 128]
                bk = nbank()
                tk.op(PE, mm_acc(ps[bk][:, 0:384], hs, lambda k: win[:, k, C_VB:C_VB + 384], 8), reads=[bwin, h_b], writes=[pb[bk]])
                s_ap, s_b, s_d = vst.next()
                tk.op(ACT, lambda s_ap=s_ap, bk=bk: nc.scalar.copy(out=s_ap[:, 0:384], in_=ps[bk][:, 0:384]), reads=[pb[bk]], writes=[s_b])
                if t >= NT // 2:
                    f_ap, f_b, f_d = fst.next()
                    tk.op(DVE, lambda f_ap=f_ap, bk=bk: nc.vector.tensor_copy(out=f_ap, in_=ps[bk][:, 0:384]), reads=[pb[bk]], writes=[f_b])
                    tk.dma(SP, f_d, [(bvp[l, (t - 16) * 128:(t - 15) * 128, :], f_ap)], reads=[f_b], writes=[DB("OUT")])
                    bk2 = nbank()
                    tk.op(PE, mm_acc(ps[bk2][:, 0:384], hs, lambda k: win[:, k, C_KB:C_KB + 384], 8), reads=[bwin, h_b], writes=[pb[bk2]])
                    f_ap, f_b, f_d = fst.next()
                    tk.op(DVE, lambda f_ap=f_ap, bk2=bk2: nc.vector.tensor_copy(out=f_ap, in_=ps[bk2][:, 0:384]), reads=[pb[bk2]], writes=[f_b])
                    tk.dma(SP, f_d, [(bkp[l, (t - 16) * 128:(t - 15) * 128, :], f_ap)], reads=[f_b], writes=[DB("OUT")])
                bk3 = nbank()
                tk.op(PE, mm_acc(ps[bk3][:, 0:256], hs, lambda k: win[:, k, C_KC:C_KC + 256], 8), reads=[bwin, h_b], writes=[pb[bk3]])
                tk.op(ACT, lambda s_ap=s_ap, bk3=bk3: nc.scalar.copy(out=s_ap[:, 384:512], in_=ps[bk3][:, 128:256]), reads=[pb[bk3]], writes=[s_b])
                tk.dma(SP, s_d, [(VB[t * 128:(t + 1) * 128, :], s_ap[:, 0:384]), (VC[t * 128:(t + 1) * 128, :], s_ap[:, 384:512])],
                       reads=[s_b], writes=[DB("V", t)])
                if t == NT - 1:
                    f_ap, f_b, f_d = fst.next()
                    tk.op(DVE, lambda f_ap=f_ap, bk3=bk3: nc.vector.tensor_copy(out=f_ap[:, 0:256], in_=ps[bk3][:, 0:256]), reads=[pb[bk3]], writes=[f_b])
                    tk.dma(SP, f_d, [(ckp[l], f_ap[:, 0:128]), (cvp[l], f_ap[:, 128:256])], reads=[f_b], writes=[DB("OUT")])
        sb.release(m2)
        tk.barrier()
        return m, win, cw, bwin, nm

    def phase_a1_sample(l, mixT, bmix, qkS, bqkS, win, cw, bwin, nm):
        gsS = sb.f32(D); shS = sb.f32(D); tmpx = sb.f32(D)
        bmodS, btmp = Buf(), Buf()
        tk.dma(SP, ds_take(), [(gsS, MODS[l, 1]), (shS, MODS[l, 0])], reads=mod_bufs(1, l, 1) + mod_bufs(1, l, 0), writes=[bmodS])
        xt = sb.f32(D); bx = Buf()
        src, sbufs = x_src(l, NT)
        tk.dma(SP, ds_take(), [(xt, src)], reads=sbufs, writes=[bx])
        hT = sb.bf16(8 * 128).rearrange("p (k t) -> p k t", k=8); bh = Buf()
        norm_tile_sample(nm, xt, bx, hT, bh, (0, 1), gsS, shS, bmodS, (tmpx, btmp))
        N = 128
        bki = [2]

        def nbank():
            b = bki[0]
            bki[0] = 2 + (bki[0] - 2 + 1) % 6
            return b

        def fm(col, bank):
            tk.op(PE, mm_acc(ps[bank][:, 0:N], lambda k: win[:, k, col:col + 128], lambda k: hT[:, k, :], 8),
                  reads=[bwin, bh], writes=[pb[bank]])
        gaS = sb.f32(256).rearrange("p (c t) -> p c t", c=2); gcS = sb.f32(256).rearrange("p (c t) -> p c t", c=2)
        ue = sb.f32(2 * 160).rearrange("p (c b j) -> p c b j", c=2, b=NB)
        stt = sb.f32(256); cacc = sb.f32(128); utmp = sb.f32(64).rearrange("p (c q) -> p c q", c=2); urow = sb.f32(256)
        bga, bgc, bue, bst, bcacc, butmp, burow = Buf(), Buf(), Buf(), Buf(), Buf(), Buf(), Buf()
        tk.dma(SP, ds_take(), [(stt[0:32, :], sconv[l])], writes=[bst])
        bk = nbank()

        def tr_state():
            for c in range(2):
                i = nc.tensor.transpose(out=ps[bk][:, c * 32:(c + 1) * 32], in_=stt[0:32, c * 128:(c + 1) * 128], identity=identf[0:32, 0:32])
            return i
        tk.op(PE, tr_state, reads=[bst, bC], writes=[pb[bk]])
        tk.op(DVE, lambda: nc.vector.tensor_copy(out=ue[:, :, :, 0:2], in_=ps[bk][:, 0:64].rearrange("p (c b j) -> p c b j", c=2, b=NB)),
              reads=[pb[bk]], writes=[bue])
        for c in range(2):
            b1 = nbank(); fm(C_GA + 128 * c, b1)
            tk.op(ACT, lambda c=c, b1=b1: nc.scalar.copy(out=gaS[:, c, :], in_=ps[b1][:, 0:N]), reads=[pb[b1]], writes=[bga])
            b2 = nbank(); fm(C_GC + 128 * c, b2)
            tk.op(ACT, lambda c=c, b2=b2: nc.scalar.copy(out=gcS[:, c, :], in_=ps[b2][:, 0:N]), reads=[pb[b2]], writes=[bgc])
            b3 = nbank(); fm(C_XA + 128 * c, b3)
            tk.op(DVE, lambda c=c, b3=b3: nc.vector.tensor_tensor(out=ue[:, c, :, 2:10], in0=ps[b3][:, 0:N].rearrange("p (b i) -> p b i", b=NB),
                                                                    in1=gcS[:, c, :].rearrange("p (b i) -> p b i", b=NB), op=ALU.mult),
                  reads=[pb[b3], bgc, bue], writes=[bue])
            ca3 = cacc.rearrange("p (b i) -> p b i", b=NB)
            tk.op(DVE, lambda c=c: nc.vector.tensor_scalar(out=ca3, in0=ue[:, c, :, 2:10], scalar1=cw[:, c, 2:3], scalar2=None, op0=ALU.mult),
                  reads=[bue, bwin], writes=[bcacc])
            tk.op(DVE, lambda c=c: nc.vector.scalar_tensor_tensor(out=ca3, in0=ue[:, c, :, 1:9], scalar=cw[:, c, 1:2], in1=ca3,
                                                                  op0=ALU.mult, op1=ALU.add), reads=[bue, bwin, bcacc], writes=[bcacc])
            tk.op(DVE, lambda c=c: nc.vector.scalar_tensor_tensor(out=ca3, in0=ue[:, c, :, 0:8], scalar=cw[:, c, 0:1], in1=ca3,
                                                                  op0=ALU.mult, op1=ALU.add), reads=[bue, bwin, bcacc], writes=[bcacc])
            tk.op(DVE, lambda c=c: nc.vector.tensor_tensor(out=mixT[:, c, SEQ:SEQ + N], in0=cacc, in1=gaS[:, c, :], op=ALU.mult),
                  reads=[bcacc, bga], writes=[bmix])
            tk.op(DVE, lambda c=c: nc.vector.tensor_copy(out=utmp[:, c, :].rearrange("p (b j) -> p b j", b=NB), in_=ue[:, c, :, 8:10]),
                  reads=[bue], writes=[butmp])
        bk = nbank()

        def tr_u():
            for c in range(2):
                i = nc.tensor.transpose(out=ps[bk][0:32, c * 128:(c + 1) * 128], in_=utmp[:, c, :], identity=identf)
            return i
        tk.op(PE, tr_u, reads=[butmp, bC], writes=[pb[bk]])
        tk.op(DVE, lambda: nc.vector.tensor_copy(out=urow[0:32, :], in_=ps[bk][0:32, 0:256]), reads=[pb[bk]], writes=[burow])
        tk.dma(SP, ds_take(), [(convs[l].rearrange("b j f -> (b j) f"), urow[0:32, :])], reads=[burow], writes=[DB("OUT")])
        for ci, col in enumerate([C_QB, C_QB + 128, C_QB + 256, C_KB, C_KB + 128, C_KB + 256, C_QC, C_QC + 128, C_QC + 256, C_KC]):
            b1 = nbank(); fm(col, b1)
            tk.op(ACT, lambda ci=ci, b1=b1: nc.scalar.copy(out=qkS[:, ci, :], in_=ps[b1][:, 0:N]), reads=[pb[b1]], writes=[bqkS])
        kvf = sb.f32(768 + 256); kvb = sb.bf16(512); bkvf, bkvb = Buf(), Buf()
        b1 = nbank()
        tk.op(PE, mm_acc(ps[b1][:, 0:384], lambda k: hT[:, k, :], lambda k: win[:, k, C_KB:C_KB + 384], 8), reads=[bwin, bh], writes=[pb[b1]])
        tk.op(ACT, lambda: nc.scalar.copy(out=kvf[:, 0:384], in_=ps[b1][:, 0:384]), reads=[pb[b1]], writes=[bkvf])
        b2 = nbank()
        tk.op(PE, mm_acc(ps[b2][:, 0:384], lambda k: hT[:, k, :], lambda k: win[:, k, C_VB:C_VB + 384], 8), reads=[bwin, bh], writes=[pb[b2]])
        tk.op(ACT, lambda: nc.scalar.copy(out=kvf[:, 384:768], in_=ps[b2][:, 0:384]), reads=[pb[b2]], writes=[bkvf])
        tk.op(DVE, lambda: nc.vector.tensor_copy(out=kvb[:, 0:384], in_=ps[b2][:, 0:384]), reads=[pb[b2]], writes=[bkvb])
        b3 = nbank()
        tk.op(PE, mm_acc(ps[b3][:, 0:256], lambda k: hT[:, k, :], lambda k: win[:, k, C_KC:C_KC + 256], 8), reads=[bwin, bh], writes=[pb[b3]])
        tk.op(ACT, lambda: nc.scalar.copy(out=kvf[:, 768:1024], in_=ps[b3][:, 0:256]), reads=[pb[b3]], writes=[bkvf])
        tk.op(DVE, lambda: nc.vector.tensor_copy(out=kvb[:, 384:512], in_=ps[b3][:, 128:256]), reads=[pb[b3]], writes=[bkvb])
        tk.dma(SP, ds_take(), [(VSB[:, :], kvb[:, 0:384]), (VSC[:, :], kvb[:, 384:512])], reads=[bkvb], writes=[DB("VS", l)])
        prs = []
        for b in range(NB):
            r = slice(8 * b, 8 * b + 8)
            prs += [(bks[l, b, WB - 8:WB, :], kvf[r, 0:384]), (bvs[l, b, WB - 8:WB, :], kvf[r, 384:768]),
                    (cks[l, b, 120:128, :], kvf[r, 768:896]), (cvs[l, b, 120:128, :], kvf[r, 896:1024])]
        tk.dma(SP, ds_take(), prs, reads=[bkvf], writes=[DB("OUT")])
        prs = []
        for b in range(NB):
            prs += [(bks[l, b, 0:WB - 8, :], cbk[l, b, 8:WB, :]), (bvs[l, b, 0:WB - 8, :], cbv[l, b, 8:WB, :]),
                    (cks[l, b, 0:120, :], cck[l, b, 8:128, :]), (cvs[l, b, 0:120, :], ccv[l, b, 8:128, :])]
        tk.dma(SP, ds_take(), prs, writes=[DB("OUT")])

    def phase_a2(l, mixT, bmix):
        m = phase_begin()
        qT = sb.bf16(SEQ); kT = sb.bf16(SEQ); bq, bkk = Buf(), Buf(); dq, dk = ds_take(), ds_take()
        Vd = [sb.bf16(32 * 128).rearrange("p (t f) -> p t f", t=32) for _ in range(3)]
        bV = [Buf() for _ in range(3)]; dV = ds_take(3)
        acc = sb.f32(2 * SEQ).rearrange("p (o t) -> p o t", o=2); bacc = Buf()
        Pst = Ring([sb.bf16(512) for _ in range(4)], None, "P")
        es = sb.f32(1); bes = Buf(); des = ds_take()
        sbanks = ((0, 1), (2, 3)); obanks = (4, 5)
        itc = [0]

        def sl(d, r, blk):
            s0 = d * 128 * blk + r
            return slice(s0, s0 + 127 * d + 1, d)

        def load_V(dst, bdst, dd, src_t, rowlen, coloff, d):
            nblk = 32 // d
            dv = dst.rearrange("p (r b) f -> p r b f", r=d)
            prs = []
            if d == 1:
                for q4 in range(4):
                    ap = bass.AP(src_t.tensor, coloff + q4 * 8 * 128 * rowlen, [[rowlen, 128], [128 * rowlen, 8], [1, 128]])
                    prs.append((dst[:, q4 * 8:(q4 + 1) * 8, :], ap))
            else:
                for r in range(d):
                    ap = bass.AP(src_t.tensor, coloff + r * rowlen, [[d * rowlen, 128], [d * 128 * rowlen, nblk], [1, 128]])
                    prs.append((dv[:, r, :, :], ap))
            with ncdma():
                tk.dma(SP, dd, prs, reads=[DB("V", t) for t in range(NT)], writes=[bdst])

        for kind, j in [("B", 0), ("B", 1), ("B", 2), ("C", 0), ("C", 1), ("C", 2)]:
            isB = kind == "B"
            qc_, kc_ = (j, 3 + j) if isB else (6 + j, 9)
            tk.dma(SP, dq, [(qT, QK[qc_])], reads=[DB("QK", qc_, g) for g in range(8)], writes=[bq])
            if isB or j == 0:
                tk.dma(SP, dk, [(kT, QK[kc_])], reads=[DB("QK", kc_, g) for g in range(8)], writes=[bkk])
            if isB:
                for di, d in enumerate(DILS):
                    load_V(Vd[di], bV[di], dV[di], VB, 384, 128 * j, d)
            elif j == 0:
                load_V(Vd[0], bV[0], dV[0], VC, 128, 0, 1)
            if not isB:
                tk.dma(SP, des, [(es, sinkP[l, j, :].rearrange("(p o) -> p o", o=1))], writes=[bes])
                tk.op(ACT, lambda: nc.scalar.activation(out=es, in_=es, func=AF.Exp), reads=[bes], writes=[bes])
            tk.op(POOL, lambda: nc.gpsimd.memset(acc, 0.0), writes=[bacc])
            mask = maskB if isB else maskC
            for di, d in enumerate(DILS if isB else (1,)):
                nblk = 32 // d
                Vt, bVt = Vd[di], bV[di]
                for r in range(d):
                    for b0 in range(0, nblk, 2):
                        blks = (b0, b0 + 1)
                        it = itc[0]; itc[0] += 1
                        Pp = []
                        for s in (0, 1):
                            bank = sbanks[s][it % 2]

                            def f(s=s, bank=bank):
                                for bi, blk in enumerate(blks):
                                    qs = qT[64 * s:64 * s + 64, sl(d, r, blk)]
                                    for w, kb in enumerate((blk - 1, blk)):
                                        if kb < 0:
                                            continue
                                        i = nc.tensor.matmul(ps[bank][:, (bi * 2 + w) * 128:(bi * 2 + w + 1) * 128],
                                                             lhsT=kT[64 * s:64 * s + 64, sl(d, r, kb)], rhs=qs, start=True, stop=True)
                                return i
                            tk.op(PE, f, reads=[bq, bkk], writes=[pb[bank]])
                            p_ap, p_b, _ = Pst.next()
                            tk.op(ACT, lambda p_ap=p_ap, bank=bank: nc.scalar.activation(out=p_ap, in_=ps[bank][:, :], func=AF.Exp, scale=SC),
                                  reads=[pb[bank]], writes=[p_b])
                            tk.op(POOL, lambda p_ap=p_ap: nc.gpsimd.tensor_tensor(out=p_ap, in0=p_ap, in1=mask, op=ALU.mult),
                                  reads=[p_b, bC], writes=[p_b])
                            Pp.append((p_ap, p_b))
                        obank = obanks[it % 2]

                        def gpv():
                            for bi, blk in enumerate(blks):
                                for s in (0, 1):
                                    p_ap = Pp[s][0]
                                    ws = [w for w in (0, 1) if blk - 1 + w >= 0]
                                    for which in (0, 1):
                                        for wi, w in enumerate(ws):
                                            kb = blk - 1 + w
                                            lhsT = Vt[:, r * nblk + kb, 64 * s:64 * s + 64] if which == 0 else onesb[:, 0:64]
                                            i = nc.tensor.matmul(ps[obank][64 * s:64 * s + 64, (bi * 2 + which) * 128:(bi * 2 + which + 1) * 128],
                                                                 lhsT=lhsT, rhs=p_ap[:, (bi * 2 + w) * 128:(bi * 2 + w + 1) * 128],
                                                                 start=(wi == 0), stop=(wi == len(ws) - 1))
                            return i
                        tk.op(PE, gpv, reads=[Pp[0][1], Pp[1][1], bVt, bC], writes=[pb[obank]])
                        for bi, blk in enumerate(blks):
                            tk.op(DVE, lambda bi=bi, blk=blk, obank=obank: nc.vector.tensor_tensor(
                                out=acc[:, :, sl(d, r, blk)], in0=ps[obank][:, bi * 256:(bi + 1) * 256].rearrange("p (o t) -> p o t", o=2),
                                in1=acc[:, :, sl(d, r, blk)], op=ALU.add), reads=[pb[obank], bacc], writes=[bacc])
            if not isB:
                tk.op(DVE, lambda: nc.vector.tensor_scalar(out=acc[:, 1, :], in0=acc[:, 1, :], scalar1=es[:, 0:1], scalar2=None, op0=ALU.add),
                      reads=[bacc, bes], writes=[bacc])
            tk.op(DVE, lambda: nc.vector.reciprocal(out=acc[:, 1, :], in_=acc[:, 1, :]), reads=[bacc], writes=[bacc])
            ch = (2 + j) if isB else (5 + j)
            tk.op(DVE, lambda ch=ch: nc.vector.tensor_tensor(out=mixT[:, ch, 0:SEQ], in0=acc[:, 0, :], in1=acc[:, 1, :], op=ALU.mult),
                  reads=[bacc], writes=[bmix])
        phase_end(m)

    def phase_a2s(l, mixT, bmix, qkS, bqkS):
        m = phase_begin()
        Kc = Ring([sb.bf16(16 * 384).rearrange("p (t f) -> p t f", t=16) for _ in range(2)], ds_take(2, sw=True), "Kc")
        Vc = Ring([sb.bf16(16 * 384).rearrange("p (t f) -> p t f", t=16) for _ in range(2)], ds_take(2, sw=True), "Vc")
        KT = sb.bf16(3 * WB).rearrange("p (j t) -> p j t", j=3); bKT = Buf()
        Vn = sb.bf16(NB * 384).rearrange("p (b f) -> p b f", b=NB); Vcn = sb.bf16(NB * 128).rearrange("p (b f) -> p b f", b=NB)
        bVn = Buf()
        tk.dma(SP, ds_take(), [(Vn[0:8], VSB.rearrange("(b i) f -> i b f", i=8)), (Vcn[0:8], VSC.rearrange("(b i) f -> i b f", i=8))],
               reads=[DB("VS", l)], writes=[bVn])
        Ps = [sb.bf16(408), sb.bf16(408)]; bPs = [Buf(), Buf()]
        rl = sb.f32(8); brl = Buf()
        ysb = sb.bf16(NB * 384).rearrange("p (b f) -> p b f", b=NB); bys = Buf()
        ycs = sb.bf16(NB * 128).rearrange("p (b f) -> p b f", b=NB); byc = Buf()
        es24 = sb.f32(2); bes = Buf()
        tk.dma(SP, ds_take(), [(es24[0:24, :], sinkS[l])], writes=[bes])
        tk.op(ACT, lambda: nc.scalar.activation(out=es24[0:24, :], in_=es24[0:24, :], func=AF.Exp), reads=[bes], writes=[bes])
        Kcc = Ring([sb.bf16(128) for _ in range(2)], ds_take(2, sw=True), "Kcc"); Vcc = Ring([sb.bf16(128) for _ in range(2)], ds_take(2, sw=True), "Vcc")
        KcT = sb.bf16(128); bKcT = Buf()
        SA, SBk = 0, 1
        k_d0 = ds_take(sw=True)
        for b in range(NB):
            qsl = slice(8 * b, 8 * b + 8)
            k_ap, k_b, k_d = Kc.next(); v_ap, v_b, v_d = Vc.next()
            ksrc = cbk[l, b].rearrange("(t p) f -> p t f", p=128); vsrc = cbv[l, b].rearrange("(t p) f -> p t f", p=128)
            for hh in range(2):
                tk.dma(POOL, k_d0, [(k_ap[:, 8 * hh:8 * hh + 8, :], ksrc[:, 8 * hh:8 * hh + 8, :])], writes=[k_b])
            for hh in range(2):
                tk.dma(POOL, k_d0, [(v_ap[:, 8 * hh:8 * hh + 8, :], vsrc[:, 8 * hh:8 * hh + 8, :])], writes=[v_b])
            for j in range(3):
                for tq in range(2):
                    bank = 2 + (j * 2 + tq) % 2

                    def ftr(j=j, tq=tq, bank=bank):
                        for u in range(8):
                            i = nc.tensor.transpose(out=psb[bank][:, u * 128:(u + 1) * 128], in_=k_ap[:, 8 * tq + u, 128 * j:128 * j + 128], identity=identb)
                        return i
                    tk.op(PE, ftr, reads=[k_b, bC], writes=[pb[bank]])
                    e = DVE if tq == 0 else ACT
                    if e is DVE:
                        tk.op(DVE, lambda j=j, tq=tq, bank=bank: nc.vector.tensor_copy(out=KT[:, j, tq * 1024:(tq + 1) * 1024], in_=psb[bank][:, :]),
                              reads=[pb[bank]], writes=[bKT])
                    else:
                        tk.op(ACT, lambda j=j, tq=tq, bank=bank: nc.scalar.copy(out=KT[:, j, tq * 1024:(tq + 1) * 1024], in_=psb[bank][:, :]),
                              reads=[pb[bank]], writes=[bKT])
            for s in (0, 1):
                bank = (SA, SBk)[s]
                pr = slice(64 * s, 64 * s + 64)

                def fs(s=s, bank=bank, pr=pr):
                    for j in range(3):
                        for t in range(16):
                            c0 = (j * 16 + t) * 8
                            i = nc.tensor.matmul(ps[bank][:, c0:c0 + 8], lhsT=KT[pr, j, t * 128:(t + 1) * 128], rhs=qkS[pr, j, qsl], start=True, stop=True)
                        i = nc.tensor.matmul(ps[bank][0:8, 384 + j * 8:384 + j * 8 + 8], lhsT=qkS[pr, 3 + j, qsl], rhs=qkS[pr, j, qsl], start=True, stop=True)
                    return i
                tk.op(PE, fs, reads=[bKT, bqkS], writes=[pb[bank]])
                tk.op(ACT, lambda s=s, bank=bank: nc.scalar.activation(out=Ps[s][:, 0:384], in_=ps[bank][:, 0:384], func=AF.Exp, scale=SC),
                      reads=[pb[bank]], writes=[bPs[s]])
                tk.op(ACT, lambda s=s, bank=bank: nc.scalar.activation(out=Ps[s][0:8, 384:408], in_=ps[bank][0:8, 384:408], func=AF.Exp, scale=SC),
                      reads=[pb[bank]], writes=[bPs[s]])
                mb = bass.AP(multB.tensor, multB.offset, [list(multB.ap[0]), [0, 3], [1, 128]])
                mbn = bass.AP(multBn.tensor, multBn.offset, [[multBn.ap[0][0], 8], [0, 3], [1, 8]])
                tk.op(DVE, lambda s=s, mb=mb: nc.vector.tensor_tensor(out=Ps[s][:, 0:384].rearrange("p (j c) -> p j c", j=3),
                                                                     in0=Ps[s][:, 0:384].rearrange("p (j c) -> p j c", j=3), in1=mb, op=ALU.mult),
                      reads=[bPs[s], bC], writes=[bPs[s]])
                tk.op(DVE, lambda s=s, mbn=mbn: nc.vector.tensor_tensor(out=Ps[s][0:8, 384:408].rearrange("p (j c) -> p j c", j=3),
                                                                       in0=Ps[s][0:8, 384:408].rearrange("p (j c) -> p j c", j=3), in1=mbn, op=ALU.mult),
                      reads=[bPs[s], bC], writes=[bPs[s]])
            ob = 4 + b % 2

            def fpv():
                for j in range(3):
                    for s in (0, 1):
                        h = 2 * j + s
                        for which in (0, 1):
                            o_ap = ps[ob][0:8, h * 65:h * 65 + 64] if which == 0 else ps[ob][0:8, h * 65 + 64:h * 65 + 65]
                            for t in range(16):
                                c0 = (j * 16 + t) * 8
                                rhs = v_ap[:, t, h * 64:(h + 1) * 64] if which == 0 else onesb[:, 0:1]
                                nc.tensor.matmul(o_ap, lhsT=Ps[s][:, c0:c0 + 8], rhs=rhs, start=(t == 0), stop=False)
                            rhs = Vn[0:8, b, h * 64:(h + 1) * 64] if which == 0 else onesb[0:8, 0:1]
                            i = nc.tensor.matmul(o_ap, lhsT=Ps[s][0:8, 384 + j * 8:384 + j * 8 + 8], rhs=rhs, start=False, stop=True)
                return i
            tk.op(PE, fpv, reads=[bPs[0], bPs[1], v_b, bVn, bC], writes=[pb[ob]])
            ov = ps[ob][0:8, 0:390].rearrange("p (h c) -> p h c", h=6)
            tk.op(DVE, lambda ov=ov: nc.vector.reciprocal(out=rl[0:8, 0:6].rearrange("p (h o) -> p h o", o=1), in_=ov[:, :, 64:65]),
                  reads=[pb[ob]], writes=[brl])
            rlb = bass.AP(rl.tensor, rl.offset, [[rl.ap[0][0], 8], [1, 6], [0, 64]])
            tk.op(DVE, lambda ov=ov, rlb=rlb, b=b: nc.vector.tensor_tensor(out=ysb[0:8, b, :].rearrange("p (h c) -> p h c", h=6), in0=ov[:, :, 0:64],
                                                                         in1=rlb, op=ALU.mult), reads=[pb[ob], brl], writes=[bys])
            kc_ap, kc_b, kc_d = Kcc.next(); vc_ap, vc_b, vc_d = Vcc.next()
            tk.dma(POOL, k_d0, [(kc_ap, cck[l, b])], writes=[kc_b])
            tk.dma(POOL, k_d0, [(vc_ap, ccv[l, b])], writes=[vc_b])
            tk.op(PE, lambda: nc.tensor.transpose(out=psb[2][:, 0:128], in_=kc_ap, identity=identb), reads=[kc_b, bC], writes=[pb[2]])
            tk.op(ACT, lambda: nc.scalar.copy(out=KcT, in_=psb[2][:, 0:128]), reads=[pb[2]], writes=[bKcT])
            for g in (0, 1):
                bank = (SA, SBk)[g]
                pr = slice(64 * g, 64 * g + 64)

                def fsc(g=g, bank=bank, pr=pr):
                    nc.tensor.matmul(ps[bank][:, 0:24], lhsT=KcT[pr, :], rhs=qkS[pr, 6:9, qsl], start=True, stop=True)
                    return nc.tensor.matmul(ps[bank][0:8, 24:48], lhsT=qkS[pr, 9, qsl], rhs=qkS[pr, 6:9, qsl], start=True, stop=True)
                tk.op(PE, fsc, reads=[bKcT, bqkS], writes=[pb[bank]])
                tk.op(ACT, lambda g=g, bank=bank: nc.scalar.activation(out=Ps[g][:, 0:24], in_=ps[bank][:, 0:24], func=AF.Exp, scale=SC),
                      reads=[pb[bank]], writes=[bPs[g]])
                tk.op(ACT, lambda g=g, bank=bank: nc.scalar.activation(out=Ps[g][0:8, 24:48], in_=ps[bank][0:8, 24:48], func=AF.Exp, scale=SC),
                      reads=[pb[bank]], writes=[bPs[g]])
                mc = bass.AP(maskCs.tensor, maskCs.offset, [list(maskCs.ap[0]), [0, 3], [1, 8]])
                mcn = bass.AP(maskCn.tensor, maskCn.offset, [[maskCn.ap[0][0], 8], [0, 3], [1, 8]])
                tk.op(DVE, lambda g=g, mc=mc: nc.vector.tensor_tensor(out=Ps[g][:, 0:24].rearrange("p (j c) -> p j c", j=3),
                                                                     in0=Ps[g][:, 0:24].rearrange("p (j c) -> p j c", j=3), in1=mc, op=ALU.mult),
                      reads=[bPs[g], bC], writes=[bPs[g]])
                tk.op(DVE, lambda g=g, mcn=mcn: nc.vector.tensor_tensor(out=Ps[g][0:8, 24:48].rearrange("p (j c) -> p j c", j=3),
                                                                       in0=Ps[g][0:8, 24:48].rearrange("p (j c) -> p j c", j=3), in1=mcn, op=ALU.mult),
                      reads=[bPs[g], bC], writes=[bPs[g]])
            oc = 6 + b % 2

            def fpc():
                for g in (0, 1):
                    for which in (0, 1):
                        o_ap = ps[oc][0:24, g * 65:g * 65 + 64] if which == 0 else ps[oc][0:24, g * 65 + 64:g * 65 + 65]
                        nc.tensor.matmul(o_ap, lhsT=Ps[g][:, 0:24], rhs=(vc_ap[:, g * 64:(g + 1) * 64] if which == 0 else onesb[:, 0:1]), start=True, stop=False)
                        i = nc.tensor.matmul(o_ap, lhsT=Ps[g][0:8, 24:48], rhs=(Vcn[0:8, b, g * 64:(g + 1) * 64] if which == 0 else onesb[0:8, 0:1]),
                                             start=False, stop=True)
                return i
            tk.op(PE, fpc, reads=[bPs[0], bPs[1], vc_b, bVn, bC], writes=[pb[oc]])
            ocv = ps[oc][0:24, 0:130].rearrange("p (g c) -> p g c", g=2)
            tk.op(DVE, lambda ocv=ocv: nc.vector.tensor_tensor(out=rl[0:24, 6:8].rearrange("p (g o) -> p g o", o=1), in0=ocv[:, :, 64:65],
                                                               in1=es24[0:24, :].rearrange("p (g o) -> p g o", o=1), op=ALU.add),
                  reads=[pb[oc], bes, brl], writes=[brl])
            tk.op(DVE, lambda: nc.vector.reciprocal(out=rl[0:24, 6:8], in_=rl[0:24, 6:8]), reads=[brl], writes=[brl])
            rcb = bass.AP(rl.tensor, rl.offset + 6, [[rl.ap[0][0], 24], [1, 2], [0, 64]])
            tk.op(DVE, lambda ocv=ocv, rcb=rcb, b=b: nc.vector.tensor_tensor(out=ycs[0:24, b, :].rearrange("p (g c) -> p g c", g=2), in0=ocv[:, :, 0:64],
                                                                           in1=rcb, op=ALU.mult), reads=[pb[oc], brl], writes=[byc])
        def ftb():
            for b in range(NB):
                for j in range(3):
                    c0 = (j * NB + b) * 8
                    i = nc.tensor.transpose(out=psb[2][:, c0:c0 + 8], in_=ysb[0:8, b, 128 * j:128 * j + 128], identity=identb[0:8, 0:8])
            return i
        tk.op(PE, ftb, reads=[bys, bC], writes=[pb[2]])
        tk.op(DVE, lambda: nc.vector.tensor_copy(out=mixT[:, 2:5, SEQ:SEQ + 128], in_=psb[2][:, 0:384].rearrange("p (j t) -> p j t", j=3)),
              reads=[pb[2]], writes=[bmix])

        def ftc():
            for b in range(NB):
                i = nc.tensor.transpose(out=psb[3][:, b * 24:(b + 1) * 24], in_=ycs[0:24, b, :], identity=identb[0:24, 0:24])
            return i
        tk.op(PE, ftc, reads=[byc, bC], writes=[pb[3]])
        tk.op(DVE, lambda: nc.vector.tensor_copy(out=mixT[:, 5:8, SEQ:SEQ + 128].rearrange("p h (b i) -> p h b i", b=NB),
                                                 in_=psb[3][:, 0:384].rearrange("p (b h i) -> p h b i", b=NB, h=3)),
              reads=[pb[3]], writes=[bmix])
        phase_end(m)

    def phase_a3(l, mixT, bmix):
        m = phase_begin()
        wo = sb.bf16(8 * D).rearrange("p (k c) -> p k c", k=8); bwo = Buf()
        prs = []
        for k in range(5):
            for h2 in range(2):
                prs.append((wo[:, k, h2 * 512:(h2 + 1) * 512], w_out[l, k * 128:(k + 1) * 128, h2 * 512:(h2 + 1) * 512]))
        for j in range(3):
            for q, hh in enumerate((j, j + 3)):
                r0 = 640 + 64 * hh
                for h2 in range(2):
                    prs.append((wo[64 * q:64 * q + 64, 5 + j, h2 * 512:(h2 + 1) * 512], w_out[l, r0:r0 + 64, h2 * 512:(h2 + 1) * 512]))
        dsw = ds_take(sw=True)
        for pr in prs:
            tk.dma(POOL, dsw, [pr], writes=[bwo])
        gp = sb.f32(D); gsm = sb.f32(D); bg = Buf()
        tk.dma(SP, ds_take(), [(gp, MODP[l, 2]), (gsm, MODS[l, 2])], reads=mod_bufs(0, l, 2) + mod_bufs(1, l, 2), writes=[bg])
        xr = Ring([sb.f32(D) for _ in range(3)], ds_take(3), "x")
        orr = Ring([sb.f32(D) for _ in range(2)], ds_take(2), "o")
        for t in range(NT + 1):
            x_ap, x_b, x_d = xr.next()
            src, sbufs = x_src(l, t)
            tk.dma(SP, x_d, [(x_ap, src)], reads=sbufs, writes=[x_b])
            o_ap, o_b, o_d = orr.next()
            gate = gp if t < NT else gsm
            for h2 in range(2):
                bank = (t % 2) * 2 + h2
                tk.op(PE, mm_acc(ps[bank][:, :], lambda k: mixT[:, k, t * 128:(t + 1) * 128], lambda k: wo[:, k, h2 * 512:(h2 + 1) * 512], 8),
                      reads=[bmix, bwo], writes=[pb[bank]])
                hs = slice(h2 * 512, (h2 + 1) * 512)
                tk.op(DVE, lambda bank=bank, hs=hs: nc.vector.tensor_tensor(out=o_ap[:, hs], in0=ps[bank][:, :], in1=gate[:, hs], op=ALU.mult),
                      reads=[pb[bank], bg], writes=[o_b])
                tk.op(POOL, lambda hs=hs: nc.gpsimd.tensor_tensor(out=o_ap[:, hs], in0=o_ap[:, hs], in1=x_ap[:, hs], op=ALU.add),
                      reads=[o_b, x_b], writes=[o_b])
            tk.dma(SP, o_d, [(XA[t * 128:(t + 1) * 128, :], o_ap)], reads=[o_b], writes=[DB("XA", t)])
        phase_end(m)

    def phase_b(l):
        m = phase_begin()
        wg = sb.bf16(8 * 2 * DFF).rearrange("p (k c) -> p k c", k=8); wd = sb.bf16(NFC * D).rearrange("p (k c) -> p k c", k=NFC)
        bwg, bwd = Buf(), Buf()
        srcg = w_gu[l].rearrange("(k p) c -> p k c", p=128)
        dsw = ds_take(sw=True)
        for c0 in range(0, 2 * DFF, 512):
            tk.dma(POOL, dsw, [(wg[:, :, c0:c0 + 512], srcg[:, :, c0:c0 + 512])], writes=[bwg])
        srcd = w_down[l].rearrange("(k p) c -> p k c", p=128)
        dsw2 = ds_take(sw=True)
        for k0 in range(0, NFC, 4):
            k1 = min(k0 + 4, NFC)
            for h2 in range(2):
                tk.dma(POOL, dsw2, [(wd[:, k0:k1, h2 * 512:(h2 + 1) * 512], srcd[:, k0:k1, h2 * 512:(h2 + 1) * 512])], writes=[bwd])
        nm = make_norm(l, 1)
        gp = sb.f32(D); bg = Buf()
        tk.dma(SP, ds_take(), [(gp, MODP[l, 5])], reads=mod_bufs(0, l, 5), writes=[bg])
        GT = 2
        xr = Ring([sb.f32(D) for _ in range(GT + 1)], ds_take(GT + 1), "x")
        hT = sb.bf16(8 * 128 * GT).rearrange("p (k t) -> p k t", k=8); bh = Buf()
        aT = sb.bf16(NFC * 128 * GT).rearrange("p (k t) -> p k t", k=NFC); baT = Buf()
        sg = Ring([sb.f32(128 * GT) for _ in range(2)], None, "sg")
        orr = Ring([sb.f32(D) for _ in range(2)], ds_take(2), "o")
        ss2 = sb.f32(1); rs2 = sb.f32(1); bss2, brs2 = Buf(), Buf()
        dst = XB if l < nlayers - 1 else None
        ngrp = NT // GT + 1
        for g in range(ngrp):
            tiles = [GT * g + i for i in range(GT)] if g < NT // GT else [NT]
            N = 128 * len(tiles)
            xts = []
            if g == NT // GT:
                mk2 = sb.mark()
                gsS = sb.f32(D); shS = sb.f32(D); gtS = sb.f32(D); bmS = Buf()
                tk.dma(SP, ds_take(), [(gsS, MODS[l, 4]), (shS, MODS[l, 3]), (gtS, MODS[l, 5])],
                       reads=mod_bufs(1, l, 4) + mod_bufs(1, l, 3) + mod_bufs(1, l, 5), writes=[bmS])
            for i, t in enumerate(tiles):
                x_ap, x_b, x_d = xr.next()
                tk.dma(SP, x_d, [(x_ap, XA[t * 128:(t + 1) * 128, :])], reads=[DB("XA", t)], writes=[x_b])
                xts.append((x_ap, x_b))
                if t < NT:
                    norm_tile_prompt(nm, x_ap, x_b, hT[:, :, i * 128:(i + 1) * 128], bh, (6, 7))
                else:
                    o_ap, o_b, _ = orr.next()
                    norm_tile_sample(nm, x_ap, x_b, hT[:, :, 0:128], bh, (6, 7), gsS, shS, bmS, (o_ap, o_b))
            for c in range(NFC):
                bg_, bu_ = (c % 2) * 2, (c % 2) * 2 + 1
                tk.op(PE, mm_acc(ps[bg_][:, 0:N], lambda k: wg[:, k, c * 128:(c + 1) * 128], lambda k: hT[:, k, 0:N], 8),
                      reads=[bwg, bh], writes=[pb[bg_]])
                tk.op(PE, mm_acc(ps[bu_][:, 0:N], lambda k: wg[:, k, DFF + c * 128:DFF + (c + 1) * 128], lambda k: hT[:, k, 0:N], 8),
                      reads=[bwg, bh], writes=[pb[bu_]])
                s_ap, s_b, _ = sg.next()
                tk.op(ACT, lambda s_ap=s_ap, bg_=bg_: nc.scalar.activation(out=s_ap[:, 0:N], in_=ps[bg_][:, 0:N], func=AF.Silu),
                      reads=[pb[bg_]], writes=[s_b])
                tk.op(DVE, lambda s_ap=s_ap, bu_=bu_, c=c: nc.vector.tensor_tensor(out=aT[:, c, 0:N], in0=ps[bu_][:, 0:N], in1=s_ap[:, 0:N], op=ALU.mult),
                      reads=[pb[bu_], s_b], writes=[baT])
            for i, t in enumerate(tiles):
                x_ap, x_b = xts[i]
                o_ap, o_b, o_d = orr.next()
                gate = gp if t < NT else gtS
                gb = [bg] if t < NT else [bmS]
                for h2 in range(2):
                    bank = 4 + h2
                    tk.op(PE, mm_acc(ps[bank][:, :], lambda k: aT[:, k, i * 128:(i + 1) * 128], lambda k: wd[:, k, h2 * 512:(h2 + 1) * 512], NFC),
                          reads=[baT, bwd], writes=[pb[bank]])
                    hs = slice(h2 * 512, (h2 + 1) * 512)
                    tk.op(DVE, lambda bank=bank, hs=hs: nc.vector.tensor_tensor(out=o_ap[:, hs], in0=ps[bank][:, :], in1=gate[:, hs], op=ALU.mult),
                          reads=[pb[bank]] + gb, writes=[o_b])
                    tk.op(POOL, lambda hs=hs: nc.gpsimd.tensor_tensor(out=o_ap[:, hs], in0=o_ap[:, hs], in1=x_ap[:, hs], op=ALU.add),
                          reads=[o_b, x_b], writes=[o_b])
                if dst is not None:
                    tk.dma(SP, o_d, [(dst[t * 128:(t + 1) * 128, :], o_ap)], reads=[o_b], writes=[DB("XB", t)])
                else:
                    tk.op(ACT, lambda: nc.scalar.activation(out=nm["junk"], in_=o_ap, func=AF.Square, accum_out=ss2[:, 0:1]),
                          reads=[o_b], writes=[nm["bjunk"], bss2])
                    rstd_from_ss(ss2, rs2, 1, bss2, brs2)
                    tk.op(DVE, lambda: nc.vector.scalar_tensor_tensor(out=o_ap, in0=o_ap, scalar=rs2[:, 0:1], in1=gfin, op0=ALU.mult, op1=ALU.mult),
                          reads=[o_b, brs2, bC], writes=[o_b])
                    od = yp[t * 128:(t + 1) * 128, :] if t < NT else ys
                    tk.dma(SP, o_d, [(od, o_ap)], reads=[o_b], writes=[DB("OUT")])
        phase_end(m)

    phase_mod()
    if stop_after == "mod":
        tk.barrier()
        return nc
    for l in range(nlayers):
        lm = phase_begin()
        mixT = sb.bf16(8 * (SEQ + 128)).rearrange("p (k t) -> p k t", k=8); bmix = Buf()
        qkS = sb.bf16(10 * 128).rearrange("p (k t) -> p k t", k=10); bqkS = Buf()
        ds_keep = dsi[0]
        m1, win, cw, bwin, nm = phase_a1(l, mixT, bmix)
        import os
        if not os.environ.get("SKIP_A1S"):
            phase_a1_sample(l, mixT, bmix, qkS, bqkS, win, cw, bwin, nm)
        phase_end(m1)
        if stop_after == ("a1", l):
            tk.barrier()
            return nc
        phase_a2(l, mixT, bmix)
        phase_a2s(l, mixT, bmix, qkS, bqkS)
        if stop_after == ("a2", l):
            tk.barrier()
            return nc
        phase_a3(l, mixT, bmix)
        sb.release(lm)
        tk.barrier()
        phase_b(l)
    tk.barrier()
    return nc


def _consts():
    c = {}
    c["c_ident"] = np.eye(128, dtype=np.float32)
    j = np.arange(128)[:, None]; i = np.arange(128)[None, :]
    prevB = (j >= i).astype(np.float32); prevC = (j > i).astype(np.float32); cur = (j <= i).astype(np.float32)
    c["c_maskB"] = np.concatenate([prevB, cur, prevB, cur], axis=1)
    c["c_maskC"] = np.concatenate([prevC, cur, prevC, cur], axis=1)
    rho = (np.arange(16)[None, :, None] * 128 + np.arange(128)[:, None, None])
    qi = np.arange(8)[None, None, :]
    mult = (rho >= 1920 + qi).astype(np.float32) + ((rho % 4 == qi % 4) & (rho >= 1536 + qi)).astype(np.float32) \
        + (rho % 16 == qi).astype(np.float32)
    c["c_multB"] = mult.reshape(128, 128).astype(np.float32)
    jj = np.arange(8)[:, None]; ii = np.arange(8)[None, :]
    c["c_multBn"] = ((jj <= ii).astype(np.float32) + 2.0 * (jj == ii) + 1.0 * (jj == ii - 4)).astype(np.float32)
    c["c_maskCs"] = (np.arange(128)[:, None] >= ii + 1).astype(np.float32)
    c["c_maskCn"] = (jj <= ii).astype(np.float32)
    return c


def make_in_map(c, inp, consts):
    b0, b1 = NB * c, NB * (c + 1)
    f = lambda a: np.ascontiguousarray(a, dtype=np.float32)
    sinks = np.asarray(inp["sinks"], np.float32)
    sinkP = np.stack([np.stack([np.concatenate([np.full(64, sinks[l, j]), np.full(64, sinks[l, j + 3])]) for j in range(3)]) for l in range(L)])
    sinkS = np.zeros((L, 24, 2), np.float32)
    for l in range(L):
        for g in range(2):
            for h in range(3):
                sinkS[l, 8 * h:8 * h + 8, g] = sinks[l, 3 * g + h]
    m = {
        "xp": f(inp["x_prompt"][c]), "xs": f(inp["x_sample"][b0:b1]).reshape(128, D),
        "cpe": f(np.broadcast_to(np.asarray(inp["c_prompt"])[c], (128, D))), "cse": f(np.repeat(np.asarray(inp["c_sample"])[b0:b1], DS, axis=0)),
        "sconv": f(inp["state_conv"][:, b0:b1]).reshape(L, 32, 256),
        "cbk": f(inp["cache_b_k"][:, b0:b1]).reshape(L, NB, WB, 384), "cbv": f(inp["cache_b_v"][:, b0:b1]).reshape(L, NB, WB, 384),
        "cck": f(inp["cache_c_k"][:, b0:b1]).reshape(L, NB, 128, 128), "ccv": f(inp["cache_c_v"][:, b0:b1]).reshape(L, NB, 128, 128),
        "w_mod": f(inp["w_mod"]), "b_mod": f(inp["b_mod"]), "norm_mix": f(inp["norm_mix"]), "norm_ffn": f(inp["norm_ffn"]),
        "w_in": f(inp["w_in"]), "conv_w": f(inp["conv_w"]), "sinkP": f(sinkP), "sinkS": sinkS,
        "w_out": f(inp["w_out"]), "w_gu": f(inp["w_gate_up"]), "w_down": f(inp["w_down"]), "norm_final": f(inp["norm_final"]),
    }
    m.update(consts)
    return m


_NC = [None]


def kernel(**inputs):
    inp = {k: np.asarray(v) for k, v in inputs.items()}
    ncores = 8
    consts = _consts()
    in_maps = [make_in_map(c, inp, consts) for c in range(ncores)]
    if _NC[0] is None:
        _NC[0] = build()
    res = run_bass_kernel_spmd(_NC[0], in_maps, core_ids=list(range(ncores)))
    R = res.results
    cat = lambda k, shp: np.stack([np.asarray(R[c][k], np.float32).reshape(shp) for c in range(ncores)])
    y_p = cat("yp", (SEQ, D))
    y_s = cat("ys", (NB, DS, D)).reshape(ncores * NB, DS, D)
    per_b = lambda k, shp: np.stack([np.asarray(R[c][k], np.float32).reshape((L,) + shp) for c in range(ncores)], axis=1)
    per_s = lambda k, shp: np.concatenate([np.asarray(R[c][k], np.float32).reshape((L, NB) + shp) for c in range(ncores)], axis=1)
    return (y_p, y_s,
            per_b("convp", (2, 256)), per_s("convs", (2, 256)),
            per_b("bkp", (WB, 6, 64)), per_s("bks", (WB, 6, 64)),
            per_b("bvp", (WB, 6, 64)), per_s("bvs", (WB, 6, 64)),
            per_b("ckp", (128, 2, 64)), per_s("cks", (128, 2, 64)),
            per_b("cvp", (128, 2, 64)), per_s("cvs", (128, 2, 64)))
```

```python
import numpy as np
import concourse.bass as bass
import concourse.mybir as mybir
from concourse.bass_utils import run_bass_kernel_spmd

F32 = mybir.dt.float32
BF16 = mybir.dt.bfloat16
AF = mybir.ActivationFunctionType
ALU = mybir.AluOpType
AX = mybir.AxisListType


class Buf:
    __slots__ = ("name", "last_w", "readers", "excl")

    def __init__(self, name="", excl=False):
        self.name = name
        self.last_w = None
        self.readers = {}
        self.excl = excl


class Eng:
    def __init__(self, name, eng, sem, self_sync):
        self.name = name
        self.eng = eng
        self.sem = sem
        self.count = 0
        self.waited = {}
        self.self_sync = self_sync


class DSem:
    def __init__(self, sem):
        self.sem = sem
        self.count = 0


class TK:
    def __init__(self, nc, sems):
        self.nc = nc
        self.free_sems = list(sems)
        self.semobj = {}
        mk = lambda n, e, ss: Eng(n, e, self._sem(n), ss)
        self.pe = mk("pe", nc.tensor, False)
        self.act = mk("act", nc.scalar, True)
        self.dve = mk("dve", nc.vector, True)
        self.pool = mk("pool", nc.gpsimd, True)
        self.sp = mk("sp", nc.sync, False)
        self.engs = [self.pe, self.act, self.dve, self.pool, self.sp]
        self.dsems = []

    def _sem(self, name):
        s = self.free_sems.pop()
        self.semobj[id(s)] = s
        return s

    def dsem(self):
        d = DSem(self._sem("d"))
        self.dsems.append(d)
        return d

    def _deps(self, reads, writes):
        deps = {}

        def add(ev):
            if ev is None:
                return
            k, v = ev
            if deps.get(k, 0) < v:
                deps[k] = v
        for b in reads:
            add(b.last_w)
            if b.excl:
                for k, v in b.readers.items():
                    add((k, v))
        for b in writes:
            add(b.last_w)
            for k, v in b.readers.items():
                add((k, v))
        return deps

    def _wait(self, e, deps):
        for k, v in deps.items():
            if k == id(e.sem) and not e.self_sync:
                continue
            if e.waited.get(k, 0) >= v:
                continue
            e.eng.wait_ge(self.semobj[k], v)
            e.waited[k] = v

    def _mark(self, ev, reads, writes):
        k, v = ev
        for b in reads:
            if b.excl:
                b.last_w = ev
                b.readers = {}
            elif b.readers.get(k, 0) < v:
                b.readers[k] = v
        for b in writes:
            b.last_w = ev
            b.readers = {}

    def op(self, e, fn, reads=(), writes=()):
        self._wait(e, self._deps(reads, writes))
        inst = fn()
        e.count += 1
        inst.then_inc(e.sem, 1)
        self._mark((id(e.sem), e.count), reads, writes)

    def dma(self, q, ds, pairs, reads=(), writes=()):
        deps = self._deps(reads, writes)
        if ds.count:
            k = id(ds.sem)
            if deps.get(k, 0) < ds.count:
                deps[k] = ds.count
        self._wait(q, deps)
        for (o, i) in pairs:
            q.eng.dma_start(out=o, in_=i).then_inc(ds.sem, 16)
            ds.count += 16
        self._mark((id(ds.sem), ds.count), reads, writes)

    def barrier(self):
        tot = {}
        for e in self.engs:
            if e.count:
                tot[id(e.sem)] = e.count
        for d in self.dsems:
            if d.count:
                tot[id(d.sem)] = d.count
        for e in self.engs:
            for k, v in tot.items():
                if k == id(e.sem) and not e.self_sync:
                    continue
                if e.waited.get(k, 0) >= v:
                    continue
                e.eng.wait_ge(self.semobj[k], v)
                e.waited[k] = v


D = 1024
SEQ = 4096
NT = SEQ // 128
NB = 16
DS = 8
L = 2
INW = 2560
DFF = 2816
NFC = DFF // 128
EPS = 1e-6
SC = 0.125
WB = 2048
C_GA, C_GC, C_XA, C_QB, C_KB, C_VB, C_QC, C_KC, C_VC = 0, 256, 512, 768, 1152, 1536, 1920, 2304, 2432
DILS = (1, 4, 16)


class SBA:
    def __init__(self, big, words):
        self.big = big
        self.words = words
        self.off = 0

    def f32(self, n):
        assert self.off + n <= self.words, ("SBUF overflow", self.off, n, self.words)
        ap = self.big[:, self.off:self.off + n]
        self.off += n
        return ap

    def bf16(self, n):
        w = (n + 1) // 2
        ap = self.f32(w).bitcast(BF16)
        return ap[:, 0:n]

    def mark(self):
        return self.off

    def release(self, m):
        self.off = m


class Ring:
    def __init__(self, aps, dsems=None, name="r"):
        self.aps = aps
        self.bufs = [Buf(f"{name}{i}") for i in range(len(aps))]
        self.ds = dsems
        self.i = -1

    def next(self):
        self.i = (self.i + 1) % len(self.aps)
        return self.cur()

    def cur(self):
        i = self.i
        return self.aps[i], self.bufs[i], (self.ds[i] if self.ds else None)


def build(nlayers=L, stop_after=None, dbg=False):
    nc = bass.Bass("TRN2", target_bir_lowering=False)

    def din(name, shape):
        return nc.dram_tensor(name, shape, F32, kind="ExternalInput").ap()

    def dout(name, shape):
        return nc.dram_tensor(name, shape, F32, kind="ExternalOutput").ap()

    def dscr(name, shape, dt):
        if dbg:
            return nc.dram_tensor(name, shape, dt, kind="ExternalOutput").ap()
        return nc.dram_tensor(name, shape, dt).ap()

    xp = din("xp", [SEQ, D]); xs = din("xs", [128, D])
    cpe = din("cpe", [128, D]); cse = din("cse", [128, D])
    sconv = din("sconv", [L, 32, 256])
    cbk = din("cbk", [L, NB, WB, 384]); cbv = din("cbv", [L, NB, WB, 384])
    cck = din("cck", [L, NB, 128, 128]); ccv = din("ccv", [L, NB, 128, 128])
    w_mod = din("w_mod", [L, D, 6 * D]); b_mod = din("b_mod", [L, 6 * D])
    norm_mix = din("norm_mix", [L, D]); norm_ffn = din("norm_ffn", [L, D])
    w_in = din("w_in", [L, D, INW]); conv_w = din("conv_w", [L, 3, 256])
    sinkP = din("sinkP", [L, 3, 128]); sinkS = din("sinkS", [L, 24, 2])
    w_out = din("w_out", [L, D, D]); w_gu = din("w_gu", [L, D, 2 * DFF]); w_down = din("w_down", [L, DFF, D])
    norm_final = din("norm_final", [D])
    c_ident = din("c_ident", [128, 128])
    c_maskB = din("c_maskB", [128, 512]); c_maskC = din("c_maskC", [128, 512])
    c_multB = din("c_multB", [128, 128]); c_multBn = din("c_multBn", [8, 8])
    c_maskCs = din("c_maskCs", [128, 8]); c_maskCn = din("c_maskCn", [8, 8])
    yp = dout("yp", [SEQ, D]); ys = dout("ys", [128, D])
    convp = dout("convp", [L, 2, 256]); convs = dout("convs", [L, NB, 2, 256])
    bkp = dout("bkp", [L, WB, 384]); bks = dout("bks", [L, NB, WB, 384])
    bvp = dout("bvp", [L, WB, 384]); bvs = dout("bvs", [L, NB, WB, 384])
    ckp = dout("ckp", [L, 128, 128]); cks = dout("cks", [L, NB, 128, 128])
    cvp = dout("cvp", [L, 128, 128]); cvs = dout("cvs", [L, NB, 128, 128])
    MODP = dscr("MODP", [L, 6, 128, D], F32); MODS = dscr("MODS", [L, 6, 128, D], F32)
    QK = dscr("QK", [10, 128, SEQ], BF16)
    VB = dscr("VB", [SEQ, 384], BF16); VC = dscr("VC", [SEQ, 128], BF16)
    VSB = dscr("VSB", [128, 384], BF16); VSC = dscr("VSC", [128, 128], BF16)
    XA = dscr("XA", [SEQ + 128, D], F32); XB = dscr("XB", [SEQ + 128, D], F32)
    DBd = {}

    outc = [0]

    def DB(*key):
        if key == ("OUT",):
            outc[0] += 1
            key = ("OUT", outc[0])
        if key not in DBd:
            DBd[key] = Buf(str(key))
        return DBd[key]

    SBW = 51 * 1024
    big = nc.sbuf_tensor("big", [128, SBW], F32).__enter__()
    sb = SBA(big, SBW)
    ps = [nc.psum_tensor(f"ps{i}", [128, 512], F32).__enter__() for i in range(8)]
    psb = [p[:].bitcast(BF16) for p in ps]
    pb = [Buf(f"ps{i}", excl=True) for i in range(8)]
    sems = [nc.alloc_semaphore(name=f"ks{i}") for i in range(100)]
    tk = TK(nc, sems)
    PE, ACT, DVE, POOL, SP = tk.pe, tk.act, tk.dve, tk.pool, tk.sp
    dpool = [tk.dsem() for _ in range(74)]
    dpool_sw = [tk.dsem() for _ in range(16)]
    dsi = [0, 0]

    def ds_take(n=1, sw=False):
        pool, ix = (dpool_sw, 1) if sw else (dpool, 0)
        r = pool[dsi[ix]:dsi[ix] + n]
        assert len(r) == n, "out of dma sems"
        dsi[ix] += n
        return r if n > 1 else r[0]

    def ncdma():
        return nc.allow_non_contiguous_dma(reason="small strided loads")

    identb = sb.bf16(128); identf = sb.f32(128)
    maskB = sb.bf16(512); maskC = sb.bf16(512)
    multB = sb.bf16(128); multBn = sb.bf16(8); maskCs = sb.bf16(8); maskCn = sb.bf16(8)
    onesb = sb.bf16(64); neghalf = sb.f32(4); gfin = sb.f32(D)
    bC = Buf("const")
    dc = ds_take(sw=True)
    tk.dma(POOL, dc, [(identb, c_ident), (maskB, c_maskB), (maskC, c_maskC), (multB, c_multB),
                      (multBn[0:8, :], c_multBn), (maskCs, c_maskCs), (maskCn[0:8, :], c_maskCn)], writes=[bC])
    tk.dma(SP, ds_take(), [(identf, c_ident), (gfin, norm_final.partition_broadcast(128))], writes=[bC])
    tk.op(DVE, lambda: nc.vector.memset(onesb, 1.0), writes=[bC])
    tk.op(DVE, lambda: nc.vector.memset(neghalf, -0.5), writes=[bC])
    dcc = ds_take(2)
    for l_ in range(nlayers):
        prs = []
        for b in range(NB):
            prs += [(bks[l_, b, 0:WB - 8, :], cbk[l_, b, 8:WB, :]), (bvs[l_, b, 0:WB - 8, :], cbv[l_, b, 8:WB, :]),
                    (cks[l_, b, 0:120, :], cck[l_, b, 8:128, :]), (cvs[l_, b, 0:120, :], ccv[l_, b, 8:128, :])]
        tk.dma(ACT, dcc[l_], prs, writes=[DB("OUT")])
    ds_base = list(dsi)

    def phase_begin():
        dsi[0], dsi[1] = ds_base
        return sb.mark()

    def phase_end(m):
        sb.release(m)
        tk.barrier()

    def rstd_from_ss(ss, rs, n, bss, brs):
        tk.op(POOL, lambda: nc.gpsimd.tensor_scalar(out=rs[:, 0:n], in0=ss[:, 0:n], scalar1=1.0 / D, scalar2=EPS,
                                                    op0=ALU.mult, op1=ALU.add), reads=[bss], writes=[brs])
        tk.op(POOL, lambda: nc.gpsimd.tensor_tensor(out=rs[:, 0:n], in0=rs[:, 0:n], in1=neghalf[:, 0:n], op=ALU.pow),
              reads=[brs, bC], writes=[brs])

    def transpose8(src_bf, bsrc, bank, bbank):
        def f():
            for k in range(8):
                i = nc.tensor.transpose(out=psb[bank][:, k * 128:(k + 1) * 128], in_=src_bf[:, k * 128:(k + 1) * 128],
                                        identity=identb)
            return i
        tk.op(PE, f, reads=[bsrc, bC], writes=[bbank])

    def mm_acc(out, lhs_fn, rhs_fn, nk):
        def f():
            for k in range(nk):
                i = nc.tensor.matmul(out, lhsT=lhs_fn(k), rhs=rhs_fn(k), start=(k == 0), stop=(k == nk - 1))
            return i
        return f

    st = dict(nc=nc, tk=tk, sb=sb, ps=ps, psb=psb, pb=pb)

    def phase_mod():
        m = phase_begin()
        cp = sb.f32(D); cs = sb.f32(D)
        scb = [sb.bf16(D), sb.bf16(D)]
        scT = sb.bf16(2 * D).rearrange("p (g k t) -> p g k t", g=2, k=8)
        gam = [sb.f32(D), sb.f32(D)]
        wm = Ring([sb.bf16(8 * 512).rearrange("p (k c) -> p k c", k=8) for _ in range(2)], ds_take(2, sw=True), "wm")
        bm = Ring([sb.f32(512) for _ in range(2)], ds_take(2), "bm")
        stg = Ring([sb.f32(512) for _ in range(4)], ds_take(4), "stg")
        bcp, bsc, bscT, bg = Buf(), Buf(), Buf(), Buf()
        tk.dma(SP, ds_take(), [(cp, cpe), (cs, cse)], writes=[bcp])
        tk.op(ACT, lambda: nc.scalar.activation(out=scb[0], in_=cp, func=AF.Silu), reads=[bcp], writes=[bsc])
        tk.op(ACT, lambda: nc.scalar.activation(out=scb[1], in_=cs, func=AF.Silu), reads=[bcp], writes=[bsc])
        for g in range(2):
            transpose8(scb[g], bsc, g, pb[g])
            tk.op(DVE, lambda g=g: nc.vector.tensor_copy(out=scT[:, g], in_=psb[g].rearrange("p (k t) -> p k t", k=8)),
                  reads=[pb[g]], writes=[bscT])
        dg = ds_take()
        for l in range(nlayers):
            tk.dma(SP, dg, [(gam[0], norm_mix[l].partition_broadcast(128)), (gam[1], norm_ffn[l].partition_broadcast(128))],
                   writes=[bg])
            for n in range(12):
                w_ap, w_b, w_d = wm.next()
                b_ap, b_b, b_d = bm.next()
                src = w_mod[l, :, n * 512:(n + 1) * 512].rearrange("(k p) c -> p k c", p=128)
                tk.dma(POOL, w_d, [(w_ap, src)], writes=[w_b])
                tk.dma(SP, b_d, [(b_ap, b_mod[l, n * 512:(n + 1) * 512].partition_broadcast(128))], writes=[b_b])
                j, half = n // 2, n % 2
                for g in range(2):
                    bank = 2 + ((2 * n + g) % 4)
                    tk.op(PE, mm_acc(ps[bank][:, :], lambda k, g=g: scT[:, g, k, :], lambda k, w_ap=w_ap: w_ap[:, k, :], 8),
                          reads=[bscT, w_b], writes=[pb[bank]])
                    s_ap, s_b, s_d = stg.next()
                    tk.op(DVE, lambda s_ap=s_ap, bank=bank, b_ap=b_ap: nc.vector.tensor_tensor(
                        out=s_ap, in0=ps[bank][:, :], in1=b_ap, op=ALU.add), reads=[pb[bank], b_b], writes=[s_b])
                    if j in (1, 4):
                        gg = gam[0 if j == 1 else 1][:, half * 512:(half + 1) * 512]
                        tk.op(DVE, lambda s_ap=s_ap, gg=gg: nc.vector.scalar_tensor_tensor(
                            out=s_ap, in0=s_ap, scalar=1.0, in1=gg, op0=ALU.add, op1=ALU.mult), reads=[s_b, bg], writes=[s_b])
                    dst = (MODP, MODS)[g][l, j, :, half * 512:(half + 1) * 512]
                    tk.dma(SP, s_d, [(dst, s_ap)], reads=[s_b], writes=[DB("MOD", g, l, j, half)])
        phase_end(m)

    def mod_bufs(g, l, j):
        return [DB("MOD", g, l, j, 0), DB("MOD", g, l, j, 1)]

    def make_norm(l, which):
        o = {}
        jg, jsh = (1, 0) if which == 0 else (4, 3)
        o["gs"] = sb.f32(8); o["sh"] = sb.f32(8); o["b"] = Buf()
        with ncdma():
            tk.dma(SP, ds_take(), [(o["gs"], MODP[l, jg, 0, :].rearrange("(k p) -> p k", p=128)),
                                   (o["sh"], MODP[l, jsh, 0, :].rearrange("(k p) -> p k", p=128))],
                   reads=mod_bufs(0, l, jg) + mod_bufs(0, l, jsh), writes=[o["b"]])
        o["junk"] = sb.bf16(D); o["bjunk"] = Buf()
        o["ss"] = sb.f32(4); o["rs"] = sb.f32(4); o["bss"] = Buf(); o["brs"] = Buf()
        o["xsb"] = Ring([sb.bf16(D) for _ in range(2)], None, "xsb")
        o["tb"] = 0
        o["l"], o["which"], o["jg"], o["jsh"] = l, which, jg, jsh
        return o

    def norm_tile_prompt(o, xt, bx, hT_dst, bh, tbanks):
        ss, rs = o["ss"], o["rs"]
        tk.op(ACT, lambda: nc.scalar.activation(out=o["junk"], in_=xt, func=AF.Square, accum_out=ss[:, 0:1]),
              reads=[bx], writes=[o["bjunk"], o["bss"]])
        rstd_from_ss(ss, rs, 1, o["bss"], o["brs"])
        x_ap, x_b, _ = o["xsb"].next()
        tk.op(ACT, lambda: nc.scalar.activation(out=x_ap, in_=xt, func=AF.Copy, scale=rs[:, 0:1]),
              reads=[bx, o["brs"]], writes=[x_b])
        bank = tbanks[o["tb"] % len(tbanks)]; o["tb"] += 1
        transpose8(x_ap, x_b, bank, pb[bank])
        pv = psb[bank].rearrange("p (k t) -> p k t", k=8)
        for k in range(8):
            e = DVE if k % 2 == 0 else POOL
            if e is DVE:
                tk.op(DVE, lambda k=k: nc.vector.tensor_scalar(out=hT_dst[:, k, :], in0=pv[:, k, :], scalar1=o["gs"][:, k:k + 1],
                                                               scalar2=o["sh"][:, k:k + 1], op0=ALU.mult, op1=ALU.add),
                      reads=[pb[bank], o["b"]], writes=[bh])
            else:
                tk.op(ACT, lambda k=k: nc.scalar.activation(out=hT_dst[:, k, :], in_=pv[:, k, :], func=AF.Identity,
                                                            scale=o["gs"][:, k:k + 1], bias=o["sh"][:, k:k + 1]),
                      reads=[pb[bank], o["b"]], writes=[bh])

    def norm_tile_sample(o, xt, bx, hT_dst, bh, tbanks, gsS, shS, bmodS, tmp):
        ss, rs = o["ss"], o["rs"]
        tk.op(ACT, lambda: nc.scalar.activation(out=o["junk"], in_=xt, func=AF.Square, accum_out=ss[:, 0:1]),
              reads=[bx], writes=[o["bjunk"], o["bss"]])
        rstd_from_ss(ss, rs, 1, o["bss"], o["brs"])
        x_ap, x_b, _ = o["xsb"].next()
        tap, tb_ = tmp
        tk.op(DVE, lambda: nc.vector.scalar_tensor_tensor(out=tap, in0=xt, scalar=rs[:, 0:1], in1=gsS, op0=ALU.mult, op1=ALU.mult),
              reads=[bx, o["brs"], bmodS], writes=[tb_])
        tk.op(DVE, lambda: nc.vector.tensor_tensor(out=x_ap, in0=tap, in1=shS, op=ALU.add), reads=[tb_, bmodS], writes=[x_b])
        bank = tbanks[o["tb"] % len(tbanks)]; o["tb"] += 1
        transpose8(x_ap, x_b, bank, pb[bank])
        tk.op(DVE, lambda: nc.vector.tensor_copy(out=hT_dst, in_=psb[bank].rearrange("p (k t) -> p k t", k=8)),
              reads=[pb[bank]], writes=[bh])

    def x_src(l, t):
        if l == 0:
            return (xp[t * 128:(t + 1) * 128, :] if t < NT else xs), []
        return XB[t * 128:(t + 1) * 128, :], [DB("XB", t)]

    def load_w_in(l):
        win = sb.bf16(8 * INW).rearrange("p (k c) -> p k c", k=8)
        bwin = Buf()
        src = w_in[l].rearrange("(k p) c -> p k c", p=128)
        dsw = ds_take(2, sw=True)
        pairs = []
        for c0 in (0, 512, 1024, 1536):
            c1 = min(c0 + 512, C_QC)
            pairs.append((win[:, :, c0:c1], src[:, :, c0:c1]))
        for j in range(3):
            o0 = C_QC + 128 * j
            pairs.append((win[:, :, o0:o0 + 64], src[:, :, C_QC + 64 * j:C_QC + 64 * j + 64]))
            pairs.append((win[:, :, o0 + 64:o0 + 128], src[:, :, C_QC + 64 * (j + 3):C_QC + 64 * (j + 3) + 64]))
        pairs.append((win[:, :, C_KC:INW], src[:, :, C_KC:INW]))
        for pi, pr in enumerate(pairs):
            tk.dma(POOL, dsw[pi % 2], [pr], writes=[bwin])
        cw = sb.f32(6).rearrange("p (c i) -> p c i", c=2)
        with ncdma():
            tk.dma(SP, ds_take(), [(cw[:, c, i:i + 1], conv_w[l, i, c * 128:(c + 1) * 128].rearrange("(p o) -> p o", o=1))
                                   for c in range(2) for i in range(3)], writes=[bwin])
        return win, cw, bwin

    def phase_a1(l, mixT, bmix):
        m = phase_begin()
        win, cw, bwin = load_w_in(l)
        nm = make_norm(l, 0)
        m2 = sb.mark()
        xr = Ring([sb.f32(D) for _ in range(3)], ds_take(3), "x")
        hr = Ring([sb.bf16(8 * 512).rearrange("p (k t) -> p k t", k=8) for _ in range(2)], None, "hT")
        qst = Ring([sb.bf16(512) for _ in range(3)], ds_take(3), "qst")
        vst = Ring([sb.bf16(512) for _ in range(2)], ds_take(2), "vst")
        fst = Ring([sb.f32(384) for _ in range(3)], ds_take(3), "fst")
        gaS = sb.f32(1024).rearrange("p (c t) -> p c t", c=2); gcS = sb.f32(1024).rearrange("p (c t) -> p c t", c=2)
        ub = sb.f32(2 * 514).rearrange("p (c t) -> p c t", c=2)
        cacc = sb.f32(512)
        bga, bgc, bub, bcacc = Buf(), Buf(), Buf(), Buf()
        tk.op(POOL, lambda: nc.gpsimd.memset(ub[:, :, 0:2], 0.0), writes=[bub])
        pbank = [2]

        def nbank():
            b = pbank[0]
            pbank[0] = 2 + (pbank[0] - 2 + 1) % 6
            return b

        import os
        for g in range(int(os.environ.get('NGROUPS', NT // 4))):
            h_ap, h_b, _ = hr.next()
            for tt in range(4):
                t = 4 * g + tt
                x_ap, x_b, x_d = xr.next()
                src, sbufs = x_src(l, t)
                tk.dma(SP, x_d, [(x_ap, src)], reads=sbufs, writes=[x_b])
                norm_tile_prompt(nm, x_ap, x_b, h_ap[:, :, tt * 128:(tt + 1) * 128], h_b, (0, 1))
            N = 512
            pos0 = g * 512

            def fm(col, bank):
                tk.op(PE, mm_acc(ps[bank][:, 0:N], lambda k: win[:, k, col:col + 128], lambda k: h_ap[:, k, 0:N], 8),
                      reads=[bwin, h_b], writes=[pb[bank]])
            PARTS = os.environ.get('A1_PARTS', 'cqt')
            for c in (range(2) if 'c' in PARTS else []):
                bk = nbank(); fm(C_GA + 128 * c, bk)
                tk.op(ACT, lambda c=c, bk=bk: nc.scalar.copy(out=gaS[:, c, :], in_=ps[bk][:, 0:N]), reads=[pb[bk]], writes=[bga])
            for c in (range(2) if 'c' in PARTS else []):
                bk = nbank(); fm(C_GC + 128 * c, bk)
                tk.op(ACT, lambda c=c, bk=bk: nc.scalar.copy(out=gcS[:, c, :], in_=ps[bk][:, 0:N]), reads=[pb[bk]], writes=[bgc])
            for c in (range(2) if 'c' in PARTS else []):
                bk = nbank(); fm(C_XA + 128 * c, bk)
                tk.op(DVE, lambda c=c, bk=bk: nc.vector.tensor_tensor(out=ub[:, c, 2:2 + N], in0=ps[bk][:, 0:N], in1=gcS[:, c, :],
                                                                        op=ALU.mult), reads=[pb[bk], bgc], writes=[bub])
                tk.op(DVE, lambda c=c: nc.vector.tensor_scalar(out=cacc, in0=ub[:, c, 2:2 + N], scalar1=cw[:, c, 2:3], scalar2=None,
                                                               op0=ALU.mult), reads=[bub, bwin], writes=[bcacc])
                tk.op(DVE, lambda c=c: nc.vector.scalar_tensor_tensor(out=cacc, in0=ub[:, c, 1:1 + N], scalar=cw[:, c, 1:2], in1=cacc,
                                                                      op0=ALU.mult, op1=ALU.add), reads=[bub, bwin, bcacc], writes=[bcacc])
                tk.op(DVE, lambda c=c: nc.vector.scalar_tensor_tensor(out=cacc, in0=ub[:, c, 0:N], scalar=cw[:, c, 0:1], in1=cacc,
                                                                      op0=ALU.mult, op1=ALU.add), reads=[bub, bwin, bcacc], writes=[bcacc])
                tk.op(DVE, lambda c=c: nc.vector.tensor_tensor(out=mixT[:, c, pos0:pos0 + N], in0=cacc, in1=gaS[:, c, :], op=ALU.mult),
                      reads=[bcacc, bga], writes=[bmix])
                if g == NT // 4 - 1:
                    with ncdma():
                        tk.dma(SP, ds_take(), [(convp[l, :, c * 128:(c + 1) * 128].rearrange("j p -> p j"), ub[:, c, N:N + 2])],
                               reads=[bub], writes=[DB("OUT")])
                tk.op(POOL, lambda c=c: nc.gpsimd.tensor_copy(out=ub[:, c, 0:2], in_=ub[:, c, N:N + 2]), reads=[bub], writes=[bub])
            for ci, col in enumerate([C_QB, C_QB + 128, C_QB + 256, C_KB, C_KB + 128, C_KB + 256,
                                      C_QC, C_QC + 128, C_QC + 256, C_KC] if 'q' in PARTS else []):
                bk = nbank(); fm(col, bk)
                s_ap, s_b, s_d = qst.next()
                e = ACT if ci % 2 == 0 else DVE
                if e is ACT:
                    tk.op(ACT, lambda s_ap=s_ap, bk=bk: nc.scalar.copy(out=s_ap, in_=ps[bk][:, 0:N]), reads=[pb[bk]], writes=[s_b])
                else:
                    tk.op(DVE, lambda s_ap=s_ap, bk=bk: nc.vector.tensor_copy(out=s_ap, in_=ps[bk][:, 0:N]), reads=[pb[bk]], writes=[s_b])
                tk.dma(SP, s_d, [(QK[ci, :, pos0:pos0 + N], s_ap)], reads=[s_b], writes=[DB("QK", ci, g)])
            for tt in (range(4) if 't' in PARTS else []):
                t = 4 * g + tt
                hs = lambda k, tt=tt: h_ap[:, k, tt * 128:(tt + 1) * 128]
                bk = nbank()
                tk.op(PE, mm_acc(ps[bk][:, 0:384], hs, lambda k: win[:, k, C_VB:C_VB + 384], 8), reads=[bwin, h_b], writes=[pb[bk]])
                s_ap, s_b, s_d = vst.next()
                tk.op(ACT, lambda s_ap=s_ap, bk=bk: nc.scalar.copy(out=s_ap[:, 0:384], in_=ps[bk][:, 0:384]), reads=[pb[bk]], writes=[s_b])
                if t >= NT // 2:
                    f_ap, f_b, f_d = fst.next()
                    tk.op(DVE, lambda f_ap=f_ap, bk=bk: nc.vector.tensor_copy(out=f_ap, in_=ps[bk][:, 0:384]), reads=[pb[bk]], writes=[f_b])
                    tk.dma(SP, f_d, [(bvp[l, (t - 16) * 128:(t - 15) * 128, :], f_ap)], reads=[f_b], writes=[DB("OUT")])
                    bk2 = nbank()
                    tk.op(PE, mm_acc(ps[bk2][:, 0:384], hs, lambda k: win[:, k, C_KB:C_KB + 384], 8), reads=[bwin, h_b], writes=[pb[bk2]])
                    f_ap, f_b, f_d = fst.next()
                    tk.op(DVE, lambda f_ap=f_ap, bk2=bk2: nc.vector.tensor_copy(out=f_ap, in_=ps[bk2][:, 0:384]), reads=[pb[bk2]], writes=[f_b])
                    tk.dma(SP, f_d, [(bkp[l, (t - 16) * 128:(t - 15) * 128, :], f_ap)], reads=[f_b], writes=[DB("OUT")])
                bk3 = nbank()
                tk.op(PE, mm_acc(ps[bk3][:, 0:256], hs, lambda k: win[:, k, C_KC:C_KC + 256], 8), reads=[bwin, h_b], writes=[pb[bk3]])
                tk.op(ACT, lambda s_ap=s_ap, bk3=bk3: nc.scalar.copy(out=s_ap[:, 384:512], in_=ps[bk3][:, 128:256]), reads=[pb[bk3]], writes=[s_b])
                tk.dma(SP, s_d, [(VB[t * 128:(t + 1) * 128, :], s_ap[:, 0:384]), (VC[t * 128:(t + 1) * 128, :], s_ap[:, 384:512])],
                       reads=[s_b], writes=[DB("V", t)])
                if t == NT - 1:
                    f_ap, f_b, f_d = fst.next()
                    tk.op(DVE, lambda f_ap=f_ap, bk3=bk3: nc.vector.tensor_copy(out=f_ap[:, 0:256], in_=ps[bk3][:, 0:256]), reads=[pb[bk3]], writes=[f_b])
                    tk.dma(SP, f_d, [(ckp[l], f_ap[:, 0:128]), (cvp[l], f_ap[:, 128:256])], reads=[f_b], writes=[DB("OUT")])
        sb.release(m2)
        tk.barrier()
        return m, win, cw, bwin, nm

    def phase_a1_sample(l, mixT, bmix, qkS, bqkS, win, cw, bwin, nm):
        gsS = sb.f32(D); shS = sb.f32(D); tmpx = sb.f32(D)
        bmodS, btmp = Buf(), Buf()
        tk.dma(SP, ds_take(), [(gsS, MODS[l, 1]), (shS, MODS[l, 0])], reads=mod_bufs(1, l, 1) + mod_bufs(1, l, 0), writes=[bmodS])
        xt = sb.f32(D); bx = Buf()
        src, sbufs = x_src(l, NT)
        tk.dma(SP, ds_take(), [(xt, src)], reads=sbufs, writes=[bx])
        hT = sb.bf16(8 * 128).rearrange("p (k t) -> p k t", k=8); bh = Buf()
        norm_tile_sample(nm, xt, bx, hT, bh, (0, 1), gsS, shS, bmodS, (tmpx, btmp))
        N = 128
        bki = [2]

        def nbank():
            b = bki[0]
            bki[0] = 2 + (bki[0] - 2 + 1) % 6
            return b

        def fm(col, bank):
            tk.op(PE, mm_acc(ps[bank][:, 0:N], lambda k: win[:, k, col:col + 128], lambda k: hT[:, k, :], 8),
                  reads=[bwin, bh], writes=[pb[bank]])
        gaS = sb.f32(256).rearrange("p (c t) -> p c t", c=2); gcS = sb.f32(256).rearrange("p (c t) -> p c t", c=2)
        ue = sb.f32(2 * 160).rearrange("p (c b j) -> p c b j", c=2, b=NB)
        stt = sb.f32(256); cacc = sb.f32(128); utmp = sb.f32(64).rearrange("p (c q) -> p c q", c=2); urow = sb.f32(256)
        bga, bgc, bue, bst, bcacc, butmp, burow = Buf(), Buf(), Buf(), Buf(), Buf(), Buf(), Buf()
        tk.dma(SP, ds_take(), [(stt[0:32, :], sconv[l])], writes=[bst])
        bk = nbank()

        def tr_state():
            for c in range(2):
                i = nc.tensor.transpose(out=ps[bk][:, c * 32:(c + 1) * 32], in_=stt[0:32, c * 128:(c + 1) * 128], identity=identf[0:32, 0:32])
            return i
        tk.op(PE, tr_state, reads=[bst, bC], writes=[pb[bk]])
        tk.op(DVE, lambda: nc.vector.tensor_copy(out=ue[:, :, :, 0:2], in_=ps[bk][:, 0:64].rearrange("p (c b j) -> p c b j", c=2, b=NB)),
              reads=[pb[bk]], writes=[bue])
        for c in range(2):
            b1 = nbank(); fm(C_GA + 128 * c, b1)
            tk.op(ACT, lambda c=c, b1=b1: nc.scalar.copy(out=gaS[:, c, :], in_=ps[b1][:, 0:N]), reads=[pb[b1]], writes=[bga])
            b2 = nbank(); fm(C_GC + 128 * c, b2)
            tk.op(ACT, lambda c=c, b2=b2: nc.scalar.copy(out=gcS[:, c, :], in_=ps[b2][:, 0:N]), reads=[pb[b2]], writes=[bgc])
            b3 = nbank(); fm(C_XA + 128 * c, b3)
            tk.op(DVE, lambda c=c, b3=b3: nc.vector.tensor_tensor(out=ue[:, c, :, 2:10], in0=ps[b3][:, 0:N].rearrange("p (b i) -> p b i", b=NB),
                                                                    in1=gcS[:, c, :].rearrange("p (b i) -> p b i", b=NB), op=ALU.mult),
                  reads=[pb[b3], bgc, bue], writes=[bue])
            ca3 = cacc.rearrange("p (b i) -> p b i", b=NB)
            tk.op(DVE, lambda c=c: nc.vector.tensor_scalar(out=ca3, in0=ue[:, c, :, 2:10], scalar1=cw[:, c, 2:3], scalar2=None, op0=ALU.mult),
                  reads=[bue, bwin], writes=[bcacc])
            tk.op(DVE, lambda c=c: nc.vector.scalar_tensor_tensor(out=ca3, in0=ue[:, c, :, 1:9], scalar=cw[:, c, 1:2], in1=ca3,
                                                                  op0=ALU.mult, op1=ALU.add), reads=[bue, bwin, bcacc], writes=[bcacc])
            tk.op(DVE, lambda c=c: nc.vector.scalar_tensor_tensor(out=ca3, in0=ue[:, c, :, 0:8], scalar=cw[:, c, 0:1], in1=ca3,
                                                                  op0=ALU.mult, op1=ALU.add), reads=[bue, bwin, bcacc], writes=[bcacc])
            tk.op(DVE, lambda c=c: nc.vector.tensor_tensor(out=mixT[:, c, SEQ:SEQ + N], in0=cacc, in1=gaS[:, c, :], op=ALU.mult),
                  reads=[bcacc, bga], writes=[bmix])
            tk.op(DVE, lambda c=c: nc.vector.tensor_copy(out=utmp[:, c, :].rearrange("p (b j) -> p b j", b=NB), in_=ue[:, c, :, 8:10]),
                  reads=[bue], writes=[butmp])
        bk = nbank()

        def tr_u():
            for c in range(2):
                i = nc.tensor.transpose(out=ps[bk][0:32, c * 128:(c + 1) * 128], in_=utmp[:, c, :], identity=identf)
            return i
        tk.op(PE, tr_u, reads=[butmp, bC], writes=[pb[bk]])
        tk.op(DVE, lambda: nc.vector.tensor_copy(out=urow[0:32, :], in_=ps[bk][0:32, 0:256]), reads=[pb[bk]], writes=[burow])
        tk.dma(SP, ds_take(), [(convs[l].rearrange("b j f -> (b j) f"), urow[0:32, :])], reads=[burow], writes=[DB("OUT")])
        for ci, col in enumerate([C_QB, C_QB + 128, C_QB + 256, C_KB, C_KB + 128, C_KB + 256, C_QC, C_QC + 128, C_QC + 256, C_KC]):
            b1 = nbank(); fm(col, b1)
            tk.op(ACT, lambda ci=ci, b1=b1: nc.scalar.copy(out=qkS[:, ci, :], in_=ps[b1][:, 0:N]), reads=[pb[b1]], writes=[bqkS])
        kvf = sb.f32(768 + 256); kvb = sb.bf16(512); bkvf, bkvb = Buf(), Buf()
        b1 = nbank()
        tk.op(PE, mm_acc(ps[b1][:, 0:384], lambda k: hT[:, k, :], lambda k: win[:, k, C_KB:C_KB + 384], 8), reads=[bwin, bh], writes=[pb[b1]])
        tk.op(ACT, lambda: nc.scalar.copy(out=kvf[:, 0:384], in_=ps[b1][:, 0:384]), reads=[pb[b1]], writes=[bkvf])
        b2 = nbank()
        tk.op(PE, mm_acc(ps[b2][:, 0:384], lambda k: hT[:, k, :], lambda k: win[:, k, C_VB:C_VB + 384], 8), reads=[bwin, bh], writes=[pb[b2]])
        tk.op(ACT, lambda: nc.scalar.copy(out=kvf[:, 384:768], in_=ps[b2][:, 0:384]), reads=[pb[b2]], writes=[bkvf])
        tk.op(DVE, lambda: nc.vector.tensor_copy(out=kvb[:, 0:384], in_=ps[b2][:, 0:384]), reads=[pb[b2]], writes=[bkvb])
        b3 = nbank()
        tk.op(PE, mm_acc(ps[b3][:, 0:256], lambda k: hT[:, k, :], lambda k: win[:, k, C_KC:C_KC + 256], 8), reads=[bwin, bh], writes=[pb[b3]])
        tk.op(ACT, lambda: nc.scalar.copy(out=kvf[:, 768:1024], in_=ps[b3][:, 0:256]), reads=[pb[b3]], writes=[bkvf])
        tk.op(DVE, lambda: nc.vector.tensor_copy(out=kvb[:, 384:512], in_=ps[b3][:, 128:256]), reads=[pb[b3]], writes=[bkvb])
        tk.dma(SP, ds_take(), [(VSB[:, :], kvb[:, 0:384]), (VSC[:, :], kvb[:, 384:512])], reads=[bkvb], writes=[DB("VS", l)])
        prs = []
        for b in range(NB):
            r = slice(8 * b, 8 * b + 8)
            prs += [(bks[l, b, WB - 8:WB, :], kvf[r, 0:384]), (bvs[l, b, WB - 8:WB, :], kvf[r, 384:768]),
                    (cks[l, b, 120:128, :], kvf[r, 768:896]), (cvs[l, b, 120:128, :], kvf[r, 896:1024])]
        tk.dma(SP, ds_take(), prs, reads=[bkvf], writes=[DB("OUT")])

    def phase_a2(l, mixT, bmix):
        m = phase_begin()
        qT = sb.bf16(SEQ); kT = sb.bf16(SEQ); bq, bkk = Buf(), Buf(); dq, dk = ds_take(), ds_take()
        Vd = [sb.bf16(32 * 128).rearrange("p (t f) -> p t f", t=32) for _ in range(3)]
        bV = [Buf() for _ in range(3)]; dV = ds_take(3)
        acc = sb.f32(2 * SEQ).rearrange("p (o t) -> p o t", o=2); bacc = Buf()
        Pst = Ring([sb.bf16(512) for _ in range(4)], None, "P")
        es = sb.f32(1); bes = Buf(); des = ds_take()
        sbanks = ((0, 1), (2, 3)); obanks = (4, 5)
        itc = [0]

        def sl(d, r, blk):
            s0 = d * 128 * blk + r
            return slice(s0, s0 + 127 * d + 1, d)

        def load_V(dst, bdst, dd, src_t, rowlen, coloff, d):
            nblk = 32 // d
            dv = dst.rearrange("p (r b) f -> p r b f", r=d)
            prs = []
            if d == 1:
                for q4 in range(4):
                    ap = bass.AP(src_t.tensor, coloff + q4 * 8 * 128 * rowlen, [[rowlen, 128], [128 * rowlen, 8], [1, 128]])
                    prs.append((dst[:, q4 * 8:(q4 + 1) * 8, :], ap))
            else:
                for r in range(d):
                    ap = bass.AP(src_t.tensor, coloff + r * rowlen, [[d * rowlen, 128], [d * 128 * rowlen, nblk], [1, 128]])
                    prs.append((dv[:, r, :, :], ap))
            with ncdma():
                tk.dma(SP, dd, prs, reads=[DB("V", t) for t in range(NT)], writes=[bdst])

        for kind, j in [("B", 0), ("B", 1), ("B", 2), ("C", 0), ("C", 1), ("C", 2)]:
            isB = kind == "B"
            qc_, kc_ = (j, 3 + j) if isB else (6 + j, 9)
            tk.dma(SP, dq, [(qT, QK[qc_])], reads=[DB("QK", qc_, g) for g in range(8)], writes=[bq])
            if isB or j == 0:
                tk.dma(SP, dk, [(kT, QK[kc_])], reads=[DB("QK", kc_, g) for g in range(8)], writes=[bkk])
            if isB:
                for di, d in enumerate(DILS):
                    load_V(Vd[di], bV[di], dV[di], VB, 384, 128 * j, d)
            elif j == 0:
                load_V(Vd[0], bV[0], dV[0], VC, 128, 0, 1)
            if not isB:
                tk.dma(SP, des, [(es, sinkP[l, j, :].rearrange("(p o) -> p o", o=1))], writes=[bes])
                tk.op(ACT, lambda: nc.scalar.activation(out=es, in_=es, func=AF.Exp), reads=[bes], writes=[bes])
            tk.op(POOL, lambda: nc.gpsimd.memset(acc, 0.0), writes=[bacc])
            mask = maskB if isB else maskC
            def stage_a(di, d, r, b0, it):
                blks = (b0, b0 + 1)
                Pp = []
                for s in (0, 1):
                    bank = sbanks[s][it % 2]

                    def f(s=s, bank=bank):
                        for bi, blk in enumerate(blks):
                            qs = qT[64 * s:64 * s + 64, sl(d, r, blk)]
                            for w, kb in enumerate((blk - 1, blk)):
                                if kb < 0:
                                    continue
                                i = nc.tensor.matmul(ps[bank][:, (bi * 2 + w) * 128:(bi * 2 + w + 1) * 128],
                                                     lhsT=kT[64 * s:64 * s + 64, sl(d, r, kb)], rhs=qs, start=True, stop=True)
                        return i
                    tk.op(PE, f, reads=[bq, bkk], writes=[pb[bank]])
                    p_ap, p_b, _ = Pst.next()
                    tk.op(ACT, lambda p_ap=p_ap, bank=bank: nc.scalar.activation(out=p_ap, in_=ps[bank][:, :], func=AF.Exp, scale=SC),
                          reads=[pb[bank]], writes=[p_b])
                    tk.op(POOL, lambda p_ap=p_ap: nc.gpsimd.tensor_tensor(out=p_ap, in0=p_ap, in1=mask, op=ALU.mult),
                          reads=[p_b, bC], writes=[p_b])
                    Pp.append((p_ap, p_b))
                return Pp

            def stage_b(di, d, r, b0, it, Pp):
                blks = (b0, b0 + 1)
                nblk = 32 // d
                Vt, bVt = Vd[di], bV[di]
                obank = obanks[it % 2]

                def gpv():
                    for bi, blk in enumerate(blks):
                        for s in (0, 1):
                            p_ap = Pp[s][0]
                            ws = [w for w in (0, 1) if blk - 1 + w >= 0]
                            for which in (0, 1):
                                for wi, w in enumerate(ws):
                                    kb = blk - 1 + w
                                    lhsT = Vt[:, r * nblk + kb, 64 * s:64 * s + 64] if which == 0 else onesb[:, 0:64]
                                    i = nc.tensor.matmul(ps[obank][64 * s:64 * s + 64, (bi * 2 + which) * 128:(bi * 2 + which + 1) * 128],
                                                         lhsT=lhsT, rhs=p_ap[:, (bi * 2 + w) * 128:(bi * 2 + w + 1) * 128],
                                                         start=(wi == 0), stop=(wi == len(ws) - 1))
                    return i
                tk.op(PE, gpv, reads=[Pp[0][1], Pp[1][1], bVt, bC], writes=[pb[obank]])
                for bi, blk in enumerate(blks):
                    tk.op(DVE, lambda bi=bi, blk=blk: nc.vector.tensor_tensor(
                        out=acc[:, :, sl(d, r, blk)], in0=ps[obank][:, bi * 256:(bi + 1) * 256].rearrange("p (o t) -> p o t", o=2),
                        in1=acc[:, :, sl(d, r, blk)], op=ALU.add), reads=[pb[obank], bacc], writes=[bacc])

            its = []
            for di, d in enumerate(DILS if isB else (1,)):
                for r in range(d):
                    for b0 in range(0, 32 // d, 2):
                        its.append((di, d, r, b0, itc[0])); itc[0] += 1
            prev = None
            for info in its:
                Pp = stage_a(*info)
                if prev is not None:
                    stage_b(*prev)
                prev = info + (Pp,)
            stage_b(*prev)
            if not isB:
                tk.op(DVE, lambda: nc.vector.tensor_scalar(out=acc[:, 1, :], in0=acc[:, 1, :], scalar1=es[:, 0:1], scalar2=None, op0=ALU.add),
                      reads=[bacc, bes], writes=[bacc])
            tk.op(DVE, lambda: nc.vector.reciprocal(out=acc[:, 1, :], in_=acc[:, 1, :]), reads=[bacc], writes=[bacc])
            ch = (2 + j) if isB else (5 + j)
            tk.op(DVE, lambda ch=ch: nc.vector.tensor_tensor(out=mixT[:, ch, 0:SEQ], in0=acc[:, 0, :], in1=acc[:, 1, :], op=ALU.mult),
                  reads=[bacc], writes=[bmix])
        phase_end(m)

    def phase_a2s(l, mixT, bmix, qkS, bqkS):
        m = phase_begin()
        Kc = Ring([sb.bf16(16 * 384).rearrange("p (t f) -> p t f", t=16) for _ in range(2)], ds_take(2, sw=True), "Kc")
        Vc = Ring([sb.bf16(16 * 384).rearrange("p (t f) -> p t f", t=16) for _ in range(2)], ds_take(2, sw=True), "Vc")
        KT = sb.bf16(3 * WB).rearrange("p (j t) -> p j t", j=3); bKT = Buf()
        Vn = sb.bf16(NB * 384).rearrange("p (b f) -> p b f", b=NB); Vcn = sb.bf16(NB * 128).rearrange("p (b f) -> p b f", b=NB)
        bVn = Buf()
        tk.dma(SP, ds_take(), [(Vn[0:8], VSB.rearrange("(b i) f -> i b f", i=8)), (Vcn[0:8], VSC.rearrange("(b i) f -> i b f", i=8))],
               reads=[DB("VS", l)], writes=[bVn])
        Ps = [sb.bf16(408), sb.bf16(408)]; bPs = [Buf(), Buf()]
        rl = sb.f32(8); brl = Buf()
        ysb = sb.bf16(NB * 384).rearrange("p (b f) -> p b f", b=NB); bys = Buf()
        ycs = sb.bf16(NB * 128).rearrange("p (b f) -> p b f", b=NB); byc = Buf()
        es24 = sb.f32(2); bes = Buf()
        tk.dma(SP, ds_take(), [(es24[0:24, :], sinkS[l])], writes=[bes])
        tk.op(ACT, lambda: nc.scalar.activation(out=es24[0:24, :], in_=es24[0:24, :], func=AF.Exp), reads=[bes], writes=[bes])
        Kcc = Ring([sb.bf16(128) for _ in range(2)], ds_take(2, sw=True), "Kcc"); Vcc = Ring([sb.bf16(128) for _ in range(2)], ds_take(2, sw=True), "Vcc")
        KcT = sb.bf16(128); bKcT = Buf()
        SA, SBk = 0, 1
        k_d0 = ds_take(sw=True)
        for b in range(NB):
            qsl = slice(8 * b, 8 * b + 8)
            k_ap, k_b, k_d = Kc.next(); v_ap, v_b, v_d = Vc.next()
            ksrc = cbk[l, b].rearrange("(t p) f -> p t f", p=128); vsrc = cbv[l, b].rearrange("(t p) f -> p t f", p=128)
            for hh in range(2):
                tk.dma(POOL, k_d0, [(k_ap[:, 8 * hh:8 * hh + 8, :], ksrc[:, 8 * hh:8 * hh + 8, :])], writes=[k_b])
            for hh in range(2):
                tk.dma(POOL, k_d0, [(v_ap[:, 8 * hh:8 * hh + 8, :], vsrc[:, 8 * hh:8 * hh + 8, :])], writes=[v_b])
            for j in range(3):
                for tq in range(2):
                    bank = 2 + (j * 2 + tq) % 2

                    def ftr(j=j, tq=tq, bank=bank):
                        for u in range(8):
                            i = nc.tensor.transpose(out=psb[bank][:, u * 128:(u + 1) * 128], in_=k_ap[:, 8 * tq + u, 128 * j:128 * j + 128], identity=identb)
                        return i
                    tk.op(PE, ftr, reads=[k_b, bC], writes=[pb[bank]])
                    e = DVE if tq == 0 else ACT
                    if e is DVE:
                        tk.op(DVE, lambda j=j, tq=tq, bank=bank: nc.vector.tensor_copy(out=KT[:, j, tq * 1024:(tq + 1) * 1024], in_=psb[bank][:, :]),
                              reads=[pb[bank]], writes=[bKT])
                    else:
                        tk.op(ACT, lambda j=j, tq=tq, bank=bank: nc.scalar.copy(out=KT[:, j, tq * 1024:(tq + 1) * 1024], in_=psb[bank][:, :]),
                              reads=[pb[bank]], writes=[bKT])
            for s in (0, 1):
                bank = (SA, SBk)[s]
                pr = slice(64 * s, 64 * s + 64)

                def fs(s=s, bank=bank, pr=pr):
                    for j in range(3):
                        for t in range(16):
                            c0 = (j * 16 + t) * 8
                            i = nc.tensor.matmul(ps[bank][:, c0:c0 + 8], lhsT=KT[pr, j, t * 128:(t + 1) * 128], rhs=qkS[pr, j, qsl], start=True, stop=True)
                        i = nc.tensor.matmul(ps[bank][0:8, 384 + j * 8:384 + j * 8 + 8], lhsT=qkS[pr, 3 + j, qsl], rhs=qkS[pr, j, qsl], start=True, stop=True)
                    return i
                tk.op(PE, fs, reads=[bKT, bqkS], writes=[pb[bank]])
                tk.op(ACT, lambda s=s, bank=bank: nc.scalar.activation(out=Ps[s][:, 0:384], in_=ps[bank][:, 0:384], func=AF.Exp, scale=SC),
                      reads=[pb[bank]], writes=[bPs[s]])
                tk.op(ACT, lambda s=s, bank=bank: nc.scalar.activation(out=Ps[s][0:8, 384:408], in_=ps[bank][0:8, 384:408], func=AF.Exp, scale=SC),
                      reads=[pb[bank]], writes=[bPs[s]])
                mb = bass.AP(multB.tensor, multB.offset, [list(multB.ap[0]), [0, 3], [1, 128]])
                mbn = bass.AP(multBn.tensor, multBn.offset, [[multBn.ap[0][0], 8], [0, 3], [1, 8]])
                tk.op(DVE, lambda s=s, mb=mb: nc.vector.tensor_tensor(out=Ps[s][:, 0:384].rearrange("p (j c) -> p j c", j=3),
                                                                     in0=Ps[s][:, 0:384].rearrange("p (j c) -> p j c", j=3), in1=mb, op=ALU.mult),
                      reads=[bPs[s], bC], writes=[bPs[s]])
                tk.op(DVE, lambda s=s, mbn=mbn: nc.vector.tensor_tensor(out=Ps[s][0:8, 384:408].rearrange("p (j c) -> p j c", j=3),
                                                                       in0=Ps[s][0:8, 384:408].rearrange("p (j c) -> p j c", j=3), in1=mbn, op=ALU.mult),
                      reads=[bPs[s], bC], writes=[bPs[s]])
            ob = 4 + b % 2

            def fpv():
                for j in range(3):
                    for s in (0, 1):
                        h = 2 * j + s
                        for which in (0, 1):
                            o_ap = ps[ob][0:8, h * 65:h * 65 + 64] if which == 0 else ps[ob][0:8, h * 65 + 64:h * 65 + 65]
                            for t in range(16):
                                c0 = (j * 16 + t) * 8
                                rhs = v_ap[:, t, h * 64:(h + 1) * 64] if which == 0 else onesb[:, 0:1]
                                nc.tensor.matmul(o_ap, lhsT=Ps[s][:, c0:c0 + 8], rhs=rhs, start=(t == 0), stop=False)
                            rhs = Vn[0:8, b, h * 64:(h + 1) * 64] if which == 0 else onesb[0:8, 0:1]
                            i = nc.tensor.matmul(o_ap, lhsT=Ps[s][0:8, 384 + j * 8:384 + j * 8 + 8], rhs=rhs, start=False, stop=True)
                return i
            tk.op(PE, fpv, reads=[bPs[0], bPs[1], v_b, bVn, bC], writes=[pb[ob]])
            ov = ps[ob][0:8, 0:390].rearrange("p (h c) -> p h c", h=6)
            tk.op(DVE, lambda ov=ov: nc.vector.reciprocal(out=rl[0:8, 0:6].rearrange("p (h o) -> p h o", o=1), in_=ov[:, :, 64:65]),
                  reads=[pb[ob]], writes=[brl])
            rlb = bass.AP(rl.tensor, rl.offset, [[rl.ap[0][0], 8], [1, 6], [0, 64]])
            tk.op(DVE, lambda ov=ov, rlb=rlb, b=b: nc.vector.tensor_tensor(out=ysb[0:8, b, :].rearrange("p (h c) -> p h c", h=6), in0=ov[:, :, 0:64],
                                                                         in1=rlb, op=ALU.mult), reads=[pb[ob], brl], writes=[bys])
            kc_ap, kc_b, kc_d = Kcc.next(); vc_ap, vc_b, vc_d = Vcc.next()
            tk.dma(POOL, k_d0, [(kc_ap, cck[l, b])], writes=[kc_b])
            tk.dma(POOL, k_d0, [(vc_ap, ccv[l, b])], writes=[vc_b])
            tk.op(PE, lambda: nc.tensor.transpose(out=psb[2][:, 0:128], in_=kc_ap, identity=identb), reads=[kc_b, bC], writes=[pb[2]])
            tk.op(ACT, lambda: nc.scalar.copy(out=KcT, in_=psb[2][:, 0:128]), reads=[pb[2]], writes=[bKcT])
            for g in (0, 1):
                bank = (SA, SBk)[g]
                pr = slice(64 * g, 64 * g + 64)

                def fsc(g=g, bank=bank, pr=pr):
                    nc.tensor.matmul(ps[bank][:, 0:24], lhsT=KcT[pr, :], rhs=qkS[pr, 6:9, qsl], start=True, stop=True)
                    return nc.tensor.matmul(ps[bank][0:8, 24:48], lhsT=qkS[pr, 9, qsl], rhs=qkS[pr, 6:9, qsl], start=True, stop=True)
                tk.op(PE, fsc, reads=[bKcT, bqkS], writes=[pb[bank]])
                tk.op(ACT, lambda g=g, bank=bank: nc.scalar.activation(out=Ps[g][:, 0:24], in_=ps[bank][:, 0:24], func=AF.Exp, scale=SC),
                      reads=[pb[bank]], writes=[bPs[g]])
                tk.op(ACT, lambda g=g, bank=bank: nc.scalar.activation(out=Ps[g][0:8, 24:48], in_=ps[bank][0:8, 24:48], func=AF.Exp, scale=SC),
                      reads=[pb[bank]], writes=[bPs[g]])
                mc = bass.AP(maskCs.tensor, maskCs.offset, [list(maskCs.ap[0]), [0, 3], [1, 8]])
                mcn = bass.AP(maskCn.tensor, maskCn.offset, [[maskCn.ap[0][0], 8], [0, 3], [1, 8]])
                tk.op(DVE, lambda g=g, mc=mc: nc.vector.tensor_tensor(out=Ps[g][:, 0:24].rearrange("p (j c) -> p j c", j=3),
                                                                     in0=Ps[g][:, 0:24].rearrange("p (j c) -> p j c", j=3), in1=mc, op=ALU.mult),
                      reads=[bPs[g], bC], writes=[bPs[g]])
                tk.op(DVE, lambda g=g, mcn=mcn: nc.vector.tensor_tensor(out=Ps[g][0:8, 24:48].rearrange("p (j c) -> p j c", j=3),
                                                                       in0=Ps[g][0:8, 24:48].rearrange("p (j c) -> p j c", j=3), in1=mcn, op=ALU.mult),
                      reads=[bPs[g], bC], writes=[bPs[g]])
            oc = 6 + b % 2

            def fpc():
                for g in (0, 1):
                    for which in (0, 1):
                        o_ap = ps[oc][0:24, g * 65:g * 65 + 64] if which == 0 else ps[oc][0:24, g * 65 + 64:g * 65 + 65]
                        nc.tensor.matmul(o_ap, lhsT=Ps[g][:, 0:24], rhs=(vc_ap[:, g * 64:(g + 1) * 64] if which == 0 else onesb[:, 0:1]), start=True, stop=False)
                        i = nc.tensor.matmul(o_ap, lhsT=Ps[g][0:8, 24:48], rhs=(Vcn[0:8, b, g * 64:(g + 1) * 64] if which == 0 else onesb[0:8, 0:1]),
                                             start=False, stop=True)
                return i
            tk.op(PE, fpc, reads=[bPs[0], bPs[1], vc_b, bVn, bC], writes=[pb[oc]])
            ocv = ps[oc][0:24, 0:130].rearrange("p (g c) -> p g c", g=2)
            tk.op(DVE, lambda ocv=ocv: nc.vector.tensor_tensor(out=rl[0:24, 6:8].rearrange("p (g o) -> p g o", o=1), in0=ocv[:, :, 64:65],
                                                               in1=es24[0:24, :].rearrange("p (g o) -> p g o", o=1), op=ALU.add),
                  reads=[pb[oc], bes, brl], writes=[brl])
            tk.op(DVE, lambda: nc.vector.reciprocal(out=rl[0:24, 6:8], in_=rl[0:24, 6:8]), reads=[brl], writes=[brl])
            rcb = bass.AP(rl.tensor, rl.offset + 6, [[rl.ap[0][0], 24], [1, 2], [0, 64]])
            tk.op(DVE, lambda ocv=ocv, rcb=rcb, b=b: nc.vector.tensor_tensor(out=ycs[0:24, b, :].rearrange("p (g c) -> p g c", g=2), in0=ocv[:, :, 0:64],
                                                                           in1=rcb, op=ALU.mult), reads=[pb[oc], brl], writes=[byc])
        def ftb():
            for b in range(NB):
                for j in range(3):
                    c0 = (j * NB + b) * 8
                    i = nc.tensor.transpose(out=psb[2][:, c0:c0 + 8], in_=ysb[0:8, b, 128 * j:128 * j + 128], identity=identb[0:8, 0:8])
            return i
        tk.op(PE, ftb, reads=[bys, bC], writes=[pb[2]])
        tk.op(DVE, lambda: nc.vector.tensor_copy(out=mixT[:, 2:5, SEQ:SEQ + 128], in_=psb[2][:, 0:384].rearrange("p (j t) -> p j t", j=3)),
              reads=[pb[2]], writes=[bmix])

        def ftc():
            for b in range(NB):
                i = nc.tensor.transpose(out=psb[3][:, b * 24:(b + 1) * 24], in_=ycs[0:24, b, :], identity=identb[0:24, 0:24])
            return i
        tk.op(PE, ftc, reads=[byc, bC], writes=[pb[3]])
        tk.op(DVE, lambda: nc.vector.tensor_copy(out=mixT[:, 5:8, SEQ:SEQ + 128].rearrange("p h (b i) -> p h b i", b=NB),
                                                 in_=psb[3][:, 0:384].rearrange("p (b h i) -> p h b i", b=NB, h=3)),
              reads=[pb[3]], writes=[bmix])
        phase_end(m)

    def phase_a3(l, mixT, bmix):
        m = phase_begin()
        wo = sb.bf16(8 * D).rearrange("p (k c) -> p k c", k=8); bwo = Buf()
        prs = []
        for k in range(5):
            for h2 in range(2):
                prs.append((wo[:, k, h2 * 512:(h2 + 1) * 512], w_out[l, k * 128:(k + 1) * 128, h2 * 512:(h2 + 1) * 512]))
        for j in range(3):
            for q, hh in enumerate((j, j + 3)):
                r0 = 640 + 64 * hh
                for h2 in range(2):
                    prs.append((wo[64 * q:64 * q + 64, 5 + j, h2 * 512:(h2 + 1) * 512], w_out[l, r0:r0 + 64, h2 * 512:(h2 + 1) * 512]))
        dsw = ds_take(2, sw=True)
        for pi, pr in enumerate(prs):
            tk.dma(POOL, dsw[pi % 2], [pr], writes=[bwo])
        gp = sb.f32(D); gsm = sb.f32(D); bg = Buf()
        tk.dma(SP, ds_take(), [(gp, MODP[l, 2]), (gsm, MODS[l, 2])], reads=mod_bufs(0, l, 2) + mod_bufs(1, l, 2), writes=[bg])
        xr = Ring([sb.f32(D) for _ in range(3)], ds_take(3), "x")
        orr = Ring([sb.f32(D) for _ in range(2)], ds_take(2), "o")
        for t in range(NT + 1):
            x_ap, x_b, x_d = xr.next()
            src, sbufs = x_src(l, t)
            tk.dma(SP, x_d, [(x_ap, src)], reads=sbufs, writes=[x_b])
            o_ap, o_b, o_d = orr.next()
            gate = gp if t < NT else gsm
            for h2 in range(2):
                bank = (t % 2) * 2 + h2
                tk.op(PE, mm_acc(ps[bank][:, :], lambda k: mixT[:, k, t * 128:(t + 1) * 128], lambda k: wo[:, k, h2 * 512:(h2 + 1) * 512], 8),
                      reads=[bmix, bwo], writes=[pb[bank]])
                hs = slice(h2 * 512, (h2 + 1) * 512)
                tk.op(DVE, lambda bank=bank, hs=hs: nc.vector.tensor_tensor(out=o_ap[:, hs], in0=ps[bank][:, :], in1=gate[:, hs], op=ALU.mult),
                      reads=[pb[bank], bg], writes=[o_b])
                tk.op(POOL, lambda hs=hs: nc.gpsimd.tensor_tensor(out=o_ap[:, hs], in0=o_ap[:, hs], in1=x_ap[:, hs], op=ALU.add),
                      reads=[o_b, x_b], writes=[o_b])
            tk.dma(SP, o_d, [(XA[t * 128:(t + 1) * 128, :], o_ap)], reads=[o_b], writes=[DB("XA", t)])
        phase_end(m)

    def phase_b(l):
        m = phase_begin()
        wg = sb.bf16(8 * 2 * DFF).rearrange("p (k c) -> p k c", k=8); wd = sb.bf16(NFC * D).rearrange("p (k c) -> p k c", k=NFC)
        bwg, bwd = Buf(), Buf()
        srcg = w_gu[l].rearrange("(k p) c -> p k c", p=128)
        dsw = ds_take(2, sw=True)
        for ci_, c0 in enumerate(range(0, 2 * DFF, 512)):
            tk.dma(POOL, dsw[ci_ % 2], [(wg[:, :, c0:c0 + 512], srcg[:, :, c0:c0 + 512])], writes=[bwg])
        srcd = w_down[l].rearrange("(k p) c -> p k c", p=128)
        dsw2 = ds_take(sw=True)
        for k0 in range(0, NFC, 4):
            k1 = min(k0 + 4, NFC)
            for h2 in range(2):
                tk.dma(POOL, dsw2, [(wd[:, k0:k1, h2 * 512:(h2 + 1) * 512], srcd[:, k0:k1, h2 * 512:(h2 + 1) * 512])], writes=[bwd])
        nm = make_norm(l, 1)
        gp = sb.f32(D); bg = Buf()
        tk.dma(SP, ds_take(), [(gp, MODP[l, 5])], reads=mod_bufs(0, l, 5), writes=[bg])
        GT = 2
        xr = Ring([sb.f32(D) for _ in range(GT + 1)], ds_take(GT + 1), "x")
        hTr = Ring([sb.bf16(8 * 128 * GT).rearrange("p (k t) -> p k t", k=8) for _ in range(2)], None, "hT2")
        aT = sb.bf16(NFC * 128 * GT).rearrange("p (k t) -> p k t", k=NFC); baT = Buf()
        sg = Ring([sb.f32(128 * GT) for _ in range(2)], None, "sg")
        orr = Ring([sb.f32(D) for _ in range(2)], ds_take(2), "o")
        ss2 = sb.f32(1); rs2 = sb.f32(1); bss2, brs2 = Buf(), Buf()
        dst = XB if l < nlayers - 1 else None
        ngrp = NT // GT + 1
        for g in range(ngrp):
            tiles = [GT * g + i for i in range(GT)] if g < NT // GT else [NT]
            N = 128 * len(tiles)
            xts = []
            hT, bh, _ = hTr.next()
            if g == NT // GT:
                mk2 = sb.mark()
                gsS = sb.f32(D); shS = sb.f32(D); gtS = sb.f32(D); bmS = Buf()
                tk.dma(SP, ds_take(), [(gsS, MODS[l, 4]), (shS, MODS[l, 3]), (gtS, MODS[l, 5])],
                       reads=mod_bufs(1, l, 4) + mod_bufs(1, l, 3) + mod_bufs(1, l, 5), writes=[bmS])
            for i, t in enumerate(tiles):
                x_ap, x_b, x_d = xr.next()
                tk.dma(SP, x_d, [(x_ap, XA[t * 128:(t + 1) * 128, :])], reads=[DB("XA", t)], writes=[x_b])
                xts.append((x_ap, x_b))
                if t < NT:
                    norm_tile_prompt(nm, x_ap, x_b, hT[:, :, i * 128:(i + 1) * 128], bh, (6, 7))
                else:
                    o_ap, o_b, _ = orr.next()
                    norm_tile_sample(nm, x_ap, x_b, hT[:, :, 0:128], bh, (6, 7), gsS, shS, bmS, (o_ap, o_b))
            for c in range(NFC):
                bg_, bu_ = (c % 2) * 2, (c % 2) * 2 + 1
                tk.op(PE, mm_acc(ps[bg_][:, 0:N], lambda k: wg[:, k, c * 128:(c + 1) * 128], lambda k: hT[:, k, 0:N], 8),
                      reads=[bwg, bh], writes=[pb[bg_]])
                tk.op(PE, mm_acc(ps[bu_][:, 0:N], lambda k: wg[:, k, DFF + c * 128:DFF + (c + 1) * 128], lambda k: hT[:, k, 0:N], 8),
                      reads=[bwg, bh], writes=[pb[bu_]])
                s_ap, s_b, _ = sg.next()
                tk.op(ACT, lambda s_ap=s_ap, bg_=bg_: nc.scalar.activation(out=s_ap[:, 0:N], in_=ps[bg_][:, 0:N], func=AF.Silu),
                      reads=[pb[bg_]], writes=[s_b])
                tk.op(DVE, lambda s_ap=s_ap, bu_=bu_, c=c: nc.vector.tensor_tensor(out=aT[:, c, 0:N], in0=ps[bu_][:, 0:N], in1=s_ap[:, 0:N], op=ALU.mult),
                      reads=[pb[bu_], s_b], writes=[baT])
            for i, t in enumerate(tiles):
                x_ap, x_b = xts[i]
                o_ap, o_b, o_d = orr.next()
                gate = gp if t < NT else gtS
                gb = [bg] if t < NT else [bmS]
                for h2 in range(2):
                    bank = 4 + h2
                    tk.op(PE, mm_acc(ps[bank][:, :], lambda k: aT[:, k, i * 128:(i + 1) * 128], lambda k: wd[:, k, h2 * 512:(h2 + 1) * 512], NFC),
                          reads=[baT, bwd], writes=[pb[bank]])
                    hs = slice(h2 * 512, (h2 + 1) * 512)
                    tk.op(DVE, lambda bank=bank, hs=hs: nc.vector.tensor_tensor(out=o_ap[:, hs], in0=ps[bank][:, :], in1=gate[:, hs], op=ALU.mult),
                          reads=[pb[bank]] + gb, writes=[o_b])
                    tk.op(POOL, lambda hs=hs: nc.gpsimd.tensor_tensor(out=o_ap[:, hs], in0=o_ap[:, hs], in1=x_ap[:, hs], op=ALU.add),
                          reads=[o_b, x_b], writes=[o_b])
                if dst is not None:
                    tk.dma(SP, o_d, [(dst[t * 128:(t + 1) * 128, :], o_ap)], reads=[o_b], writes=[DB("XB", t)])
                else:
                    tk.op(ACT, lambda: nc.scalar.activation(out=nm["junk"], in_=o_ap, func=AF.Square, accum_out=ss2[:, 0:1]),
                          reads=[o_b], writes=[nm["bjunk"], bss2])
                    rstd_from_ss(ss2, rs2, 1, bss2, brs2)
                    tk.op(DVE, lambda: nc.vector.scalar_tensor_tensor(out=o_ap, in0=o_ap, scalar=rs2[:, 0:1], in1=gfin, op0=ALU.mult, op1=ALU.mult),
                          reads=[o_b, brs2, bC], writes=[o_b])
                    od = yp[t * 128:(t + 1) * 128, :] if t < NT else ys
                    tk.dma(SP, o_d, [(od, o_ap)], reads=[o_b], writes=[DB("OUT")])
        phase_end(m)

    phase_mod()
    if stop_after == "mod":
        tk.barrier()
        return nc
    for l in range(nlayers):
        lm = phase_begin()
        mixT = sb.bf16(8 * (SEQ + 128)).rearrange("p (k t) -> p k t", k=8); bmix = Buf()
        qkS = sb.bf16(10 * 128).rearrange("p (k t) -> p k t", k=10); bqkS = Buf()
        ds_keep = dsi[0]
        m1, win, cw, bwin, nm = phase_a1(l, mixT, bmix)
        import os
        if not os.environ.get("SKIP_A1S"):
            phase_a1_sample(l, mixT, bmix, qkS, bqkS, win, cw, bwin, nm)
        phase_end(m1)
        if stop_after == ("a1", l):
            tk.barrier()
            return nc
        phase_a2(l, mixT, bmix)
        phase_a2s(l, mixT, bmix, qkS, bqkS)
        if stop_after == ("a2", l):
            tk.barrier()
            return nc
        phase_a3(l, mixT, bmix)
        sb.release(lm)
        tk.barrier()
        phase_b(l)
    tk.barrier()
    return nc


def _consts():
    c = {}
    c["c_ident"] = np.eye(128, dtype=np.float32)
    j = np.arange(128)[:, None]; i = np.arange(128)[None, :]
    prevB = (j >= i).astype(np.float32); prevC = (j > i).astype(np.float32); cur = (j <= i).astype(np.float32)
    c["c_maskB"] = np.concatenate([prevB, cur, prevB, cur], axis=1)
    c["c_maskC"] = np.concatenate([prevC, cur, prevC, cur], axis=1)
    rho = (np.arange(16)[None, :, None] * 128 + np.arange(128)[:, None, None])
    qi = np.arange(8)[None, None, :]
    mult = (rho >= 1920 + qi).astype(np.float32) + ((rho % 4 == qi % 4) & (rho >= 1536 + qi)).astype(np.float32) \
        + (rho % 16 == qi).astype(np.float32)
    c["c_multB"] = mult.reshape(128, 128).astype(np.float32)
    jj = np.arange(8)[:, None]; ii = np.arange(8)[None, :]
    c["c_multBn"] = ((jj <= ii).astype(np.float32) + 2.0 * (jj == ii) + 1.0 * (jj == ii - 4)).astype(np.float32)
    c["c_maskCs"] = (np.arange(128)[:, None] >= ii + 1).astype(np.float32)
    c["c_maskCn"] = (jj <= ii).astype(np.float32)
    return c


def make_in_map(c, inp, consts):
    b0, b1 = NB * c, NB * (c + 1)
    f = lambda a: np.ascontiguousarray(a, dtype=np.float32)
    sinks = np.asarray(inp["sinks"], np.float32)
    sinkP = np.stack([np.stack([np.concatenate([np.full(64, sinks[l, j]), np.full(64, sinks[l, j + 3])]) for j in range(3)]) for l in range(L)])
    sinkS = np.zeros((L, 24, 2), np.float32)
    for l in range(L):
        for g in range(2):
            for h in range(3):
                sinkS[l, 8 * h:8 * h + 8, g] = sinks[l, 3 * g + h]
    m = {
        "xp": f(inp["x_prompt"][c]), "xs": f(inp["x_sample"][b0:b1]).reshape(128, D),
        "cpe": f(np.broadcast_to(np.asarray(inp["c_prompt"])[c], (128, D))), "cse": f(np.repeat(np.asarray(inp["c_sample"])[b0:b1], DS, axis=0)),
        "sconv": f(inp["state_conv"][:, b0:b1]).reshape(L, 32, 256),
        "cbk": f(inp["cache_b_k"][:, b0:b1]).reshape(L, NB, WB, 384), "cbv": f(inp["cache_b_v"][:, b0:b1]).reshape(L, NB, WB, 384),
        "cck": f(inp["cache_c_k"][:, b0:b1]).reshape(L, NB, 128, 128), "ccv": f(inp["cache_c_v"][:, b0:b1]).reshape(L, NB, 128, 128),
        "w_mod": f(inp["w_mod"]), "b_mod": f(inp["b_mod"]), "norm_mix": f(inp["norm_mix"]), "norm_ffn": f(inp["norm_ffn"]),
        "w_in": f(inp["w_in"]), "conv_w": f(inp["conv_w"]), "sinkP": f(sinkP), "sinkS": sinkS,
        "w_out": f(inp["w_out"]), "w_gu": f(inp["w_gate_up"]), "w_down": f(inp["w_down"]), "norm_final": f(inp["norm_final"]),
    }
    m.update(consts)
    return m


_NC = [None]


def kernel(**inputs):
    inp = {k: np.asarray(v) for k, v in inputs.items()}
    ncores = 8
    consts = _consts()
    in_maps = [make_in_map(c, inp, consts) for c in range(ncores)]
    if _NC[0] is None:
        _NC[0] = build()
    res = run_bass_kernel_spmd(_NC[0], in_maps, core_ids=list(range(ncores)))
    R = res.results
    cat = lambda k, shp: np.stack([np.asarray(R[c][k], np.float32).reshape(shp) for c in range(ncores)])
    y_p = cat("yp", (SEQ, D))
    y_s = cat("ys", (NB, DS, D)).reshape(ncores * NB, DS, D)
    per_b = lambda k, shp: np.stack([np.asarray(R[c][k], np.float32).reshape((L,) + shp) for c in range(ncores)], axis=1)
    per_s = lambda k, shp: np.concatenate([np.asarray(R[c][k], np.float32).reshape((L, NB) + shp) for c in range(ncores)], axis=1)
    return (y_p, y_s,
            per_b("convp", (2, 256)), per_s("convs", (2, 256)),
            per_b("bkp", (WB, 6, 64)), per_s("bks", (WB, 6, 64)),
            per_b("bvp", (WB, 6, 64)), per_s("bvs", (WB, 6, 64)),
            per_b("ckp", (128, 2, 64)), per_s("cks", (128, 2, 64)),
            per_b("cvp", (128, 2, 64)), per_s("cvs", (128, 2, 64)))
```

```python
import numpy as np
import concourse.bass as bass
import concourse.mybir as mybir
from concourse.bass_utils import run_bass_kernel_spmd

F32 = mybir.dt.float32
BF16 = mybir.dt.bfloat16
AF = mybir.ActivationFunctionType
ALU = mybir.AluOpType
AX = mybir.AxisListType


class Buf:
    __slots__ = ("name", "last_w", "readers", "excl")

    def __init__(self, name="", excl=False):
        self.name = name
        self.last_w = None
        self.readers = {}
        self.excl = excl


class Eng:
    def __init__(self, name, eng, sem, self_sync):
        self.name = name
        self.eng = eng
        self.sem = sem
        self.count = 0
        self.waited = {}
        self.self_sync = self_sync


class DSem:
    def __init__(self, sem):
        self.sem = sem
        self.count = 0


class TK:
    def __init__(self, nc, sems):
        self.nc = nc
        self.free_sems = list(sems)
        self.semobj = {}
        mk = lambda n, e, ss: Eng(n, e, self._sem(n), ss)
        self.pe = mk("pe", nc.tensor, False)
        self.act = mk("act", nc.scalar, True)
        self.dve = mk("dve", nc.vector, True)
        self.pool = mk("pool", nc.gpsimd, True)
        self.sp = mk("sp", nc.sync, False)
        self.engs = [self.pe, self.act, self.dve, self.pool, self.sp]
        self.dsems = []

    def _sem(self, name):
        s = self.free_sems.pop()
        self.semobj[id(s)] = s
        return s

    def dsem(self):
        d = DSem(self._sem("d"))
        self.dsems.append(d)
        return d

    def _deps(self, reads, writes):
        deps = {}

        def add(ev):
            if ev is None:
                return
            k, v = ev
            if deps.get(k, 0) < v:
                deps[k] = v
        for b in reads:
            add(b.last_w)
            if b.excl:
                for k, v in b.readers.items():
                    add((k, v))
        for b in writes:
            add(b.last_w)
            for k, v in b.readers.items():
                add((k, v))
        return deps

    def _wait(self, e, deps):
        for k, v in deps.items():
            if k == id(e.sem) and not e.self_sync:
                continue
            if e.waited.get(k, 0) >= v:
                continue
            e.eng.wait_ge(self.semobj[k], v)
            e.waited[k] = v

    def _mark(self, ev, reads, writes):
        k, v = ev
        for b in reads:
            if b.excl:
                b.last_w = ev
                b.readers = {}
            elif b.readers.get(k, 0) < v:
                b.readers[k] = v
        for b in writes:
            b.last_w = ev
            b.readers = {}

    def op(self, e, fn, reads=(), writes=()):
        self._wait(e, self._deps(reads, writes))
        inst = fn()
        e.count += 1
        inst.then_inc(e.sem, 1)
        self._mark((id(e.sem), e.count), reads, writes)

    def dma(self, q, ds, pairs, reads=(), writes=()):
        deps = self._deps(reads, writes)
        if ds.count:
            k = id(ds.sem)
            if deps.get(k, 0) < ds.count:
                deps[k] = ds.count
        self._wait(q, deps)
        for (o, i) in pairs:
            q.eng.dma_start(out=o, in_=i).then_inc(ds.sem, 16)
            ds.count += 16
        self._mark((id(ds.sem), ds.count), reads, writes)

    def barrier(self):
        tot = {}
        for e in self.engs:
            if e.count:
                tot[id(e.sem)] = e.count
        for d in self.dsems:
            if d.count:
                tot[id(d.sem)] = d.count
        for e in self.engs:
            for k, v in tot.items():
                if k == id(e.sem) and not e.self_sync:
                    continue
                if e.waited.get(k, 0) >= v:
                    continue
                e.eng.wait_ge(self.semobj[k], v)
                e.waited[k] = v


D = 1024
SEQ = 4096
NT = SEQ // 128
NB = 16
DS = 8
L = 2
INW = 2560
DFF = 2816
NFC = DFF // 128
EPS = 1e-6
SC = 0.125
WB = 2048
C_GA, C_GC, C_XA, C_QB, C_KB, C_VB, C_QC, C_KC, C_VC = 0, 256, 512, 768, 1152, 1536, 1920, 2304, 2432
DILS = (1, 4, 16)


class SBA:
    def __init__(self, big, words):
        self.big = big
        self.words = words
        self.off = 0

    def f32(self, n):
        assert self.off + n <= self.words, ("SBUF overflow", self.off, n, self.words)
        ap = self.big[:, self.off:self.off + n]
        self.off += n
        return ap

    def bf16(self, n):
        w = (n + 1) // 2
        ap = self.f32(w).bitcast(BF16)
        return ap[:, 0:n]

    def mark(self):
        return self.off

    def release(self, m):
        self.off = m


class Ring:
    def __init__(self, aps, dsems=None, name="r"):
        self.aps = aps
        self.bufs = [Buf(f"{name}{i}") for i in range(len(aps))]
        self.ds = dsems
        self.i = -1

    def next(self):
        self.i = (self.i + 1) % len(self.aps)
        return self.cur()

    def cur(self):
        i = self.i
        return self.aps[i], self.bufs[i], (self.ds[i] if self.ds else None)


def build(nlayers=L, stop_after=None, dbg=False):
    nc = bass.Bass("TRN2", target_bir_lowering=False)

    def din(name, shape):
        return nc.dram_tensor(name, shape, F32, kind="ExternalInput").ap()

    def dout(name, shape):
        return nc.dram_tensor(name, shape, F32, kind="ExternalOutput").ap()

    def dscr(name, shape, dt):
        if dbg:
            return nc.dram_tensor(name, shape, dt, kind="ExternalOutput").ap()
        return nc.dram_tensor(name, shape, dt).ap()

    xp = din("xp", [SEQ, D]); xs = din("xs", [128, D])
    cpe = din("cpe", [128, D]); cse = din("cse", [128, D])
    sconv = din("sconv", [L, 32, 256])
    cbk = din("cbk", [L, NB, WB, 384]); cbv = din("cbv", [L, NB, WB, 384])
    cck = din("cck", [L, NB, 128, 128]); ccv = din("ccv", [L, NB, 128, 128])
    w_mod = din("w_mod", [L, D, 6 * D]); b_mod = din("b_mod", [L, 6 * D])
    norm_mix = din("norm_mix", [L, D]); norm_ffn = din("norm_ffn", [L, D])
    w_in = din("w_in", [L, D, INW]); conv_w = din("conv_w", [L, 3, 256])
    sinkP = din("sinkP", [L, 3, 128]); sinkS = din("sinkS", [L, 24, 2])
    w_out = din("w_out", [L, D, D]); w_gu = din("w_gu", [L, D, 2 * DFF]); w_down = din("w_down", [L, DFF, D])
    norm_final = din("norm_final", [D])
    c_ident = din("c_ident", [128, 128])
    c_maskB = din("c_maskB", [128, 512]); c_maskC = din("c_maskC", [128, 512])
    c_multB = din("c_multB", [128, 128]); c_multBn = din("c_multBn", [8, 8])
    c_maskCs = din("c_maskCs", [128, 8]); c_maskCn = din("c_maskCn", [8, 8])
    yp = dout("yp", [SEQ, D]); ys = dout("ys", [128, D])
    convp = dout("convp", [L, 2, 256]); convs = dout("convs", [L, NB, 2, 256])
    bkp = dout("bkp", [L, WB, 384]); bks = dout("bks", [L, NB, WB, 384])
    bvp = dout("bvp", [L, WB, 384]); bvs = dout("bvs", [L, NB, WB, 384])
    ckp = dout("ckp", [L, 128, 128]); cks = dout("cks", [L, NB, 128, 128])
    cvp = dout("cvp", [L, 128, 128]); cvs = dout("cvs", [L, NB, 128, 128])
    MODP = dscr("MODP", [L, 6, 128, D], F32); MODS = dscr("MODS", [L, 6, 128, D], F32)
    QK = dscr("QK", [10, 128, SEQ], BF16)
    VB = dscr("VB", [SEQ, 384], BF16); VC = dscr("VC", [SEQ, 128], BF16)
    VSB = dscr("VSB", [128, 384], BF16); VSC = dscr("VSC", [128, 128], BF16)
    XA = dscr("XA", [SEQ + 128, D], F32); XB = dscr("XB", [SEQ + 128, D], F32)
    DBd = {}

    outc = [0]

    def DB(*key):
        if key == ("OUT",):
            outc[0] += 1
            key = ("OUT", outc[0])
        if key not in DBd:
            DBd[key] = Buf(str(key))
        return DBd[key]

    SBW = 51 * 1024
    big = nc.sbuf_tensor("big", [128, SBW], F32).__enter__()
    sb = SBA(big, SBW)
    ps = [nc.psum_tensor(f"ps{i}", [128, 512], F32).__enter__() for i in range(8)]
    psb = [p[:].bitcast(BF16) for p in ps]
    pb = [Buf(f"ps{i}", excl=True) for i in range(8)]
    sems = [nc.alloc_semaphore(name=f"ks{i}") for i in range(100)]
    tk = TK(nc, sems)
    PE, ACT, DVE, POOL, SP = tk.pe, tk.act, tk.dve, tk.pool, tk.sp
    dpool = [tk.dsem() for _ in range(74)]
    dpool_sw = [tk.dsem() for _ in range(16)]
    dsi = [0, 0]

    def ds_take(n=1, sw=False):
        pool, ix = (dpool_sw, 1) if sw else (dpool, 0)
        r = pool[dsi[ix]:dsi[ix] + n]
        assert len(r) == n, "out of dma sems"
        dsi[ix] += n
        return r if n > 1 else r[0]

    def ncdma():
        return nc.allow_non_contiguous_dma(reason="small strided loads")

    identb = sb.bf16(128); identf = sb.f32(128)
    maskB = sb.bf16(512); maskC = sb.bf16(512)
    multB = sb.bf16(128); multBn = sb.bf16(8); maskCs = sb.bf16(8); maskCn = sb.bf16(8)
    onesb = sb.bf16(64); neghalf = sb.f32(4); gfin = sb.f32(D)
    bC = Buf("const")
    dc = ds_take(sw=True)
    tk.dma(POOL, dc, [(identb, c_ident), (maskB, c_maskB), (maskC, c_maskC), (multB, c_multB),
                      (multBn[0:8, :], c_multBn), (maskCs, c_maskCs), (maskCn[0:8, :], c_maskCn)], writes=[bC])
    tk.dma(SP, ds_take(), [(identf, c_ident), (gfin, norm_final.partition_broadcast(128))], writes=[bC])
    tk.op(DVE, lambda: nc.vector.memset(onesb, 1.0), writes=[bC])
    tk.op(DVE, lambda: nc.vector.memset(neghalf, -0.5), writes=[bC])
    dcc = ds_take(2)
    for l_ in range(nlayers):
        prs = []
        for b in range(NB):
            prs += [(bks[l_, b, 0:WB - 8, :], cbk[l_, b, 8:WB, :]), (bvs[l_, b, 0:WB - 8, :], cbv[l_, b, 8:WB, :]),
                    (cks[l_, b, 0:120, :], cck[l_, b, 8:128, :]), (cvs[l_, b, 0:120, :], ccv[l_, b, 8:128, :])]
        tk.dma(ACT, dcc[l_], prs, writes=[DB("OUT")])
    ds_base = list(dsi)

    def phase_begin():
        dsi[0], dsi[1] = ds_base
        return sb.mark()

    def phase_end(m):
        sb.release(m)
        tk.barrier()

    def rstd_from_ss(ss, rs, n, bss, brs):
        tk.op(POOL, lambda: nc.gpsimd.tensor_scalar(out=rs[:, 0:n], in0=ss[:, 0:n], scalar1=1.0 / D, scalar2=EPS,
                                                    op0=ALU.mult, op1=ALU.add), reads=[bss], writes=[brs])
        tk.op(POOL, lambda: nc.gpsimd.tensor_tensor(out=rs[:, 0:n], in0=rs[:, 0:n], in1=neghalf[:, 0:n], op=ALU.pow),
              reads=[brs, bC], writes=[brs])

    def transpose8(src_bf, bsrc, bank, bbank):
        def f():
            for k in range(8):
                i = nc.tensor.transpose(out=psb[bank][:, k * 128:(k + 1) * 128], in_=src_bf[:, k * 128:(k + 1) * 128],
                                        identity=identb)
            return i
        tk.op(PE, f, reads=[bsrc, bC], writes=[bbank])

    def mm_acc(out, lhs_fn, rhs_fn, nk):
        def f():
            for k in range(nk):
                i = nc.tensor.matmul(out, lhsT=lhs_fn(k), rhs=rhs_fn(k), start=(k == 0), stop=(k == nk - 1))
            return i
        return f

    st = dict(nc=nc, tk=tk, sb=sb, ps=ps, psb=psb, pb=pb)

    def phase_mod():
        m = phase_begin()
        cp = sb.f32(D); cs = sb.f32(D)
        scb = [sb.bf16(D), sb.bf16(D)]
        scT = sb.bf16(2 * D).rearrange("p (g k t) -> p g k t", g=2, k=8)
        gam = [sb.f32(D), sb.f32(D)]
        wm = Ring([sb.bf16(8 * 512).rearrange("p (k c) -> p k c", k=8) for _ in range(2)], ds_take(2, sw=True), "wm")
        bm = Ring([sb.f32(512) for _ in range(2)], ds_take(2), "bm")
        stg = Ring([sb.f32(512) for _ in range(4)], ds_take(4), "stg")
        bcp, bsc, bscT, bg = Buf(), Buf(), Buf(), Buf()
        tk.dma(SP, ds_take(), [(cp, cpe), (cs, cse)], writes=[bcp])
        tk.op(ACT, lambda: nc.scalar.activation(out=scb[0], in_=cp, func=AF.Silu), reads=[bcp], writes=[bsc])
        tk.op(ACT, lambda: nc.scalar.activation(out=scb[1], in_=cs, func=AF.Silu), reads=[bcp], writes=[bsc])
        for g in range(2):
            transpose8(scb[g], bsc, g, pb[g])
            tk.op(DVE, lambda g=g: nc.vector.tensor_copy(out=scT[:, g], in_=psb[g].rearrange("p (k t) -> p k t", k=8)),
                  reads=[pb[g]], writes=[bscT])
        dg = ds_take()
        for l in range(nlayers):
            tk.dma(SP, dg, [(gam[0], norm_mix[l].partition_broadcast(128)), (gam[1], norm_ffn[l].partition_broadcast(128))],
                   writes=[bg])
            for n in range(12):
                w_ap, w_b, w_d = wm.next()
                b_ap, b_b, b_d = bm.next()
                src = w_mod[l, :, n * 512:(n + 1) * 512].rearrange("(k p) c -> p k c", p=128)
                tk.dma(POOL, w_d, [(w_ap, src)], writes=[w_b])
                tk.dma(SP, b_d, [(b_ap, b_mod[l, n * 512:(n + 1) * 512].partition_broadcast(128))], writes=[b_b])
                j, half = n // 2, n % 2
                for g in range(2):
                    bank = 2 + ((2 * n + g) % 4)
                    tk.op(PE, mm_acc(ps[bank][:, :], lambda k, g=g: scT[:, g, k, :], lambda k, w_ap=w_ap: w_ap[:, k, :], 8),
                          reads=[bscT, w_b], writes=[pb[bank]])
                    s_ap, s_b, s_d = stg.next()
                    tk.op(DVE, lambda s_ap=s_ap, bank=bank, b_ap=b_ap: nc.vector.tensor_tensor(
                        out=s_ap, in0=ps[bank][:, :], in1=b_ap, op=ALU.add), reads=[pb[bank], b_b], writes=[s_b])
                    if j in (1, 4):
                        gg = gam[0 if j == 1 else 1][:, half * 512:(half + 1) * 512]
                        tk.op(DVE, lambda s_ap=s_ap, gg=gg: nc.vector.scalar_tensor_tensor(
                            out=s_ap, in0=s_ap, scalar=1.0, in1=gg, op0=ALU.add, op1=ALU.mult), reads=[s_b, bg], writes=[s_b])
                    dst = (MODP, MODS)[g][l, j, :, half * 512:(half + 1) * 512]
                    tk.dma(SP, s_d, [(dst, s_ap)], reads=[s_b], writes=[DB("MOD", g, l, j, half)])
        phase_end(m)

    def mod_bufs(g, l, j):
        return [DB("MOD", g, l, j, 0), DB("MOD", g, l, j, 1)]

    def make_norm(l, which):
        o = {}
        jg, jsh = (1, 0) if which == 0 else (4, 3)
        o["gs"] = sb.f32(8); o["sh"] = sb.f32(8); o["b"] = Buf()
        with ncdma():
            tk.dma(SP, ds_take(), [(o["gs"], MODP[l, jg, 0, :].rearrange("(k p) -> p k", p=128)),
                                   (o["sh"], MODP[l, jsh, 0, :].rearrange("(k p) -> p k", p=128))],
                   reads=mod_bufs(0, l, jg) + mod_bufs(0, l, jsh), writes=[o["b"]])
        o["junk"] = sb.bf16(D); o["bjunk"] = Buf()
        o["ss"] = sb.f32(4); o["rs"] = sb.f32(4); o["bss"] = Buf(); o["brs"] = Buf()
        o["xsb"] = Ring([sb.bf16(D) for _ in range(2)], None, "xsb")
        o["tb"] = 0
        o["l"], o["which"], o["jg"], o["jsh"] = l, which, jg, jsh
        return o

    def norm_tile_prompt(o, xt, bx, hT_dst, bh, tbanks):
        ss, rs = o["ss"], o["rs"]
        tk.op(ACT, lambda: nc.scalar.activation(out=o["junk"], in_=xt, func=AF.Square, accum_out=ss[:, 0:1]),
              reads=[bx], writes=[o["bjunk"], o["bss"]])
        rstd_from_ss(ss, rs, 1, o["bss"], o["brs"])
        x_ap, x_b, _ = o["xsb"].next()
        tk.op(ACT, lambda: nc.scalar.activation(out=x_ap, in_=xt, func=AF.Copy, scale=rs[:, 0:1]),
              reads=[bx, o["brs"]], writes=[x_b])
        bank = tbanks[o["tb"] % len(tbanks)]; o["tb"] += 1
        transpose8(x_ap, x_b, bank, pb[bank])
        pv = psb[bank].rearrange("p (k t) -> p k t", k=8)
        for k in range(8):
            e = DVE if k % 2 == 0 else POOL
            if e is DVE:
                tk.op(DVE, lambda k=k: nc.vector.tensor_scalar(out=hT_dst[:, k, :], in0=pv[:, k, :], scalar1=o["gs"][:, k:k + 1],
                                                               scalar2=o["sh"][:, k:k + 1], op0=ALU.mult, op1=ALU.add),
                      reads=[pb[bank], o["b"]], writes=[bh])
            else:
                tk.op(ACT, lambda k=k: nc.scalar.activation(out=hT_dst[:, k, :], in_=pv[:, k, :], func=AF.Identity,
                                                            scale=o["gs"][:, k:k + 1], bias=o["sh"][:, k:k + 1]),
                      reads=[pb[bank], o["b"]], writes=[bh])

    def norm_tile_sample(o, xt, bx, hT_dst, bh, tbanks, gsS, shS, bmodS, tmp):
        ss, rs = o["ss"], o["rs"]
        tk.op(ACT, lambda: nc.scalar.activation(out=o["junk"], in_=xt, func=AF.Square, accum_out=ss[:, 0:1]),
              reads=[bx], writes=[o["bjunk"], o["bss"]])
        rstd_from_ss(ss, rs, 1, o["bss"], o["brs"])
        x_ap, x_b, _ = o["xsb"].next()
        tap, tb_ = tmp
        tk.op(DVE, lambda: nc.vector.scalar_tensor_tensor(out=tap, in0=xt, scalar=rs[:, 0:1], in1=gsS, op0=ALU.mult, op1=ALU.mult),
              reads=[bx, o["brs"], bmodS], writes=[tb_])
        tk.op(DVE, lambda: nc.vector.tensor_tensor(out=x_ap, in0=tap, in1=shS, op=ALU.add), reads=[tb_, bmodS], writes=[x_b])
        bank = tbanks[o["tb"] % len(tbanks)]; o["tb"] += 1
        transpose8(x_ap, x_b, bank, pb[bank])
        tk.op(DVE, lambda: nc.vector.tensor_copy(out=hT_dst, in_=psb[bank].rearrange("p (k t) -> p k t", k=8)),
              reads=[pb[bank]], writes=[bh])

    def x_src(l, t):
        if l == 0:
            return (xp[t * 128:(t + 1) * 128, :] if t < NT else xs), []
        return XB[t * 128:(t + 1) * 128, :], [DB("XB", t)]

    def load_w_in(l):
        win = sb.bf16(8 * INW).rearrange("p (k c) -> p k c", k=8)
        bwin = Buf()
        src = w_in[l].rearrange("(k p) c -> p k c", p=128)
        dsw = ds_take(2, sw=True)
        pairs = []
        for c0 in (0, 512, 1024, 1536):
            c1 = min(c0 + 512, C_QC)
            pairs.append((win[:, :, c0:c1], src[:, :, c0:c1]))
        for j in range(3):
            o0 = C_QC + 128 * j
            pairs.append((win[:, :, o0:o0 + 64], src[:, :, C_QC + 64 * j:C_QC + 64 * j + 64]))
            pairs.append((win[:, :, o0 + 64:o0 + 128], src[:, :, C_QC + 64 * (j + 3):C_QC + 64 * (j + 3) + 64]))
        pairs.append((win[:, :, C_KC:INW], src[:, :, C_KC:INW]))
        for pi, pr in enumerate(pairs):
            tk.dma(POOL, dsw[pi % 2], [pr], writes=[bwin])
        cw = sb.f32(6).rearrange("p (c i) -> p c i", c=2)
        with ncdma():
            tk.dma(SP, ds_take(), [(cw[:, c, i:i + 1], conv_w[l, i, c * 128:(c + 1) * 128].rearrange("(p o) -> p o", o=1))
                                   for c in range(2) for i in range(3)], writes=[bwin])
        return win, cw, bwin

    def phase_a1(l, mixT, bmix):
        m = phase_begin()
        win, cw, bwin = load_w_in(l)
        nm = make_norm(l, 0)
        m2 = sb.mark()
        xr = Ring([sb.f32(D) for _ in range(3)], ds_take(3), "x")
        hr = Ring([sb.bf16(8 * 512).rearrange("p (k t) -> p k t", k=8) for _ in range(2)], None, "hT")
        qst = Ring([sb.bf16(512) for _ in range(3)], ds_take(3), "qst")
        vst = Ring([sb.bf16(512) for _ in range(2)], ds_take(2), "vst")
        fst = Ring([sb.f32(384) for _ in range(3)], ds_take(3), "fst")
        gaS = sb.f32(1024).rearrange("p (c t) -> p c t", c=2); gcS = sb.f32(1024).rearrange("p (c t) -> p c t", c=2)
        ub = sb.f32(2 * 514).rearrange("p (c t) -> p c t", c=2)
        cacc = sb.f32(512)
        bga, bgc, bub, bcacc = Buf(), Buf(), Buf(), Buf()
        tk.op(POOL, lambda: nc.gpsimd.memset(ub[:, :, 0:2], 0.0), writes=[bub])
        pbank = [2]

        def nbank():
            b = pbank[0]
            pbank[0] = 2 + (pbank[0] - 2 + 1) % 6
            return b

        import os
        for g in range(int(os.environ.get('NGROUPS', NT // 4))):
            h_ap, h_b, _ = hr.next()
            for tt in range(4):
                t = 4 * g + tt
                x_ap, x_b, x_d = xr.next()
                src, sbufs = x_src(l, t)
                tk.dma(SP, x_d, [(x_ap, src)], reads=sbufs, writes=[x_b])
                norm_tile_prompt(nm, x_ap, x_b, h_ap[:, :, tt * 128:(tt + 1) * 128], h_b, (0, 1))
            N = 512
            pos0 = g * 512

            def fm(col, bank):
                tk.op(PE, mm_acc(ps[bank][:, 0:N], lambda k: win[:, k, col:col + 128], lambda k: h_ap[:, k, 0:N], 8),
                      reads=[bwin, h_b], writes=[pb[bank]])
            PARTS = os.environ.get('A1_PARTS', 'cqt')
            for c in (range(2) if 'c' in PARTS else []):
                bk = nbank(); fm(C_GA + 128 * c, bk)
                tk.op(ACT, lambda c=c, bk=bk: nc.scalar.copy(out=gaS[:, c, :], in_=ps[bk][:, 0:N]), reads=[pb[bk]], writes=[bga])
            for c in (range(2) if 'c' in PARTS else []):
                bk = nbank(); fm(C_GC + 128 * c, bk)
                tk.op(ACT, lambda c=c, bk=bk: nc.scalar.copy(out=gcS[:, c, :], in_=ps[bk][:, 0:N]), reads=[pb[bk]], writes=[bgc])
            for c in (range(2) if 'c' in PARTS else []):
                bk = nbank(); fm(C_XA + 128 * c, bk)
                tk.op(DVE, lambda c=c, bk=bk: nc.vector.tensor_tensor(out=ub[:, c, 2:2 + N], in0=ps[bk][:, 0:N], in1=gcS[:, c, :],
                                                                        op=ALU.mult), reads=[pb[bk], bgc], writes=[bub])
                tk.op(DVE, lambda c=c: nc.vector.tensor_scalar(out=cacc, in0=ub[:, c, 2:2 + N], scalar1=cw[:, c, 2:3], scalar2=None,
                                                               op0=ALU.mult), reads=[bub, bwin], writes=[bcacc])
                tk.op(DVE, lambda c=c: nc.vector.scalar_tensor_tensor(out=cacc, in0=ub[:, c, 1:1 + N], scalar=cw[:, c, 1:2], in1=cacc,
                                                                      op0=ALU.mult, op1=ALU.add), reads=[bub, bwin, bcacc], writes=[bcacc])
                tk.op(DVE, lambda c=c: nc.vector.scalar_tensor_tensor(out=cacc, in0=ub[:, c, 0:N], scalar=cw[:, c, 0:1], in1=cacc,
                                                                      op0=ALU.mult, op1=ALU.add), reads=[bub, bwin, bcacc], writes=[bcacc])
                tk.op(DVE, lambda c=c: nc.vector.tensor_tensor(out=mixT[:, c, pos0:pos0 + N], in0=cacc, in1=gaS[:, c, :], op=ALU.mult),
                      reads=[bcacc, bga], writes=[bmix])
                if g == NT // 4 - 1:
                    with ncdma():
                        tk.dma(SP, ds_take(), [(convp[l, :, c * 128:(c + 1) * 128].rearrange("j p -> p j"), ub[:, c, N:N + 2])],
                               reads=[bub], writes=[DB("OUT")])
                tk.op(POOL, lambda c=c: nc.gpsimd.tensor_copy(out=ub[:, c, 0:2], in_=ub[:, c, N:N + 2]), reads=[bub], writes=[bub])
            for ci, col in enumerate([C_QB, C_QB + 128, C_QB + 256, C_KB, C_KB + 128, C_KB + 256,
                                      C_QC, C_QC + 128, C_QC + 256, C_KC] if 'q' in PARTS else []):
                bk = nbank(); fm(col, bk)
                s_ap, s_b, s_d = qst.next()
                e = ACT if ci % 2 == 0 else DVE
                if e is ACT:
                    tk.op(ACT, lambda s_ap=s_ap, bk=bk: nc.scalar.copy(out=s_ap, in_=ps[bk][:, 0:N]), reads=[pb[bk]], writes=[s_b])
                else:
                    tk.op(DVE, lambda s_ap=s_ap, bk=bk: nc.vector.tensor_copy(out=s_ap, in_=ps[bk][:, 0:N]), reads=[pb[bk]], writes=[s_b])
                tk.dma(SP, s_d, [(QK[ci, :, pos0:pos0 + N], s_ap)], reads=[s_b], writes=[DB("QK", ci, g)])
            for tt in (range(4) if 't' in PARTS else []):
                t = 4 * g + tt
                hs = lambda k, tt=tt: h_ap[:, k, tt * 128:(tt + 1) * 128]
                bk = nbank()
                tk.op(PE, mm_acc(ps[bk][:, 0:384], hs, lambda k: win[:, k, C_VB:C_VB + 384], 8), reads=[bwin, h_b], writes=[pb[bk]])
                s_ap, s_b, s_d = vst.next()
                tk.op(ACT, lambda s_ap=s_ap, bk=bk: nc.scalar.copy(out=s_ap[:, 0:384], in_=ps[bk][:, 0:384]), reads=[pb[bk]], writes=[s_b])
                if t >= NT // 2:
                    f_ap, f_b, f_d = fst.next()
                    tk.op(DVE, lambda f_ap=f_ap, bk=bk: nc.vector.tensor_copy(out=f_ap, in_=ps[bk][:, 0:384]), reads=[pb[bk]], writes=[f_b])
                    tk.dma(SP, f_d, [(bvp[l, (t - 16) * 128:(t - 15) * 128, :], f_ap)], reads=[f_b], writes=[DB("OUT")])
                    bk2 = nbank()
                    tk.op(PE, mm_acc(ps[bk2][:, 0:384], hs, lambda k: win[:, k, C_KB:C_KB + 384], 8), reads=[bwin, h_b], writes=[pb[bk2]])
                    f_ap, f_b, f_d = fst.next()
                    tk.op(DVE, lambda f_ap=f_ap, bk2=bk2: nc.vector.tensor_copy(out=f_ap, in_=ps[bk2][:, 0:384]), reads=[pb[bk2]], writes=[f_b])
                    tk.dma(SP, f_d, [(bkp[l, (t - 16) * 128:(t - 15) * 128, :], f_ap)], reads=[f_b], writes=[DB("OUT")])
                bk3 = nbank()
                tk.op(PE, mm_acc(ps[bk3][:, 0:256], hs, lambda k: win[:, k, C_KC:C_KC + 256], 8), reads=[bwin, h_b], writes=[pb[bk3]])
                tk.op(ACT, lambda s_ap=s_ap, bk3=bk3: nc.scalar.copy(out=s_ap[:, 384:512], in_=ps[bk3][:, 128:256]), reads=[pb[bk3]], writes=[s_b])
                tk.dma(SP, s_d, [(VB[t * 128:(t + 1) * 128, :], s_ap[:, 0:384]), (VC[t * 128:(t + 1) * 128, :], s_ap[:, 384:512])],
                       reads=[s_b], writes=[DB("V", t)])
                if t == NT - 1:
                    f_ap, f_b, f_d = fst.next()
                    tk.op(DVE, lambda f_ap=f_ap, bk3=bk3: nc.vector.tensor_copy(out=f_ap[:, 0:256], in_=ps[bk3][:, 0:256]), reads=[pb[bk3]], writes=[f_b])
                    tk.dma(SP, f_d, [(ckp[l], f_ap[:, 0:128]), (cvp[l], f_ap[:, 128:256])], reads=[f_b], writes=[DB("OUT")])
        sb.release(m2)
        tk.barrier()
        return m, win, cw, bwin, nm

    def phase_a1_sample(l, mixT, bmix, qkS, bqkS, win, cw, bwin, nm):
        gsS = sb.f32(D); shS = sb.f32(D); tmpx = sb.f32(D)
        bmodS, btmp = Buf(), Buf()
        tk.dma(SP, ds_take(), [(gsS, MODS[l, 1]), (shS, MODS[l, 0])], reads=mod_bufs(1, l, 1) + mod_bufs(1, l, 0), writes=[bmodS])
        xt = sb.f32(D); bx = Buf()
        src, sbufs = x_src(l, NT)
        tk.dma(SP, ds_take(), [(xt, src)], reads=sbufs, writes=[bx])
        hT = sb.bf16(8 * 128).rearrange("p (k t) -> p k t", k=8); bh = Buf()
        norm_tile_sample(nm, xt, bx, hT, bh, (0, 1), gsS, shS, bmodS, (tmpx, btmp))
        N = 128
        bki = [2]

        def nbank():
            b = bki[0]
            bki[0] = 2 + (bki[0] - 2 + 1) % 6
            return b

        def fm(col, bank):
            tk.op(PE, mm_acc(ps[bank][:, 0:N], lambda k: win[:, k, col:col + 128], lambda k: hT[:, k, :], 8),
                  reads=[bwin, bh], writes=[pb[bank]])
        gaS = sb.f32(256).rearrange("p (c t) -> p c t", c=2); gcS = sb.f32(256).rearrange("p (c t) -> p c t", c=2)
        ue = sb.f32(2 * 160).rearrange("p (c b j) -> p c b j", c=2, b=NB)
        stt = sb.f32(256); cacc = sb.f32(128); utmp = sb.f32(64).rearrange("p (c q) -> p c q", c=2); urow = sb.f32(256)
        bga, bgc, bue, bst, bcacc, butmp, burow = Buf(), Buf(), Buf(), Buf(), Buf(), Buf(), Buf()
        tk.dma(SP, ds_take(), [(stt[0:32, :], sconv[l])], writes=[bst])
        bk = nbank()

        def tr_state():
            for c in range(2):
                i = nc.tensor.transpose(out=ps[bk][:, c * 32:(c + 1) * 32], in_=stt[0:32, c * 128:(c + 1) * 128], identity=identf[0:32, 0:32])
            return i
        tk.op(PE, tr_state, reads=[bst, bC], writes=[pb[bk]])
        tk.op(DVE, lambda: nc.vector.tensor_copy(out=ue[:, :, :, 0:2], in_=ps[bk][:, 0:64].rearrange("p (c b j) -> p c b j", c=2, b=NB)),
              reads=[pb[bk]], writes=[bue])
        for c in range(2):
            b1 = nbank(); fm(C_GA + 128 * c, b1)
            tk.op(ACT, lambda c=c, b1=b1: nc.scalar.copy(out=gaS[:, c, :], in_=ps[b1][:, 0:N]), reads=[pb[b1]], writes=[bga])
            b2 = nbank(); fm(C_GC + 128 * c, b2)
            tk.op(ACT, lambda c=c, b2=b2: nc.scalar.copy(out=gcS[:, c, :], in_=ps[b2][:, 0:N]), reads=[pb[b2]], writes=[bgc])
            b3 = nbank(); fm(C_XA + 128 * c, b3)
            tk.op(DVE, lambda c=c, b3=b3: nc.vector.tensor_tensor(out=ue[:, c, :, 2:10], in0=ps[b3][:, 0:N].rearrange("p (b i) -> p b i", b=NB),
                                                                    in1=gcS[:, c, :].rearrange("p (b i) -> p b i", b=NB), op=ALU.mult),
                  reads=[pb[b3], bgc, bue], writes=[bue])
            ca3 = cacc.rearrange("p (b i) -> p b i", b=NB)
            tk.op(DVE, lambda c=c: nc.vector.tensor_scalar(out=ca3, in0=ue[:, c, :, 2:10], scalar1=cw[:, c, 2:3], scalar2=None, op0=ALU.mult),
                  reads=[bue, bwin], writes=[bcacc])
            tk.op(DVE, lambda c=c: nc.vector.scalar_tensor_tensor(out=ca3, in0=ue[:, c, :, 1:9], scalar=cw[:, c, 1:2], in1=ca3,
                                                                  op0=ALU.mult, op1=ALU.add), reads=[bue, bwin, bcacc], writes=[bcacc])
            tk.op(DVE, lambda c=c: nc.vector.scalar_tensor_tensor(out=ca3, in0=ue[:, c, :, 0:8], scalar=cw[:, c, 0:1], in1=ca3,
                                                                  op0=ALU.mult, op1=ALU.add), reads=[bue, bwin, bcacc], writes=[bcacc])
            tk.op(DVE, lambda c=c: nc.vector.tensor_tensor(out=mixT[:, c, SEQ:SEQ + N], in0=cacc, in1=gaS[:, c, :], op=ALU.mult),
                  reads=[bcacc, bga], writes=[bmix])
            tk.op(DVE, lambda c=c: nc.vector.tensor_copy(out=utmp[:, c, :].rearrange("p (b j) -> p b j", b=NB), in_=ue[:, c, :, 8:10]),
                  reads=[bue], writes=[butmp])
        bk = nbank()

        def tr_u():
            for c in range(2):
                i = nc.tensor.transpose(out=ps[bk][0:32, c * 128:(c + 1) * 128], in_=utmp[:, c, :], identity=identf)
            return i
        tk.op(PE, tr_u, reads=[butmp, bC], writes=[pb[bk]])
        tk.op(DVE, lambda: nc.vector.tensor_copy(out=urow[0:32, :], in_=ps[bk][0:32, 0:256]), reads=[pb[bk]], writes=[burow])
        tk.dma(SP, ds_take(), [(convs[l].rearrange("b j f -> (b j) f"), urow[0:32, :])], reads=[burow], writes=[DB("OUT")])
        for ci, col in enumerate([C_QB, C_QB + 128, C_QB + 256, C_KB, C_KB + 128, C_KB + 256, C_QC, C_QC + 128, C_QC + 256, C_KC]):
            b1 = nbank(); fm(col, b1)
            tk.op(ACT, lambda ci=ci, b1=b1: nc.scalar.copy(out=qkS[:, ci, :], in_=ps[b1][:, 0:N]), reads=[pb[b1]], writes=[bqkS])
        kvf = sb.f32(768 + 256); kvb = sb.bf16(512); bkvf, bkvb = Buf(), Buf()
        b1 = nbank()
        tk.op(PE, mm_acc(ps[b1][:, 0:384], lambda k: hT[:, k, :], lambda k: win[:, k, C_KB:C_KB + 384], 8), reads=[bwin, bh], writes=[pb[b1]])
        tk.op(ACT, lambda: nc.scalar.copy(out=kvf[:, 0:384], in_=ps[b1][:, 0:384]), reads=[pb[b1]], writes=[bkvf])
        b2 = nbank()
        tk.op(PE, mm_acc(ps[b2][:, 0:384], lambda k: hT[:, k, :], lambda k: win[:, k, C_VB:C_VB + 384], 8), reads=[bwin, bh], writes=[pb[b2]])
        tk.op(ACT, lambda: nc.scalar.copy(out=kvf[:, 384:768], in_=ps[b2][:, 0:384]), reads=[pb[b2]], writes=[bkvf])
        tk.op(DVE, lambda: nc.vector.tensor_copy(out=kvb[:, 0:384], in_=ps[b2][:, 0:384]), reads=[pb[b2]], writes=[bkvb])
        b3 = nbank()
        tk.op(PE, mm_acc(ps[b3][:, 0:256], lambda k: hT[:, k, :], lambda k: win[:, k, C_KC:C_KC + 256], 8), reads=[bwin, bh], writes=[pb[b3]])
        tk.op(ACT, lambda: nc.scalar.copy(out=kvf[:, 768:1024], in_=ps[b3][:, 0:256]), reads=[pb[b3]], writes=[bkvf])
        tk.op(DVE, lambda: nc.vector.tensor_copy(out=kvb[:, 384:512], in_=ps[b3][:, 128:256]), reads=[pb[b3]], writes=[bkvb])
        tk.dma(SP, ds_take(), [(VSB[:, :], kvb[:, 0:384]), (VSC[:, :], kvb[:, 384:512])], reads=[bkvb], writes=[DB("VS", l)])
        prs = []
        for b in range(NB):
            r = slice(8 * b, 8 * b + 8)
            prs += [(bks[l, b, WB - 8:WB, :], kvf[r, 0:384]), (bvs[l, b, WB - 8:WB, :], kvf[r, 384:768]),
                    (cks[l, b, 120:128, :], kvf[r, 768:896]), (cvs[l, b, 120:128, :], kvf[r, 896:1024])]
        tk.dma(SP, ds_take(), prs, reads=[bkvf], writes=[DB("OUT")])

    def phase_a2(l, mixT, bmix):
        m = phase_begin()
        qT = sb.bf16(SEQ); kT = sb.bf16(SEQ); bq, bkk = Buf(), Buf(); dq, dk = ds_take(), ds_take()
        Vd = [sb.bf16(32 * 128).rearrange("p (t f) -> p t f", t=32) for _ in range(3)]
        bV = [Buf() for _ in range(3)]; dV = ds_take(3)
        acc = sb.f32(2 * SEQ).rearrange("p (o t) -> p o t", o=2); bacc = Buf()
        Pst = Ring([sb.bf16(512) for _ in range(4)], None, "P")
        es = sb.f32(1); bes = Buf(); des = ds_take()
        sbanks = ((0, 1), (2, 3)); obanks = (4, 5)
        itc = [0]

        def sl(d, r, blk):
            s0 = d * 128 * blk + r
            return slice(s0, s0 + 127 * d + 1, d)

        def load_V(dst, bdst, dd, src_t, rowlen, coloff, d):
            nblk = 32 // d
            dv = dst.rearrange("p (r b) f -> p r b f", r=d)
            prs = []
            if d == 1:
                for q4 in range(4):
                    ap = bass.AP(src_t.tensor, coloff + q4 * 8 * 128 * rowlen, [[rowlen, 128], [128 * rowlen, 8], [1, 128]])
                    prs.append((dst[:, q4 * 8:(q4 + 1) * 8, :], ap))
            else:
                for r in range(d):
                    ap = bass.AP(src_t.tensor, coloff + r * rowlen, [[d * rowlen, 128], [d * 128 * rowlen, nblk], [1, 128]])
                    prs.append((dv[:, r, :, :], ap))
            with ncdma():
                tk.dma(SP, dd, prs, reads=[DB("V", t) for t in range(NT)], writes=[bdst])

        for kind, j in [("B", 0), ("B", 1), ("B", 2), ("C", 0), ("C", 1), ("C", 2)]:
            isB = kind == "B"
            qc_, kc_ = (j, 3 + j) if isB else (6 + j, 9)
            tk.dma(SP, dq, [(qT, QK[qc_])], reads=[DB("QK", qc_, g) for g in range(8)], writes=[bq])
            if isB or j == 0:
                tk.dma(SP, dk, [(kT, QK[kc_])], reads=[DB("QK", kc_, g) for g in range(8)], writes=[bkk])
            if isB:
                for di, d in enumerate(DILS):
                    load_V(Vd[di], bV[di], dV[di], VB, 384, 128 * j, d)
            elif j == 0:
                load_V(Vd[0], bV[0], dV[0], VC, 128, 0, 1)
            if not isB:
                tk.dma(SP, des, [(es, sinkP[l, j, :].rearrange("(p o) -> p o", o=1))], writes=[bes])
                tk.op(ACT, lambda: nc.scalar.activation(out=es, in_=es, func=AF.Exp), reads=[bes], writes=[bes])
            tk.op(POOL, lambda: nc.gpsimd.memset(acc, 0.0), writes=[bacc])
            mask = maskB if isB else maskC
            def stage_a(di, d, r, b0, it):
                blks = (b0, b0 + 1)
                Pp = []
                for s in (0, 1):
                    bank = sbanks[s][it % 2]

                    def f(s=s, bank=bank):
                        for bi, blk in enumerate(blks):
                            qs = qT[64 * s:64 * s + 64, sl(d, r, blk)]
                            for w, kb in enumerate((blk - 1, blk)):
                                if kb < 0:
                                    continue
                                i = nc.tensor.matmul(ps[bank][:, (bi * 2 + w) * 128:(bi * 2 + w + 1) * 128],
                                                     lhsT=kT[64 * s:64 * s + 64, sl(d, r, kb)], rhs=qs, start=True, stop=True)
                        return i
                    tk.op(PE, f, reads=[bq, bkk], writes=[pb[bank]])
                    p_ap, p_b, _ = Pst.next()
                    tk.op(ACT, lambda p_ap=p_ap, bank=bank: nc.scalar.activation(out=p_ap, in_=ps[bank][:, :], func=AF.Exp, scale=SC),
                          reads=[pb[bank]], writes=[p_b])
                    tk.op(POOL, lambda p_ap=p_ap: nc.gpsimd.tensor_tensor(out=p_ap, in0=p_ap, in1=mask, op=ALU.mult),
                          reads=[p_b, bC], writes=[p_b])
                    Pp.append((p_ap, p_b))
                return Pp

            def stage_b(di, d, r, b0, it, Pp):
                blks = (b0, b0 + 1)
                nblk = 32 // d
                Vt, bVt = Vd[di], bV[di]
                obank = obanks[it % 2]

                def gpv():
                    for bi, blk in enumerate(blks):
                        for s in (0, 1):
                            p_ap = Pp[s][0]
                            ws = [w for w in (0, 1) if blk - 1 + w >= 0]
                            for which in (0, 1):
                                for wi, w in enumerate(ws):
                                    kb = blk - 1 + w
                                    lhsT = Vt[:, r * nblk + kb, 64 * s:64 * s + 64] if which == 0 else onesb[:, 0:64]
                                    i = nc.tensor.matmul(ps[obank][64 * s:64 * s + 64, (bi * 2 + which) * 128:(bi * 2 + which + 1) * 128],
                                                         lhsT=lhsT, rhs=p_ap[:, (bi * 2 + w) * 128:(bi * 2 + w + 1) * 128],
                                                         start=(wi == 0), stop=(wi == len(ws) - 1))
                    return i
                tk.op(PE, gpv, reads=[Pp[0][1], Pp[1][1], bVt, bC], writes=[pb[obank]])
                for bi, blk in enumerate(blks):
                    tk.op(DVE, lambda bi=bi, blk=blk: nc.vector.tensor_tensor(
                        out=acc[:, :, sl(d, r, blk)], in0=ps[obank][:, bi * 256:(bi + 1) * 256].rearrange("p (o t) -> p o t", o=2),
                        in1=acc[:, :, sl(d, r, blk)], op=ALU.add), reads=[pb[obank], bacc], writes=[bacc])

            its = []
            for di, d in enumerate(DILS if isB else (1,)):
                for r in range(d):
                    for b0 in range(0, 32 // d, 2):
                        its.append((di, d, r, b0, itc[0])); itc[0] += 1
            prev = None
            for info in its:
                Pp = stage_a(*info)
                if prev is not None:
                    stage_b(*prev)
                prev = info + (Pp,)
            stage_b(*prev)
            if not isB:
                tk.op(DVE, lambda: nc.vector.tensor_scalar(out=acc[:, 1, :], in0=acc[:, 1, :], scalar1=es[:, 0:1], scalar2=None, op0=ALU.add),
                      reads=[bacc, bes], writes=[bacc])
            tk.op(DVE, lambda: nc.vector.reciprocal(out=acc[:, 1, :], in_=acc[:, 1, :]), reads=[bacc], writes=[bacc])
            ch = (2 + j) if isB else (5 + j)
            tk.op(DVE, lambda ch=ch: nc.vector.tensor_tensor(out=mixT[:, ch, 0:SEQ], in0=acc[:, 0, :], in1=acc[:, 1, :], op=ALU.mult),
                  reads=[bacc], writes=[bmix])
        phase_end(m)

    def phase_a2s(l, mixT, bmix, qkS, bqkS):
        m = phase_begin()
        Kc = Ring([sb.bf16(16 * 384).rearrange("p (t f) -> p t f", t=16) for _ in range(2)], ds_take(2, sw=True), "Kc")
        Vc = Ring([sb.bf16(16 * 384).rearrange("p (t f) -> p t f", t=16) for _ in range(2)], ds_take(2, sw=True), "Vc")
        KT = sb.bf16(3 * WB).rearrange("p (j t) -> p j t", j=3); bKT = Buf()
        Vn = sb.bf16(NB * 384).rearrange("p (b f) -> p b f", b=NB); Vcn = sb.bf16(NB * 128).rearrange("p (b f) -> p b f", b=NB)
        bVn = Buf()
        tk.dma(SP, ds_take(), [(Vn[0:8], VSB.rearrange("(b i) f -> i b f", i=8)), (Vcn[0:8], VSC.rearrange("(b i) f -> i b f", i=8))],
               reads=[DB("VS", l)], writes=[bVn])
        Ps = [sb.bf16(408), sb.bf16(408)]; bPs = [Buf(), Buf()]
        rl = sb.f32(8); brl = Buf()
        ysb = sb.bf16(NB * 384).rearrange("p (b f) -> p b f", b=NB); bys = Buf()
        ycs = sb.bf16(NB * 128).rearrange("p (b f) -> p b f", b=NB); byc = Buf()
        es24 = sb.f32(2); bes = Buf()
        tk.dma(SP, ds_take(), [(es24[0:24, :], sinkS[l])], writes=[bes])
        tk.op(ACT, lambda: nc.scalar.activation(out=es24[0:24, :], in_=es24[0:24, :], func=AF.Exp), reads=[bes], writes=[bes])
        Kcc = Ring([sb.bf16(128) for _ in range(2)], ds_take(2, sw=True), "Kcc"); Vcc = Ring([sb.bf16(128) for _ in range(2)], ds_take(2, sw=True), "Vcc")
        KcT = sb.bf16(128); bKcT = Buf()
        SA, SBk = 0, 1
        k_d0 = ds_take(2, sw=True); kdc = [0]

        def kd():
            kdc[0] += 1
            return k_d0[kdc[0] % 2]
        for b in range(NB):
            qsl = slice(8 * b, 8 * b + 8)
            k_ap, k_b, k_d = Kc.next(); v_ap, v_b, v_d = Vc.next()
            ksrc = cbk[l, b].rearrange("(t p) f -> p t f", p=128); vsrc = cbv[l, b].rearrange("(t p) f -> p t f", p=128)
            for hh in range(2):
                tk.dma(POOL, kd(), [(k_ap[:, 8 * hh:8 * hh + 8, :], ksrc[:, 8 * hh:8 * hh + 8, :])], writes=[k_b])
            for hh in range(2):
                tk.dma(POOL, kd(), [(v_ap[:, 8 * hh:8 * hh + 8, :], vsrc[:, 8 * hh:8 * hh + 8, :])], writes=[v_b])
            for j in range(3):
                for tq in range(2):
                    bank = 2 + (j * 2 + tq) % 2

                    def ftr(j=j, tq=tq, bank=bank):
                        for u in range(8):
                            i = nc.tensor.transpose(out=psb[bank][:, u * 128:(u + 1) * 128], in_=k_ap[:, 8 * tq + u, 128 * j:128 * j + 128], identity=identb)
                        return i
                    tk.op(PE, ftr, reads=[k_b, bC], writes=[pb[bank]])
                    e = DVE if tq == 0 else ACT
                    if e is DVE:
                        tk.op(DVE, lambda j=j, tq=tq, bank=bank: nc.vector.tensor_copy(out=KT[:, j, tq * 1024:(tq + 1) * 1024], in_=psb[bank][:, :]),
                              reads=[pb[bank]], writes=[bKT])
                    else:
                        tk.op(ACT, lambda j=j, tq=tq, bank=bank: nc.scalar.copy(out=KT[:, j, tq * 1024:(tq + 1) * 1024], in_=psb[bank][:, :]),
                              reads=[pb[bank]], writes=[bKT])
            for s in (0, 1):
                bank = (SA, SBk)[s]
                pr = slice(64 * s, 64 * s + 64)

                def fs(s=s, bank=bank, pr=pr):
                    for j in range(3):
                        for t in range(16):
                            c0 = (j * 16 + t) * 8
                            i = nc.tensor.matmul(ps[bank][:, c0:c0 + 8], lhsT=KT[pr, j, t * 128:(t + 1) * 128], rhs=qkS[pr, j, qsl], start=True, stop=True)
                        i = nc.tensor.matmul(ps[bank][0:8, 384 + j * 8:384 + j * 8 + 8], lhsT=qkS[pr, 3 + j, qsl], rhs=qkS[pr, j, qsl], start=True, stop=True)
                    return i
                tk.op(PE, fs, reads=[bKT, bqkS], writes=[pb[bank]])
                tk.op(ACT, lambda s=s, bank=bank: nc.scalar.activation(out=Ps[s][:, 0:384], in_=ps[bank][:, 0:384], func=AF.Exp, scale=SC),
                      reads=[pb[bank]], writes=[bPs[s]])
                tk.op(ACT, lambda s=s, bank=bank: nc.scalar.activation(out=Ps[s][0:8, 384:408], in_=ps[bank][0:8, 384:408], func=AF.Exp, scale=SC),
                      reads=[pb[bank]], writes=[bPs[s]])
                mb = bass.AP(multB.tensor, multB.offset, [list(multB.ap[0]), [0, 3], [1, 128]])
                mbn = bass.AP(multBn.tensor, multBn.offset, [[multBn.ap[0][0], 8], [0, 3], [1, 8]])
                tk.op(DVE, lambda s=s, mb=mb: nc.vector.tensor_tensor(out=Ps[s][:, 0:384].rearrange("p (j c) -> p j c", j=3),
                                                                     in0=Ps[s][:, 0:384].rearrange("p (j c) -> p j c", j=3), in1=mb, op=ALU.mult),
                      reads=[bPs[s], bC], writes=[bPs[s]])
                tk.op(DVE, lambda s=s, mbn=mbn: nc.vector.tensor_tensor(out=Ps[s][0:8, 384:408].rearrange("p (j c) -> p j c", j=3),
                                                                       in0=Ps[s][0:8, 384:408].rearrange("p (j c) -> p j c", j=3), in1=mbn, op=ALU.mult),
                      reads=[bPs[s], bC], writes=[bPs[s]])
            ob = 4 + b % 2

            def fpv():
                for j in range(3):
                    for s in (0, 1):
                        h = 2 * j + s
                        for which in (0, 1):
                            o_ap = ps[ob][0:8, h * 65:h * 65 + 64] if which == 0 else ps[ob][0:8, h * 65 + 64:h * 65 + 65]
                            for t in range(16):
                                c0 = (j * 16 + t) * 8
                                rhs = v_ap[:, t, h * 64:(h + 1) * 64] if which == 0 else onesb[:, 0:1]
                                nc.tensor.matmul(o_ap, lhsT=Ps[s][:, c0:c0 + 8], rhs=rhs, start=(t == 0), stop=False)
                            rhs = Vn[0:8, b, h * 64:(h + 1) * 64] if which == 0 else onesb[0:8, 0:1]
                            i = nc.tensor.matmul(o_ap, lhsT=Ps[s][0:8, 384 + j * 8:384 + j * 8 + 8], rhs=rhs, start=False, stop=True)
                return i
            tk.op(PE, fpv, reads=[bPs[0], bPs[1], v_b, bVn, bC], writes=[pb[ob]])
            ov = ps[ob][0:8, 0:390].rearrange("p (h c) -> p h c", h=6)
            tk.op(DVE, lambda ov=ov: nc.vector.reciprocal(out=rl[0:8, 0:6].rearrange("p (h o) -> p h o", o=1), in_=ov[:, :, 64:65]),
                  reads=[pb[ob]], writes=[brl])
            rlb = bass.AP(rl.tensor, rl.offset, [[rl.ap[0][0], 8], [1, 6], [0, 64]])
            tk.op(DVE, lambda ov=ov, rlb=rlb, b=b: nc.vector.tensor_tensor(out=ysb[0:8, b, :].rearrange("p (h c) -> p h c", h=6), in0=ov[:, :, 0:64],
                                                                         in1=rlb, op=ALU.mult), reads=[pb[ob], brl], writes=[bys])
            kc_ap, kc_b, kc_d = Kcc.next(); vc_ap, vc_b, vc_d = Vcc.next()
            tk.dma(POOL, kd(), [(kc_ap, cck[l, b])], writes=[kc_b])
            tk.dma(POOL, kd(), [(vc_ap, ccv[l, b])], writes=[vc_b])
            tk.op(PE, lambda: nc.tensor.transpose(out=psb[2][:, 0:128], in_=kc_ap, identity=identb), reads=[kc_b, bC], writes=[pb[2]])
            tk.op(ACT, lambda: nc.scalar.copy(out=KcT, in_=psb[2][:, 0:128]), reads=[pb[2]], writes=[bKcT])
            for g in (0, 1):
                bank = (SA, SBk)[g]
                pr = slice(64 * g, 64 * g + 64)

                def fsc(g=g, bank=bank, pr=pr):
                    nc.tensor.matmul(ps[bank][:, 0:24], lhsT=KcT[pr, :], rhs=qkS[pr, 6:9, qsl], start=True, stop=True)
                    return nc.tensor.matmul(ps[bank][0:8, 24:48], lhsT=qkS[pr, 9, qsl], rhs=qkS[pr, 6:9, qsl], start=True, stop=True)
                tk.op(PE, fsc, reads=[bKcT, bqkS], writes=[pb[bank]])
                tk.op(ACT, lambda g=g, bank=bank: nc.scalar.activation(out=Ps[g][:, 0:24], in_=ps[bank][:, 0:24], func=AF.Exp, scale=SC),
                      reads=[pb[bank]], writes=[bPs[g]])
                tk.op(ACT, lambda g=g, bank=bank: nc.scalar.activation(out=Ps[g][0:8, 24:48], in_=ps[bank][0:8, 24:48], func=AF.Exp, scale=SC),
                      reads=[pb[bank]], writes=[bPs[g]])
                mc = bass.AP(maskCs.tensor, maskCs.offset, [list(maskCs.ap[0]), [0, 3], [1, 8]])
                mcn = bass.AP(maskCn.tensor, maskCn.offset, [[maskCn.ap[0][0], 8], [0, 3], [1, 8]])
                tk.op(DVE, lambda g=g, mc=mc: nc.vector.tensor_tensor(out=Ps[g][:, 0:24].rearrange("p (j c) -> p j c", j=3),
                                                                     in0=Ps[g][:, 0:24].rearrange("p (j c) -> p j c", j=3), in1=mc, op=ALU.mult),
                      reads=[bPs[g], bC], writes=[bPs[g]])
                tk.op(DVE, lambda g=g, mcn=mcn: nc.vector.tensor_tensor(out=Ps[g][0:8, 24:48].rearrange("p (j c) -> p j c", j=3),
                                                                       in0=Ps[g][0:8, 24:48].rearrange("p (j c) -> p j c", j=3), in1=mcn, op=ALU.mult),
                      reads=[bPs[g], bC], writes=[bPs[g]])
            oc = 6 + b % 2

            def fpc():
                for g in (0, 1):
                    for which in (0, 1):
                        o_ap = ps[oc][0:24, g * 65:g * 65 + 64] if which == 0 else ps[oc][0:24, g * 65 + 64:g * 65 + 65]
                        nc.tensor.matmul(o_ap, lhsT=Ps[g][:, 0:24], rhs=(vc_ap[:, g * 64:(g + 1) * 64] if which == 0 else onesb[:, 0:1]), start=True, stop=False)
                        i = nc.tensor.matmul(o_ap, lhsT=Ps[g][0:8, 24:48], rhs=(Vcn[0:8, b, g * 64:(g + 1) * 64] if which == 0 else onesb[0:8, 0:1]),
                                             start=False, stop=True)
                return i
            tk.op(PE, fpc, reads=[bPs[0], bPs[1], vc_b, bVn, bC], writes=[pb[oc]])
            ocv = ps[oc][0:24, 0:130].rearrange("p (g c) -> p g c", g=2)
            tk.op(DVE, lambda ocv=ocv: nc.vector.tensor_tensor(out=rl[0:24, 6:8].rearrange("p (g o) -> p g o", o=1), in0=ocv[:, :, 64:65],
                                                               in1=es24[0:24, :].rearrange("p (g o) -> p g o", o=1), op=ALU.add),
                  reads=[pb[oc], bes, brl], writes=[brl])
            tk.op(DVE, lambda: nc.vector.reciprocal(out=rl[0:24, 6:8], in_=rl[0:24, 6:8]), reads=[brl], writes=[brl])
            rcb = bass.AP(rl.tensor, rl.offset + 6, [[rl.ap[0][0], 24], [1, 2], [0, 64]])
            tk.op(DVE, lambda ocv=ocv, rcb=rcb, b=b: nc.vector.tensor_tensor(out=ycs[0:24, b, :].rearrange("p (g c) -> p g c", g=2), in0=ocv[:, :, 0:64],
                                                                           in1=rcb, op=ALU.mult), reads=[pb[oc], brl], writes=[byc])
        def ftb():
            for b in range(NB):
                for j in range(3):
                    c0 = (j * NB + b) * 8
                    i = nc.tensor.transpose(out=psb[2][:, c0:c0 + 8], in_=ysb[0:8, b, 128 * j:128 * j + 128], identity=identb[0:8, 0:8])
            return i
        tk.op(PE, ftb, reads=[bys, bC], writes=[pb[2]])
        tk.op(DVE, lambda: nc.vector.tensor_copy(out=mixT[:, 2:5, SEQ:SEQ + 128], in_=psb[2][:, 0:384].rearrange("p (j t) -> p j t", j=3)),
              reads=[pb[2]], writes=[bmix])

        def ftc():
            for b in range(NB):
                i = nc.tensor.transpose(out=psb[3][:, b * 24:(b + 1) * 24], in_=ycs[0:24, b, :], identity=identb[0:24, 0:24])
            return i
        tk.op(PE, ftc, reads=[byc, bC], writes=[pb[3]])
        tk.op(DVE, lambda: nc.vector.tensor_copy(out=mixT[:, 5:8, SEQ:SEQ + 128].rearrange("p h (b i) -> p h b i", b=NB),
                                                 in_=psb[3][:, 0:384].rearrange("p (b h i) -> p h b i", b=NB, h=3)),
              reads=[pb[3]], writes=[bmix])
        phase_end(m)

    def phase_a3(l, mixT, bmix):
        m = phase_begin()
        wo = sb.bf16(8 * D).rearrange("p (k c) -> p k c", k=8); bwo = Buf()
        prs = []
        for k in range(5):
            for h2 in range(2):
                prs.append((wo[:, k, h2 * 512:(h2 + 1) * 512], w_out[l, k * 128:(k + 1) * 128, h2 * 512:(h2 + 1) * 512]))
        for j in range(3):
            for q, hh in enumerate((j, j + 3)):
                r0 = 640 + 64 * hh
                for h2 in range(2):
                    prs.append((wo[64 * q:64 * q + 64, 5 + j, h2 * 512:(h2 + 1) * 512], w_out[l, r0:r0 + 64, h2 * 512:(h2 + 1) * 512]))
        dsw = ds_take(2, sw=True)
        for pi, pr in enumerate(prs):
            tk.dma(POOL, dsw[pi % 2], [pr], writes=[bwo])
        gp = sb.f32(D); gsm = sb.f32(D); bg = Buf()
        tk.dma(SP, ds_take(), [(gp, MODP[l, 2]), (gsm, MODS[l, 2])], reads=mod_bufs(0, l, 2) + mod_bufs(1, l, 2), writes=[bg])
        xr = Ring([sb.f32(D) for _ in range(3)], ds_take(3), "x")
        orr = Ring([sb.f32(D) for _ in range(2)], ds_take(2), "o")
        for t in range(NT + 1):
            x_ap, x_b, x_d = xr.next()
            src, sbufs = x_src(l, t)
            tk.dma(SP, x_d, [(x_ap, src)], reads=sbufs, writes=[x_b])
            o_ap, o_b, o_d = orr.next()
            gate = gp if t < NT else gsm
            for h2 in range(2):
                bank = (t % 2) * 2 + h2
                tk.op(PE, mm_acc(ps[bank][:, :], lambda k: mixT[:, k, t * 128:(t + 1) * 128], lambda k: wo[:, k, h2 * 512:(h2 + 1) * 512], 8),
                      reads=[bmix, bwo], writes=[pb[bank]])
                hs = slice(h2 * 512, (h2 + 1) * 512)
                tk.op(DVE, lambda bank=bank, hs=hs: nc.vector.tensor_tensor(out=o_ap[:, hs], in0=ps[bank][:, :], in1=gate[:, hs], op=ALU.mult),
                      reads=[pb[bank], bg], writes=[o_b])
                tk.op(POOL, lambda hs=hs: nc.gpsimd.tensor_tensor(out=o_ap[:, hs], in0=o_ap[:, hs], in1=x_ap[:, hs], op=ALU.add),
                      reads=[o_b, x_b], writes=[o_b])
            tk.dma(SP, o_d, [(XA[t * 128:(t + 1) * 128, :], o_ap)], reads=[o_b], writes=[DB("XA", t)])
        phase_end(m)

    def phase_b(l):
        m = phase_begin()
        wg = sb.bf16(8 * 2 * DFF).rearrange("p (k c) -> p k c", k=8); wd = sb.bf16(NFC * D).rearrange("p (k c) -> p k c", k=NFC)
        bwg, bwd = Buf(), Buf()
        srcg = w_gu[l].rearrange("(k p) c -> p k c", p=128)
        dsw = ds_take(2, sw=True)
        for ci_, c0 in enumerate(range(0, 2 * DFF, 512)):
            tk.dma(POOL, dsw[ci_ % 2], [(wg[:, :, c0:c0 + 512], srcg[:, :, c0:c0 + 512])], writes=[bwg])
        srcd = w_down[l].rearrange("(k p) c -> p k c", p=128)
        dsw2 = ds_take(sw=True)
        for k0 in range(0, NFC, 4):
            k1 = min(k0 + 4, NFC)
            for h2 in range(2):
                tk.dma(POOL, dsw2, [(wd[:, k0:k1, h2 * 512:(h2 + 1) * 512], srcd[:, k0:k1, h2 * 512:(h2 + 1) * 512])], writes=[bwd])
        nm = make_norm(l, 1)
        gp = sb.f32(D); bg = Buf()
        tk.dma(SP, ds_take(), [(gp, MODP[l, 5])], reads=mod_bufs(0, l, 5), writes=[bg])
        GT = 2
        xr = Ring([sb.f32(D) for _ in range(GT + 1)], ds_take(GT + 1), "x")
        hTr = Ring([sb.bf16(8 * 128 * GT).rearrange("p (k t) -> p k t", k=8) for _ in range(2)], None, "hT2")
        aT = sb.bf16(NFC * 128 * GT).rearrange("p (k t) -> p k t", k=NFC); baT = Buf()
        sg = Ring([sb.f32(128 * GT) for _ in range(2)], None, "sg")
        orr = Ring([sb.f32(D) for _ in range(2)], ds_take(2), "o")
        ss2 = sb.f32(1); rs2 = sb.f32(1); bss2, brs2 = Buf(), Buf()
        dst = XB if l < nlayers - 1 else None
        ngrp = NT // GT + 1
        for g in range(ngrp):
            tiles = [GT * g + i for i in range(GT)] if g < NT // GT else [NT]
            N = 128 * len(tiles)
            xts = []
            hT, bh, _ = hTr.next()
            if g == NT // GT:
                mk2 = sb.mark()
                gsS = sb.f32(D); shS = sb.f32(D); gtS = sb.f32(D); bmS = Buf()
                tk.dma(SP, ds_take(), [(gsS, MODS[l, 4]), (shS, MODS[l, 3]), (gtS, MODS[l, 5])],
                       reads=mod_bufs(1, l, 4) + mod_bufs(1, l, 3) + mod_bufs(1, l, 5), writes=[bmS])
            for i, t in enumerate(tiles):
                x_ap, x_b, x_d = xr.next()
                tk.dma(SP, x_d, [(x_ap, XA[t * 128:(t + 1) * 128, :])], reads=[DB("XA", t)], writes=[x_b])
                xts.append((x_ap, x_b))
                if t < NT:
                    norm_tile_prompt(nm, x_ap, x_b, hT[:, :, i * 128:(i + 1) * 128], bh, (6, 7))
                else:
                    o_ap, o_b, _ = orr.next()
                    norm_tile_sample(nm, x_ap, x_b, hT[:, :, 0:128], bh, (6, 7), gsS, shS, bmS, (o_ap, o_b))
            for c in range(NFC):
                bg_, bu_ = (c % 2) * 2, (c % 2) * 2 + 1
                tk.op(PE, mm_acc(ps[bg_][:, 0:N], lambda k: wg[:, k, c * 128:(c + 1) * 128], lambda k: hT[:, k, 0:N], 8),
                      reads=[bwg, bh], writes=[pb[bg_]])
                tk.op(PE, mm_acc(ps[bu_][:, 0:N], lambda k: wg[:, k, DFF + c * 128:DFF + (c + 1) * 128], lambda k: hT[:, k, 0:N], 8),
                      reads=[bwg, bh], writes=[pb[bu_]])
                s_ap, s_b, _ = sg.next()
                tk.op(ACT, lambda s_ap=s_ap, bg_=bg_: nc.scalar.activation(out=s_ap[:, 0:N], in_=ps[bg_][:, 0:N], func=AF.Silu),
                      reads=[pb[bg_]], writes=[s_b])
                tk.op(DVE, lambda s_ap=s_ap, bu_=bu_, c=c: nc.vector.tensor_tensor(out=aT[:, c, 0:N], in0=ps[bu_][:, 0:N], in1=s_ap[:, 0:N], op=ALU.mult),
                      reads=[pb[bu_], s_b], writes=[baT])
            for i, t in enumerate(tiles):
                x_ap, x_b = xts[i]
                o_ap, o_b, o_d = orr.next()
                gate = gp if t < NT else gtS
                gb = [bg] if t < NT else [bmS]
                for h2 in range(2):
                    bank = 4 + h2
                    tk.op(PE, mm_acc(ps[bank][:, :], lambda k: aT[:, k, i * 128:(i + 1) * 128], lambda k: wd[:, k, h2 * 512:(h2 + 1) * 512], NFC),
                          reads=[baT, bwd], writes=[pb[bank]])
                    hs = slice(h2 * 512, (h2 + 1) * 512)
                    tk.op(DVE, lambda bank=bank, hs=hs: nc.vector.tensor_tensor(out=o_ap[:, hs], in0=ps[bank][:, :], in1=gate[:, hs], op=ALU.mult),
                          reads=[pb[bank]] + gb, writes=[o_b])
                    tk.op(POOL, lambda hs=hs: nc.gpsimd.tensor_tensor(out=o_ap[:, hs], in0=o_ap[:, hs], in1=x_ap[:, hs], op=ALU.add),
                          reads=[o_b, x_b], writes=[o_b])
                if dst is not None:
                    tk.dma(SP, o_d, [(dst[t * 128:(t + 1) * 128, :], o_ap)], reads=[o_b], writes=[DB("XB", t)])
                else:
                    tk.op(ACT, lambda: nc.scalar.activation(out=nm["junk"], in_=o_ap, func=AF.Square, accum_out=ss2[:, 0:1]),
                          reads=[o_b], writes=[nm["bjunk"], bss2])
                    rstd_from_ss(ss2, rs2, 1, bss2, brs2)
                    tk.op(DVE, lambda: nc.vector.scalar_tensor_tensor(out=o_ap, in0=o_ap, scalar=rs2[:, 0:1], in1=gfin, op0=ALU.mult, op1=ALU.mult),
                          reads=[o_b, brs2, bC], writes=[o_b])
                    od = yp[t * 128:(t + 1) * 128, :] if t < NT else ys
                    tk.dma(SP, o_d, [(od, o_ap)], reads=[o_b], writes=[DB("OUT")])
        phase_end(m)

    phase_mod()
    if stop_after == "mod":
        tk.barrier()
        return nc
    for l in range(nlayers):
        lm = phase_begin()
        mixT = sb.bf16(8 * (SEQ + 128)).rearrange("p (k t) -> p k t", k=8); bmix = Buf()
        qkS = sb.bf16(10 * 128).rearrange("p (k t) -> p k t", k=10); bqkS = Buf()
        ds_keep = dsi[0]
        m1, win, cw, bwin, nm = phase_a1(l, mixT, bmix)
        import os
        if not os.environ.get("SKIP_A1S"):
            phase_a1_sample(l, mixT, bmix, qkS, bqkS, win, cw, bwin, nm)
        phase_end(m1)
        if stop_after == ("a1", l):
            tk.barrier()
            return nc
        phase_a2(l, mixT, bmix)
        phase_a2s(l, mixT, bmix, qkS, bqkS)
        if stop_after == ("a2", l):
            tk.barrier()
            return nc
        phase_a3(l, mixT, bmix)
        sb.release(lm)
        tk.barrier()
        phase_b(l)
    tk.barrier()
    return nc


def _consts():
    c = {}
    c["c_ident"] = np.eye(128, dtype=np.float32)
    j = np.arange(128)[:, None]; i = np.arange(128)[None, :]
    prevB = (j >= i).astype(np.float32); prevC = (j > i).astype(np.float32); cur = (j <= i).astype(np.float32)
    c["c_maskB"] = np.concatenate([prevB, cur, prevB, cur], axis=1)
    c["c_maskC"] = np.concatenate([prevC, cur, prevC, cur], axis=1)
    rho = (np.arange(16)[None, :, None] * 128 + np.arange(128)[:, None, None])
    qi = np.arange(8)[None, None, :]
    mult = (rho >= 1920 + qi).astype(np.float32) + ((rho % 4 == qi % 4) & (rho >= 1536 + qi)).astype(np.float32) \
        + (rho % 16 == qi).astype(np.float32)
    c["c_multB"] = mult.reshape(128, 128).astype(np.float32)
    jj = np.arange(8)[:, None]; ii = np.arange(8)[None, :]
    c["c_multBn"] = ((jj <= ii).astype(np.float32) + 2.0 * (jj == ii) + 1.0 * (jj == ii - 4)).astype(np.float32)
    c["c_maskCs"] = (np.arange(128)[:, None] >= ii + 1).astype(np.float32)
    c["c_maskCn"] = (jj <= ii).astype(np.float32)
    return c


def make_in_map(c, inp, consts):
    b0, b1 = NB * c, NB * (c + 1)
    f = lambda a: np.ascontiguousarray(a, dtype=np.float32)
    sinks = np.asarray(inp["sinks"], np.float32)
    sinkP = np.stack([np.stack([np.concatenate([np.full(64, sinks[l, j]), np.full(64, sinks[l, j + 3])]) for j in range(3)]) for l in range(L)])
    sinkS = np.zeros((L, 24, 2), np.float32)
    for l in range(L):
        for g in range(2):
            for h in range(3):
                sinkS[l, 8 * h:8 * h + 8, g] = sinks[l, 3 * g + h]
    m = {
        "xp": f(inp["x_prompt"][c]), "xs": f(inp["x_sample"][b0:b1]).reshape(128, D),
        "cpe": f(np.broadcast_to(np.asarray(inp["c_prompt"])[c], (128, D))), "cse": f(np.repeat(np.asarray(inp["c_sample"])[b0:b1], DS, axis=0)),
        "sconv": f(inp["state_conv"][:, b0:b1]).reshape(L, 32, 256),
        "cbk": f(inp["cache_b_k"][:, b0:b1]).reshape(L, NB, WB, 384), "cbv": f(inp["cache_b_v"][:, b0:b1]).reshape(L, NB, WB, 384),
        "cck": f(inp["cache_c_k"][:, b0:b1]).reshape(L, NB, 128, 128), "ccv": f(inp["cache_c_v"][:, b0:b1]).reshape(L, NB, 128, 128),
        "w_mod": f(inp["w_mod"]), "b_mod": f(inp["b_mod"]), "norm_mix": f(inp["norm_mix"]), "norm_ffn": f(inp["norm_ffn"]),
        "w_in": f(inp["w_in"]), "conv_w": f(inp["conv_w"]), "sinkP": f(sinkP), "sinkS": sinkS,
        "w_out": f(inp["w_out"]), "w_gu": f(inp["w_gate_up"]), "w_down": f(inp["w_down"]), "norm_final": f(inp["norm_final"]),
    }
    m.update(consts)
    return m


_NC = [None]


def kernel(**inputs):
    inp = {k: np.asarray(v) for k, v in inputs.items()}
    ncores = 8
    consts = _consts()
    in_maps = [make_in_map(c, inp, consts) for c in range(ncores)]
    if _NC[0] is None:
        _NC[0] = build()
    res = run_bass_kernel_spmd(_NC[0], in_maps, core_ids=list(range(ncores)))
    R = res.results
    cat = lambda k, shp: np.stack([np.asarray(R[c][k], np.float32).reshape(shp) for c in range(ncores)])
    y_p = cat("yp", (SEQ, D))
    y_s = cat("ys", (NB, DS, D)).reshape(ncores * NB, DS, D)
    per_b = lambda k, shp: np.stack([np.asarray(R[c][k], np.float32).reshape((L,) + shp) for c in range(ncores)], axis=1)
    per_s = lambda k, shp: np.concatenate([np.asarray(R[c][k], np.float32).reshape((L, NB) + shp) for c in range(ncores)], axis=1)
    return (y_p, y_s,
            per_b("convp", (2, 256)), per_s("convs", (2, 256)),
            per_b("bkp", (WB, 6, 64)), per_s("bks", (WB, 6, 64)),
            per_b("bvp", (WB, 6, 64)), per_s("bvs", (WB, 6, 64)),
            per_b("ckp", (128, 2, 64)), per_s("cks", (128, 2, 64)),
            per_b("cvp", (128, 2, 64)), per_s("cvs", (128, 2, 64)))
```
